# Optimizing a Trainium2 kernel written in Bass

```python
import math
import jax, jax.numpy as jnp
from jax import lax
import numpy as np

D_MODEL = 1024
BATCH = 16
SEQ = 4096
DEPTH = 2
DEC_BATCH = 4
DEC_SEQ = 4096
PAST_LEN = 128

N_META = 16
MLA_HEADS = 16
MLA_Q_LORA = 768
MLA_KV_LORA = 256
MLA_D_NOPE = 64
MLA_D_ROPE = 32
MLA_D_V = 64
MLA_SCALE = (MLA_D_NOPE + MLA_D_ROPE) ** -0.5
ROPE_BASE = 10000.0
Q_BLOCK = 128
HG_HEADS = 8
HG_D_K = 128
HG_D_V = D_MODEL // HG_HEADS
HG_HK = HG_HEADS * HG_D_K
HG_HV = HG_HEADS * HG_D_V
HG_CHUNK = 64
D_FF = 2816
CONV_WIDTH = 3
ALPHA = (2 * DEPTH) ** 0.25
BETA = (8 * DEPTH) ** -0.25
EPS = 1e-6
N_MLA_LAYERS = (DEPTH + 1) // 2
N_HGRN_LAYERS = DEPTH // 2

kernel_name = "hybrid_mla_hgrn2_convffn_encoder"


def rms_norm(x, gain):
    x32 = x.astype(jnp.float32)
    y = x32 * lax.rsqrt(jnp.mean(x32 * x32, axis=-1, keepdims=True) + EPS) * gain.astype(jnp.float32)
    return y.astype(x.dtype)


def layer_norm(x, gain, bias):
    x32 = x.astype(jnp.float32)
    mu = jnp.mean(x32, axis=-1, keepdims=True)
    xc = x32 - mu
    var = jnp.mean(xc * xc, axis=-1, keepdims=True)
    y = xc * lax.rsqrt(var + EPS) * gain.astype(jnp.float32) + bias.astype(jnp.float32)
    return y.astype(x.dtype)


def rope_tables(length):
    inv = 1.0 / (ROPE_BASE ** (jnp.arange(0, MLA_D_ROPE, 2, dtype=jnp.float32) / MLA_D_ROPE))
    ang = jnp.arange(length, dtype=jnp.float32)[:, None] * inv[None, :]
    return jnp.cos(ang), jnp.sin(ang)


def apply_rope(x, cos, sin):
    x32 = x.astype(jnp.float32)
    x1, x2 = jnp.split(x32, 2, axis=-1)
    return jnp.concatenate([x1 * cos - x2 * sin, x1 * sin + x2 * cos], axis=-1).astype(x.dtype)


def mla_attend(q_nope, q_rope, k_nope, k_rope, v):
    B, L, H, _ = q_nope.shape
    nb = -(-L // Q_BLOCK)
    pad = nb * Q_BLOCK - L
    def blocks(t):
        t = jnp.pad(t, ((0, 0), (0, pad), (0, 0), (0, 0)))
        return t.reshape(B, nb, Q_BLOCK, H, t.shape[-1]).transpose(1, 0, 2, 3, 4)
    def one_block(args):
        qn, qr = args
        s = jnp.einsum('bqhd,bkhd->bhqk', qn, k_nope) + jnp.einsum('bqhd,bkd->bhqk', qr, k_rope)
        p = jax.nn.softmax(s.astype(jnp.float32) * MLA_SCALE, axis=-1)
        return jnp.einsum('bhqk,bkhd->bqhd', p.astype(v.dtype), v)
    o = lax.map(one_block, (blocks(q_nope), blocks(q_rope)))
    return o.transpose(1, 0, 2, 3, 4).reshape(B, nb * Q_BLOCK, H, MLA_D_V)[:, :L]


def mla_mixer(h, w_in, q_norm, kv_norm, w_uq, w_ukv, w_o, cos, sin):
    B, L, _ = h.shape
    proj = h @ w_in
    c_q, c_kv, k_rope = jnp.split(proj, [MLA_Q_LORA, MLA_Q_LORA + MLA_KV_LORA], axis=-1)
    c_q = rms_norm(c_q, q_norm)
    c_kv = rms_norm(c_kv, kv_norm)
    q = (c_q @ w_uq).reshape(B, L, MLA_HEADS, MLA_D_NOPE + MLA_D_ROPE)
    kv = (c_kv @ w_ukv).reshape(B, L, MLA_HEADS, MLA_D_NOPE + MLA_D_V)
    q_nope, q_rope = jnp.split(q, [MLA_D_NOPE], axis=-1)
    k_nope, v = jnp.split(kv, [MLA_D_NOPE], axis=-1)
    q_rope = apply_rope(q_rope, cos[:, None, :], sin[:, None, :])
    k_rope = apply_rope(k_rope, cos, sin)
    o = mla_attend(q_nope, q_rope, k_nope, k_rope, v)
    return o.reshape(B, L, MLA_HEADS * MLA_D_V) @ w_o


def gla_chunked(q, k, v, logf):
    B, Lp, H, K = q.shape
    V = v.shape[-1]
    C = HG_CHUNK
    N = Lp // C
    ch = lambda t: t.reshape(B, N, C, H, t.shape[-1])
    q, k, v, logf = ch(q), ch(k), ch(v), ch(logf)
    b = jnp.cumsum(logf, axis=2)
    ref = b[:, :, C // 2:C // 2 + 1]
    a = jnp.einsum('bnthk,bnshk->bnhts', q * jnp.exp(b - ref), k * jnp.exp(ref - b))
    a = jnp.where(jnp.tril(jnp.ones((C, C), dtype=bool)), a, 0.0)
    o_intra = jnp.einsum('bnhts,bnshv->bnthv', a, v)
    b_last = b[:, :, -1]
    u = jnp.einsum('bnshk,bnshv->bnhkv', k * jnp.exp(b_last[:, :, None] - b), v)
    q_dec = q * jnp.exp(b)
    def step(s, xs):
        dec_n, u_n, q_n = xs
        o_n = jnp.einsum('bthk,bhkv->bthv', q_n, s)
        return jnp.exp(dec_n)[..., None] * s + u_n, o_n
    s0 = jnp.zeros((B, H, K, V), jnp.float32)
    _, o_inter = lax.scan(step, s0, (b_last.transpose(1, 0, 2, 3), u.transpose(1, 0, 2, 3, 4), q_dec.transpose(1, 0, 2, 3, 4)))
    o = o_intra + o_inter.transpose(1, 0, 2, 3, 4)
    return o.reshape(B, Lp, H, V)


def hgrn2_mixer(h, w_in, lb_param, o_norm, w_o, layer_idx):
    B, L, _ = h.shape
    pad = HG_CHUNK - N_META
    hp = jnp.pad(h, ((0, 0), (pad, 0), (0, 0)))
    Lp = L + pad
    proj = hp @ w_in
    q, i, f_fw, f_bw, g = jnp.split(proj, [HG_HK, HG_HK + HG_HV, 2 * HG_HK + HG_HV, 3 * HG_HK + HG_HV], axis=-1)
    p = jax.nn.softmax(lb_param.astype(jnp.float32), axis=1)
    lb = (jnp.cumsum(p, axis=1) - p[:, :1])[:, layer_idx]
    valid = (jnp.arange(Lp) >= pad)[None, :, None]
    def forget(f_logit, lb_d):
        f = lb_d + (1.0 - lb_d) * jax.nn.sigmoid(f_logit.astype(jnp.float32))
        return jnp.where(valid, f, 1.0)
    heads = lambda t: t.reshape(B, Lp, HG_HEADS, -1).astype(jnp.float32)
    qh, vh = heads(q), heads(i)
    ff = heads(forget(f_fw, lb[0]))
    fb = heads(forget(f_bw, lb[1]))
    rev = lambda t: jnp.flip(t, axis=1)
    o_fw = gla_chunked(qh, 1.0 - ff, vh, jnp.log(ff))
    o_bw = rev(gla_chunked(rev(qh), rev(1.0 - fb), rev(vh), rev(jnp.log(fb))))
    o = (o_fw + o_bw)[:, pad:]
    o = o * lax.rsqrt(jnp.mean(o * o, axis=-1, keepdims=True) + EPS) * o_norm.astype(jnp.float32).reshape(HG_HEADS, HG_D_V)
    o = o.reshape(B, L, HG_HV) * jax.nn.silu(g[:, pad:].astype(jnp.float32))
    return o.astype(h.dtype) @ w_o


def conv_ffn(h, w_up, conv_w, conv_b, w_down):
    L = h.shape[1]
    u = h @ w_up
    half = CONV_WIDTH // 2
    up = jnp.pad(u, ((0, 0), (half, half), (0, 0)))
    u = sum(up[:, j:j + L] * conv_w[j] for j in range(CONV_WIDTH)) + conv_b
    val, gate = jnp.split(u, 2, axis=-1)
    return (jax.nn.silu(gate) * val) @ w_down


def trunk(x, meta_tokens, mla_w_in, mla_q_norm, mla_kv_norm, mla_w_uq, mla_w_ukv, mla_w_o,
          hgrn_w_in, hgrn_lower_bound, hgrn_o_norm, hgrn_w_o,
          ffn_w_up, ffn_conv_w, ffn_conv_b, ffn_w_down, ln_gain, ln_bias):
    B = x.shape[0]
    meta = jnp.broadcast_to(meta_tokens[None].astype(x.dtype), (B, N_META, D_MODEL))
    h = jnp.concatenate([meta, x], axis=1)
    cos, sin = rope_tables(h.shape[1])
    for layer in range(DEPTH):
        j = layer // 2
        if layer % 2 == 0:
            mix = mla_mixer(h, mla_w_in[j], mla_q_norm[j], mla_kv_norm[j], mla_w_uq[j], mla_w_ukv[j], mla_w_o[j], cos, sin)
        else:
            mix = hgrn2_mixer(h, hgrn_w_in[j], hgrn_lower_bound, hgrn_o_norm[j], hgrn_w_o[j], layer)
        h = layer_norm(ALPHA * h + mix, ln_gain[layer, 0], ln_bias[layer, 0])
        h = layer_norm(ALPHA * h + conv_ffn(h, ffn_w_up[layer], ffn_conv_w[layer], ffn_conv_b[layer], ffn_w_down[layer]),
                       ln_gain[layer, 1], ln_bias[layer, 1])
    return h[:, N_META:]


def setup_inputs(seed: int = 0) -> dict:
    key = jax.random.key(seed)
    ks = jax.random.split(key, 24)
    f32 = jnp.float32
    def nrm(k, shape, fan_in, scale=1.0):
        return jax.random.normal(k, shape, f32) * (scale * fan_in ** -0.5)
    def gain(k, shape):
        return 1.0 + 0.02 * jax.random.normal(k, shape, f32)
    NA, NB = N_MLA_LAYERS, N_HGRN_LAYERS
    return {
        "x_prompt": jax.random.normal(ks[0], (BATCH, SEQ, D_MODEL), f32),
        "x_sample": jax.random.normal(ks[1], (DEC_BATCH, DEC_SEQ, D_MODEL), f32),
        "meta_tokens": jax.random.normal(ks[2], (N_META, D_MODEL), f32),
        "mla_w_in": nrm(ks[3], (NA, D_MODEL, MLA_Q_LORA + MLA_KV_LORA + MLA_D_ROPE), D_MODEL),
        "mla_q_norm": gain(ks[4], (NA, MLA_Q_LORA)),
        "mla_kv_norm": gain(ks[5], (NA, MLA_KV_LORA)),
        "mla_w_uq": nrm(ks[6], (NA, MLA_Q_LORA, MLA_HEADS * (MLA_D_NOPE + MLA_D_ROPE)), MLA_Q_LORA),
        "mla_w_ukv": nrm(ks[7], (NA, MLA_KV_LORA, MLA_HEADS * (MLA_D_NOPE + MLA_D_V)), MLA_KV_LORA),
        "mla_w_o": nrm(ks[8], (NA, MLA_HEADS * MLA_D_V, D_MODEL), MLA_HEADS * MLA_D_V, BETA),
        "hgrn_w_in": nrm(ks[9], (NB, D_MODEL, 3 * HG_HK + 2 * HG_HV), D_MODEL),
        "hgrn_lower_bound": 0.1 * jax.random.normal(ks[10], (2, DEPTH, HG_HK), f32),
        "hgrn_o_norm": gain(ks[11], (NB, HG_HV)),
        "hgrn_w_o": nrm(ks[12], (NB, HG_HV, D_MODEL), HG_HV, BETA),
        "ffn_w_up": nrm(ks[13], (DEPTH, D_MODEL, 2 * D_FF), D_MODEL),
        "ffn_conv_w": nrm(ks[14], (DEPTH, CONV_WIDTH, 2 * D_FF), CONV_WIDTH),
        "ffn_conv_b": 0.02 * jax.random.normal(ks[15], (DEPTH, 2 * D_FF), f32),
        "ffn_w_down": nrm(ks[16], (DEPTH, D_FF, D_MODEL), D_FF, BETA),
        "ln_gain": gain(ks[17], (DEPTH, 2, D_MODEL)),
        "ln_bias": 0.02 * jax.random.normal(ks[18], (DEPTH, 2, D_MODEL), f32),
    }


def reference(x_prompt, x_sample, meta_tokens, mla_w_in, mla_q_norm, mla_kv_norm, mla_w_uq, mla_w_ukv, mla_w_o,
              hgrn_w_in, hgrn_lower_bound, hgrn_o_norm, hgrn_w_o,
              ffn_w_up, ffn_conv_w, ffn_conv_b, ffn_w_down, ln_gain, ln_bias):
    y_prompt = trunk(x_prompt, meta_tokens, mla_w_in, mla_q_norm, mla_kv_norm, mla_w_uq, mla_w_ukv, mla_w_o,
                     hgrn_w_in, hgrn_lower_bound, hgrn_o_norm, hgrn_w_o,
                     ffn_w_up, ffn_conv_w, ffn_conv_b, ffn_w_down, ln_gain, ln_bias)
    y_sample = trunk(x_sample, meta_tokens, mla_w_in, mla_q_norm, mla_kv_norm, mla_w_uq, mla_w_ukv, mla_w_o,
                     hgrn_w_in, hgrn_lower_bound, hgrn_o_norm, hgrn_w_o,
                     ffn_w_up, ffn_conv_w, ffn_conv_b, ffn_w_down, ln_gain, ln_bias)
    return (y_prompt, y_sample)
```

```python
import math
import os
from contextlib import ExitStack
import numpy as np
import concourse.bass as bass
import concourse.mybir as mybir
from concourse.bass_utils import run_bass_kernel_spmd

F32 = mybir.dt.float32
BF16 = mybir.dt.bfloat16
AF = mybir.ActivationFunctionType
ALU = mybir.AluOpType
AX = mybir.AxisListType

D = 1024
NMETA = 16
QL, KVL, DR, DN, DV, NH = 768, 256, 32, 64, 64, 16
MLA_SCALE = (DN + DR) ** -0.5
HG_H = 8
DFF = 2816
DEPTH = 2
ALPHA = (2 * DEPTH) ** 0.25
EPS = 1e-6
COMPUTE = ("pe", "act", "dve", "pool")


class Buf:
    __slots__ = ("name", "w", "r")

    def __init__(self, name=""):
        self.name = name
        self.w = {}
        self.r = {}


class Sched:
    K = 8
    ND = 24

    def __init__(self, nc, same_engine_sync=True):
        self.nc = nc
        self.same = same_engine_sync
        self.streams = {e: [] for e in COMPUTE + ("sp",)}
        self.cnt = {e: 0 for e in COMPUTE}
        self.dcnt = {"dma": 0, "dmaA": 0}
        self.known = {s: {e: -1 for e in COMPUTE} for s in self.streams}
        self.dma_low = {(s, k): 0 for s in self.streams for k in ("dma", "dmaA")}
        self.dma_set = {(s, k): set() for s in self.streams for k in ("dma", "dmaA")}
        self.sems = {}
        self.n_inst = 0

    def alloc(self, stack):
        for e in COMPUTE:
            self.sems[e] = [stack.enter_context(self.nc.semaphore(f"s_{e}{i}")) for i in range(self.K)]
        self.sems["dma"] = [stack.enter_context(self.nc.semaphore(f"s_dma{i}")) for i in range(self.ND)]
        self.sems["dmaA"] = [stack.enter_context(self.nc.semaphore(f"s_dmaA{i}")) for i in range(self.ND)]

    def _knows(self, s, ev):
        k, idx = ev
        if k.startswith("dma"):
            return idx < self.dma_low[(s, k)] or idx in self.dma_set[(s, k)]
        return self.known[s][k] >= idx

    def _learn(self, s, ev):
        k, idx = ev
        if k.startswith("dma"):
            self.dma_set[(s, k)].add(idx)
            self.dma_low[(s, k)] = max(self.dma_low[(s, k)], idx - self.ND + 1)
        else:
            self.known[s][k] = max(self.known[s][k], idx)

    def _wait(self, s, ev):
        if self._knows(s, ev):
            return
        k, idx = ev
        if k.startswith("dma"):
            sem = self.sems[k][idx % self.ND]
            val = 16 * (idx // self.ND + 1)
        else:
            sem = self.sems[k][idx % self.K]
            val = idx // self.K + 1
        self.streams[s].append(("wait", sem, val))
        self._learn(s, ev)

    @staticmethod
    def _events(d):
        for k, v in d.items():
            if isinstance(k, tuple):
                yield k
            else:
                yield (k, v)

    def _add(self, d, ev):
        k, idx = ev
        if k.startswith("dma"):
            for kk in [kk for kk in d if isinstance(kk, tuple) and kk[0] == k and kk[1] <= idx - self.ND]:
                del d[kk]
            d[ev] = True
        elif d.get(k, -1) < idx:
            d[k] = idx

    def _deps(self, s, eng, reads, writes):
        deps = set()
        for b in reads:
            deps.update(self._events(b.w))
        for b in writes:
            if isinstance(b, tuple):
                b = b[0]
            deps.update(self._events(b.w))
            deps.update(self._events(b.r))
        for ev in sorted(deps, key=lambda e: (str(e[0]), e[1])):
            if ev[0] == eng and (eng == "pe" or not self.same):
                continue
            self._wait(s, ev)

    def _post(self, ev, reads, writes):
        for b in reads:
            self._add(b.r, ev)
        for b in writes:
            if isinstance(b, tuple):
                self._add(b[0].w, ev)
            else:
                b.w = {}
                b.r = {}
                self._add(b.w, ev)

    def op(self, eng, fn, reads=(), writes=(), track=True):
        self._deps(eng, eng, reads, writes)
        idx = self.cnt[eng]
        if track:
            self.cnt[eng] += 1
            self.streams[eng].append(("inst", fn, self.sems[eng][idx % self.K], 1))
        else:
            self.streams[eng].append(("inst", fn, None, 0))
        self._post((eng, idx), reads, writes)
        self.n_inst += 1

    def dma(self, out, in_, reads=(), writes=(), q="sp", **kw):
        s = q
        kind = "dma" if q == "sp" else "dmaA"
        self._deps(s, kind, reads, writes)
        idx = self.dcnt[kind]
        if idx >= self.ND:
            self._wait(s, (kind, idx - self.ND))
        self.dcnt[kind] += 1
        sem = self.sems[kind][idx % self.ND]
        self.streams[s].append(("inst", lambda e: e.dma_start(out=out, in_=in_, **kw), sem, 16))
        self._post((kind, idx), reads, writes)
        self.n_inst += 1

    def barrier(self):
        for s in self.streams:
            for kind in ("dma", "dmaA"):
                for idx in range(max(0, self.dcnt[kind] - self.ND), self.dcnt[kind]):
                    self._wait(s, (kind, idx))
            for e in COMPUTE:
                if self.cnt[e] > 0:
                    self._wait(s, (e, self.cnt[e] - 1))

    def emit(self, block):
        def replay(name):
            def run(eng):
                for ent in self.streams[name]:
                    if ent[0] == "wait":
                        eng.wait_ge(ent[1], ent[2])
                    else:
                        ins = ent[1](eng)
                        if ent[2] is not None:
                            ins.then_inc(ent[2], ent[3])
            return run

        block.tensor(replay("pe"))
        block.scalar(replay("act"))
        block.vector(replay("dve"))
        block.gpsimd(replay("pool"))
        block.sync(replay("sp"))


_UID = [0]


def _uniq(name):
    _UID[0] += 1
    return f"{name}_u{_UID[0]}"


class Ring:
    def __init__(self, nc, st, name, shape, dt, n):
        self.t = [st.enter_context(nc.sbuf_tensor(_uniq(f"{name}{i}"), shape, dt)) for i in range(n)]
        self.b = [Buf(f"{name}{i}") for i in range(n)]
        self.i = 0

    def next(self):
        j = self.i % len(self.t)
        self.i += 1
        return self.t[j], self.b[j]


def build(S_LEN, NS, phases=None, same=True):
    L = S_LEN + NMETA
    LP = L + 112
    NKT = (L + 127) // 128
    nc = bass.Bass("TRN2", target_bir_lowering=False)

    def din(name, shape, dt=F32):
        return nc.dram_tensor(name, list(shape), dt, kind="ExternalInput").ap()

    def dscr(name, shape, dt):
        return nc.dram_tensor(name, list(shape), dt, kind="Internal").ap()

    x = din("x", [NS, S_LEN, D])
    meta = din("meta", [NMETA, D])
    w_in_d = din("mla_w_in", [D, QL + KVL + DR])
    qn_d = din("mla_q_norm", [QL])
    kvn_d = din("mla_kv_norm", [KVL])
    w_uq_d = din("mla_w_uq", [QL, NH * 96])
    w_ukv_d = din("mla_w_ukv", [KVL, NH * 128])
    w_o_d = din("mla_w_o", [D, D])
    hw_in_d = din("hgrn_w_in", [D, 5 * D])
    hlb_d = din("hgrn_lb", [2, 2, D])
    hon_d = din("hgrn_o_norm", [D])
    hw_o_d = din("hgrn_w_o", [D, D])
    f_up_d = din("ffn_w_up", [2, D, 2 * DFF])
    f_cw_d = din("ffn_conv_w", [2, 3, 2 * DFF])
    f_cb_d = din("ffn_conv_b", [2, 2 * DFF])
    f_dn_d = din("ffn_w_down", [2, DFF, D])
    lng_d = din("ln_gain", [2, 2, D])
    lnb_d = din("ln_bias", [2, 2, D])
    cos_d = din("rope_cos", [DR, L])
    sin_d = din("rope_sin", [DR, L])
    y = nc.dram_tensor("y", [NS, S_LEN, D], F32, kind="ExternalOutput").ap()

    XT = dscr("XT", [NS, D, L], BF16)
    CQ = dscr("CQ", [NS, QL, L], BF16)
    CKV = dscr("CKV", [NS, KVL, L], BF16)
    KR = dscr("KR", [NS, DR, L], BF16)
    OT = dscr("OT", [NS, D, L], BF16)
    A_tok = dscr("A_tok", [NS, L, D], F32)
    A_T = dscr("A_T", [NS, D, L + 2], BF16)
    B_tok = dscr("B_tok", [NS, L, D], F32)
    B_T = dscr("B_T", [NS, D, LP], BF16)
    VT_d = dscr("VT_d", [NS, LP, D], BF16)
    Q_d = dscr("Q_d", [NS, D, LP], F32)
    OFW = dscr("OFW", [NS, LP, D], F32)
    OBW = dscr("OBW", [NS, LP, D], F32)
    C_tok = dscr("C_tok", [NS, L, D], F32)
    C_T = dscr("C_T", [NS, D, L + 2], BF16)

    def fm(ap2d):
        return ap2d.rearrange("(c p) t -> p c t", p=128)

    with ExitStack() as st:
        S = Sched(nc, same_engine_sync=(same and os.environ.get('MK_SAME', '1') == '1'))
        S.alloc(st)

        def sbt(stack, name, shape, dt=F32):
            return stack.enter_context(nc.sbuf_tensor(_uniq(name), list(shape), dt))

        pb, pbB = [], []
        for i in range(8):
            if i == 4:
                pb.append(st.enter_context(nc.psum_tensor("ptr", [128, 1024], BF16)))
            else:
                pb.append(st.enter_context(nc.psum_tensor(f"pb{i}", [128, 512], F32)))
            pbB.append(Buf(f"pb{i}"))
        ptr = pb[4]

        def PW(i):
            return (pbB[i], "p")

        def mm(bank, out, lhsT, rhs, start, stop, reads, track, tp=None):
            kw = {"tile_position": tp} if tp is not None else {}
            S.op("pe", lambda e: e.matmul(out, lhsT=lhsT, rhs=rhs, start=start, stop=stop, **kw),
                 reads=reads, writes=[PW(bank)], track=track)

        def transpose(bank, out, in_, ident_ap, reads, track):
            S.op("pe", lambda e: e.transpose(out, in_, ident_ap), reads=reads, writes=[PW(bank)], track=track)

        def act(out, in_, func, reads, writes, scale=None, bias=None, accum_out=None):
            kw = {}
            if scale is not None:
                kw["scale"] = scale
            if bias is not None:
                kw["bias"] = bias
            if accum_out is not None:
                kw["accum_out"] = accum_out
            S.op("act", lambda e: e.activation(out=out, in_=in_, func=func, **kw), reads=reads, writes=writes)

        def cp(eng, out, in_, reads, writes):
            if eng == "act":
                S.op("act", lambda e: e.copy(out=out, in_=in_), reads=reads, writes=writes)
            else:
                S.op(eng, lambda e: e.tensor_copy(out=out, in_=in_), reads=reads, writes=writes)

        def tt(eng, out, in0, in1, op, reads, writes):
            S.op(eng, lambda e: e.tensor_tensor(out=out, in0=in0, in1=in1, op=op), reads=reads, writes=writes)

        def ts(eng, out, in0, s1, s2, op0, op1, reads, writes):
            if s2 is None:
                if op0 == ALU.pow:
                    s1, s2, op0, op1 = 1.0, s1, ALU.mult, ALU.pow
                else:
                    s2, op1 = 0.0, ALU.add
            if True:
                S.op(eng, lambda e: e.tensor_scalar(out=out, in0=in0, scalar1=s1, scalar2=s2, op0=op0, op1=op1),
                     reads=reads, writes=writes)

        def stt(eng, out, in0, scalar, in1, op0, op1, reads, writes):
            S.op(eng, lambda e: e.scalar_tensor_tensor(out=out, in0=in0, scalar=scalar, in1=in1, op0=op0, op1=op1),
                 reads=reads, writes=writes)

        def rsqrt_inplace(ap, B):
            act(ap, ap, AF.Ln, [B], [(B, "p")])
            act(ap, ap, AF.Exp, [B], [(B, "p")], scale=-0.5)

        def memset(eng, ap, val, writes, reads=()):
            S.op(eng, lambda e: e.memset(ap, val), reads=reads, writes=writes)

        ident = sbt(st, "ident", [128, 128], BF16)
        identB = Buf("ident")
        ones = sbt(st, "ones", [128, 128], BF16)
        onesB = Buf("ones")
        zer = sbt(st, "zer", [128, 128], BF16)
        zerB = Buf("zer")
        memset("pool", ident[:], 1.0, [identB])
        S.op("pool", lambda e: e.affine_select(out=ident[:], in_=ident[:], pattern=[[-1, 128]],
                                               compare_op=ALU.is_equal, fill=0.0, base=0, channel_multiplier=1),
             reads=[identB], writes=[identB])
        memset("pool", ones[:], 1.0, [onesB])
        memset("pool", zer[:], 0.0, [zerB])
        stg = Ring(nc, st, "stg", [128, 512], F32, 2)
        cast_i = [0]

        def load_w(stack, name, src, K, N):
            kc = K // 128
            wt = sbt(stack, name, [128, kc, N], BF16)
            wb = Buf(name)
            first = True
            for k in range(kc):
                for c0 in range(0, N, 512):
                    cw = min(512, N - c0)
                    t, tb = stg.next()
                    S.dma(t[:, :cw], src[k * 128:(k + 1) * 128, c0:c0 + cw], writes=[tb])
                    eng = ("pool", "dve")[cast_i[0] % 2]
                    cast_i[0] += 1
                    cp(eng, wt[:, k, c0:c0 + cw], t[:, :cw], [tb], [wb] if first else [(wb, "p")])
                    first = False
            return wt, wb

        def load_ws(stack, specs):
            outs = []
            for (name, src, K, N) in specs:
                outs.append((sbt(stack, name, [128, K // 128, N], BF16), Buf(name)))
            with ExitStack() as tmp:
                big = Ring(nc, tmp, "stgb", [128, 1024], F32, 6)
                for (name, src, K, N), (wt, wb) in zip(specs, outs):
                    first = True
                    for k in range(K // 128):
                        for c0 in range(0, N, 1024):
                            cw = min(1024, N - c0)
                            t, tb = big.next()
                            S.dma(t[:, :cw], src[k * 128:(k + 1) * 128, c0:c0 + cw], writes=[tb])
                            eng = ("pool", "dve", "dve")[cast_i[0] % 3]
                            cast_i[0] += 1
                            cp(eng, wt[:, k, c0:c0 + cw], t[:, :cw], [tb], [wb] if first else [(wb, "p")])
                            first = False
                S.barrier()
            return outs

        def load_cols(stack, name, src1d, ncol):
            t = sbt(stack, name, [128, ncol], F32)
            b = Buf(name)
            v = src1d.rearrange("(c p) -> p c", p=128)
            first = True
            for c0 in range(0, ncol, 11):
                c1 = min(ncol, c0 + 11)
                S.dma(t[:, c0:c1], v[:, c0:c1], writes=[b] if first else [(b, "p")], allow_slow_non_contiguous=True)
                first = False
            return t, b

        def load_bcast(stack, name, src1d):
            t = sbt(stack, name, [128, D], F32)
            b = Buf(name)
            S.dma(t[:], src1d.partition_broadcast(128), writes=[b])
            return t, b

        def load_h0_rows(t, tb, s, r0, n):
            if r0 == 0:
                S.dma(t[0:NMETA, :], meta[:, :], writes=[tb])
                S.dma(t[NMETA:n, :], x[s, 0:n - NMETA, :], writes=[(tb, "p")])
            else:
                S.dma(t[0:n, :], x[s, r0 - NMETA:r0 - NMETA + n, :], writes=[tb])

        class FMStore:
            def __init__(self, stack, name, nbuf=2, width=512, q="act"):
                self.q = q
                self.ring = Ring(nc, stack, name, [128, 8, width], BF16, nbuf)
                self.width = width
                self.cur = None

            def add(self, src, srcB, n, dst, col):
                c = self.cur
                if c is None or c["dst"] is not dst or c["col0"] + c["filled"] != col or c["filled"] + n > self.width:
                    self.flush()
                    t, b = self.ring.next()
                    c = self.cur = dict(t=t, b=b, dst=dst, col0=col, filled=0)
                for k in range(8):
                    transpose(4, ptr[:, k * 128:k * 128 + n], src[:n, k * 128:(k + 1) * 128], ident[:n, :n],
                              [srcB, identB], track=(k == 7))
                f = c["filled"]
                cp("act", c["t"][:, :, f:f + n], ptr[:, :].rearrange("p (c j) -> p c j", j=128)[:, :, :n], [],
                   [c["b"] if f == 0 else (c["b"], "p"), PW(4)])
                c["filled"] = f + n
                if c["filled"] >= self.width:
                    self.flush()

            def flush(self):
                c = self.cur
                if c is not None and c["filled"] > 0:
                    S.dma(c["dst"][:, :, c["col0"]:c["col0"] + c["filled"]], c["t"][:, :, :c["filled"]], reads=[c["b"]], q=self.q)
                self.cur = None

        def to_featmajor(fms, src, srcB, n, dst, col):
            fms.add(src, srcB, n, dst, col)

        class LNCtx:
            def __init__(self, stack, layer, idx, nr=2, nxs=2, nybf=1, nyo=1, xsw=512, stq="act"):
                self.stq = stq
                self.g, self.gB = load_bcast(stack, "lng", lng_d[layer, idx, :])
                self.b, self.bB = load_bcast(stack, "lnb", lnb_d[layer, idx, :])
                self.r = Ring(nc, stack, "ln_r", [128, D], F32, nr)
                self.yo = Ring(nc, stack, "ln_y", [128, D], F32, nyo) if nyo > 0 else None
                self.ybf = Ring(nc, stack, "ln_ybf", [128, D], BF16, nybf)
                self.xs = FMStore(stack, "ln_xs", nbuf=nxs, width=xsw, q=stq)
                self.st = Ring(nc, stack, "ln_st", [128, 2, 6], F32, 2)
                self.mv = Ring(nc, stack, "ln_mv", [128, 4], F32, 2)

        def ln_epilogue(cx, banks, n, res, resB, out_rows, outT, row_lo=0):
            r, rB = cx.r.next()
            for h in range(2):
                stt("dve", r[:n, h * 512:(h + 1) * 512], res[:n, h * 512:(h + 1) * 512], ALPHA,
                    pb[banks[h]][:n, :], ALU.mult, ALU.add, [resB], [(rB, "p"), PW(banks[h])])
            sT, sB = cx.st.next()
            mv, mvB = cx.mv.next()
            for h in range(2):
                S.op("dve", (lambda o, i: (lambda e: e.bn_stats(out=o, in_=i)))(sT[:n, h, :], r[:n, h * 512:(h + 1) * 512]),
                     reads=[rB], writes=[(sB, "p")])
            S.op("dve", lambda e: e.bn_aggr(out=mv[:n, 0:2], in_=sT[:n, :, :].rearrange("p a b -> p (a b)")),
                 reads=[sB], writes=[(mvB, "p")])
            ts("dve", mv[:n, 2:3], mv[:n, 1:2], EPS, None, ALU.add, None, [mvB], [(mvB, "p")])
            rsqrt_inplace(mv[:n, 2:3], mvB)
            stt("dve", mv[:n, 3:4], mv[:n, 0:1], -1.0, mv[:n, 2:3], ALU.mult, ALU.mult, [mvB], [(mvB, "p")])
            if cx.yo is not None:
                yo, yB = cx.yo.next()
                act(yo[:n, :], r[:n, :], AF.Identity, [rB, mvB], [yB], scale=mv[:n, 2:3], bias=mv[:n, 3:4])
            else:
                yo, yB = r, rB
                act(yo[:n, :], r[:n, :], AF.Identity, [rB, mvB], [(rB, "p")], scale=mv[:n, 2:3], bias=mv[:n, 3:4])
            tt("dve", yo[:n, :], yo[:n, :], cx.g[:n, :], ALU.mult, [cx.gB, yB], [(yB, "p")])
            tt("dve", yo[:n, :], yo[:n, :], cx.b[:n, :], ALU.add, [cx.bB, yB], [(yB, "p")])
            yb = ybB = None
            if outT is not None:
                yb, ybB = cx.ybf.next()
                cp("act", yb[:n, :], yo[:n, :], [yB], [ybB])

            def fin():
                if out_rows is not None:
                    S.dma(out_rows, yo[row_lo:n, :], reads=[yB], q=cx.stq)
                if outT is not None:
                    to_featmajor(cx.xs, yb, ybB, n, outT[0], outT[1])
            return fin

        def zero_cols(dst3):
            w = dst3.shape[2]
            for c in range(8):
                S.dma(dst3[:, c, :], zer[:, 0:w], reads=[zerB], allow_slow_non_contiguous=True)

        def phase_T0():
            with ExitStack() as ph:
                xr = Ring(nc, ph, "t0x", [128, D], F32, 2)
                xb = Ring(nc, ph, "t0b", [128, D], BF16, 2)
                xs = FMStore(ph, "t0s", nbuf=2)
                for s in range(NS):
                    XTv = fm(XT[s])
                    for r0 in range(0, L, 128):
                        n = min(128, L - r0)
                        t, tb = xr.next()
                        load_h0_rows(t, tb, s, r0, n)
                        u, ub = xb.next()
                        cp("dve", u[:n, :], t[:n, :], [tb], [ub])
                        to_featmajor(xs, u, ub, n, XTv, r0)
                xs.flush()
                S.barrier()

        def phase_mla_proj():
            with ExitStack() as ph:
                ((w_in, w_inB),) = load_ws(ph, [("w_in", w_in_d, D, QL + KVL + DR)])
                wr = sbt(ph, "w_inr", [128, 8, 96], BF16)
                wrB = Buf("w_inr")
                memset("pool", wr[:], 0.0, [wrB])
                act(wr[:, :, 64:80], w_in[:, :, 1040:1056], AF.Identity, [w_inB], [(wrB, "p")], scale=-1.0)
                cp("pool", wr[:, :, 80:96], w_in[:, :, 1024:1040], [w_inB], [(wrB, "p")])
                gq, gqB = load_cols(ph, "gq", qn_d, 6)
                gkv, gkvB = load_cols(ph, "gkv", kvn_d, 2)
                cosT = sbt(ph, "cosT", [128, L], F32)
                sinT = sbt(ph, "sinT", [128, L], F32)
                csB = Buf("cs")
                S.dma(cosT[64:96, :], cos_d[:, :], writes=[csB])
                S.dma(sinT[64:96, :], sin_d[:, :], writes=[(csB, "p")])
                xt_r = Ring(nc, ph, "xt", [128, 8, 512], BF16, 2)
                sq_r = Ring(nc, ph, "sq", [128, 512], BF16, 2)
                craw = sbt(ph, "craw", [128, 8, 512], F32)
                crawB = Buf("craw")
                rq_r = Ring(nc, ph, "rq", [128, 2, 512], F32, 2)
                cn_r = Ring(nc, ph, "cn", [128, 512], BF16, 3)
                t1_r = Ring(nc, ph, "rt1", [128, 512], F32, 2)
                t2_r = Ring(nc, ph, "rt2", [128, 512], F32, 2)
                kr_r = Ring(nc, ph, "krt", [128, 512], BF16, 2)
                for s in range(NS):
                    XTv = fm(XT[s])
                    CQv = fm(CQ[s])
                    CKVv = fm(CKV[s])
                    ptiles = [(t0, min(512, L - t0)) for t0 in range(0, L, 512)]
                    xts = {}

                    def load_xt(i):
                        t0_, w_ = ptiles[i]
                        xt_, xtB_ = xt_r.next()
                        S.dma(xt_[:, :, :w_], XTv[:, :, t0_:t0_ + w_], writes=[xtB_])
                        xts[i] = (xt_, xtB_)

                    load_xt(0)
                    for ti_, (t0, w) in enumerate(ptiles):
                        xt, xtB = xts.pop(ti_)
                        if ti_ + 1 < len(ptiles):
                            load_xt(ti_ + 1)
                        for m in range(8):
                            bk = m % 2
                            for k in range(8):
                                mm(bk, pb[bk][:, :w], w_in[:, k, m * 128:(m + 1) * 128], xt[:, k, :w],
                                   k == 0, k == 7, [w_inB, xtB], k == 7)
                            sq, sqB = sq_r.next()
                            act(sq[:, :w], pb[bk][:, :w], AF.Square, [], [sqB, PW(bk)])
                            cp("dve", craw[:, m, :w], pb[bk][:, :w], [], [(crawB, "p"), PW(bk)])
                            sb_ = 2 if m < 6 else 3
                            mm(sb_, pb[sb_][:, :w], ones[:, :], sq[:, :w], m in (0, 6), m in (5, 7), [onesB, sqB], True)
                        rq, rqB = rq_r.next()
                        ts("dve", rq[:, 0, :w], pb[2][:, :w], 1.0 / QL, EPS, ALU.mult, ALU.add, [], [(rqB, "p"), PW(2)])
                        rsqrt_inplace(rq[:, 0, :w], rqB)
                        ts("dve", rq[:, 1, :w], pb[3][:, :w], 1.0 / KVL, EPS, ALU.mult, ALU.add, [], [(rqB, "p"), PW(3)])
                        rsqrt_inplace(rq[:, 1, :w], rqB)
                        for m in range(8):
                            cn, cnB = cn_r.next()
                            g_ap = gq[:, m:m + 1] if m < 6 else gkv[:, m - 6:m - 5]
                            stt("dve", cn[:, :w], craw[:, m, :w], g_ap, rq[:, 0 if m < 6 else 1, :w],
                                ALU.mult, ALU.mult, [crawB, gqB, gkvB, rqB], [cnB])
                            dst = CQv[:, m, t0:t0 + w] if m < 6 else CKVv[:, m - 6, t0:t0 + w]
                            S.dma(dst, cn[:, :w], reads=[cnB])
                        for k in range(8):
                            mm(5, pb[5][0:96, :w], w_in[:, k, 960:1056], xt[:, k, :w], k == 0, k == 7, [w_inB, xtB], k == 7)
                        for k in range(8):
                            mm(6, pb[6][0:96, :w], wr[:, k, :], xt[:, k, :w], k == 0, k == 7, [wrB, xtB], k == 7)
                        t1, t1B = t1_r.next()
                        t2, t2B = t2_r.next()
                        kr, krB = kr_r.next()
                        tt("dve", t1[64:96, :w], pb[5][64:96, :w], cosT[64:96, t0:t0 + w], ALU.mult, [csB], [t1B, PW(5)])
                        tt("dve", t2[64:96, :w], pb[6][64:96, :w], sinT[64:96, t0:t0 + w], ALU.mult, [csB], [t2B, PW(6)])
                        tt("pool", kr[64:96, :w], t1[64:96, :w], t2[64:96, :w], ALU.add, [t1B, t2B], [krB])
                        S.dma(KR[s, :, t0:t0 + w], kr[64:96, :w], reads=[krB])
                S.barrier()

        def phase_attn():
            with ExitStack() as ph:
                (w_uq, w_uqB), (w_ukv, w_ukvB) = load_ws(ph, [("w_uq", w_uq_d, QL, NH * 96), ("w_ukv", w_ukv_d, KVL, NH * 128)])
                wqr = sbt(ph, "w_uqr", [128, 6, NH * 96], BF16)
                wqrB = Buf("wqr")
                memset("pool", wqr[:], 0.0, [wqrB])
                for k in range(6):
                    a4 = w_uq[:, k, :].rearrange("p (h d) -> p h d", d=96)
                    r4 = wqr[:, k, :].rearrange("p (h d) -> p h d", d=96)
                    act(r4[:, :, 64:80], a4[:, :, 80:96], AF.Identity, [w_uqB], [(wqrB, "p")], scale=-1.0)
                    cp("pool", r4[:, :, 80:96], a4[:, :, 64:80], [w_uqB], [(wqrB, "p")])
                cosT = sbt(ph, "cosT", [128, L], F32)
                sinT = sbt(ph, "sinT", [128, L], F32)
                csB = Buf("cs")
                S.dma(cosT[64:96, :], cos_d[:, :], writes=[csB])
                S.dma(sinT[64:96, :], sin_d[:, :], writes=[(csB, "p")])
                cq = sbt(ph, "cq", [128, 6, L], BF16)
                cqB = Buf("cq")
                ckv = sbt(ph, "ckv", [128, 2, L], BF16)
                ckvB = Buf("ckv")
                KT = [sbt(ph, f"KT{i}", [128, L], BF16) for i in range(2)]
                KTB = [Buf(f"KT{i}") for i in range(2)]
                VA = [sbt(ph, f"VA{i}", [128, NKT, 128], BF16) for i in range(2)]
                VAB = [Buf(f"VA{i}") for i in range(2)]
                for i in range(2):
                    memset("pool", VA[i][:, :, 64:128], 1.0, [VAB[i]])
                QT_r = Ring(nc, ph, "QT", [128, 512], BF16, 2)
                PT_r = Ring(nc, ph, "PT", [128, 512], BF16, 4)
                t1_r = Ring(nc, ph, "at1", [128, 512], F32, 2)
                t2_r = Ring(nc, ph, "at2", [128, 512], F32, 2)
                rd_r = Ring(nc, ph, "rden", [128, 512], F32, 2)
                on_r = Ring(nc, ph, "on", [128, 512], BF16, 2)
                st_banks = [0, 1, 2]
                ot_banks = [3, 5]
                sti = [0]
                oti = [0]
                items = [(s, h, t0) for s in range(NS) for h in range(NH) for t0 in range(0, L, 512)]
                qbuf = {}

                def load_seq(s):
                    S.dma(cq[:, :, :], fm(CQ[s])[:, :, :], writes=[cqB])
                    S.dma(ckv[:, :, :], fm(CKV[s])[:, :, :], writes=[ckvB])
                    for i in range(2):
                        S.dma(KT[i][64:96, :], KR[s, :, :], writes=[(KTB[i], "p")])

                def build_kv(s, h):
                    hb = h % 2
                    for t0 in range(0, L, 512):
                        w = min(512, L - t0)
                        for k in range(2):
                            mm(6, pb[6][0:64, :w], w_ukv[:, k, h * 128:h * 128 + 64], ckv[:, k, t0:t0 + w],
                               k == 0, k == 1, [w_ukvB, ckvB], k == 1)
                        cp("dve", KT[hb][0:64, t0:t0 + w], pb[6][0:64, :w], [], [(KTB[hb], "p"), PW(6)])
                    for g0 in range(0, NKT, 8):
                        g1 = min(NKT, g0 + 8)
                        for kt in range(g0, g1):
                            nk = min(128, L - kt * 128)
                            for k in range(2):
                                mm(7, pb[7][:nk, (kt - g0) * 64:(kt - g0 + 1) * 64], ckv[:, k, kt * 128:kt * 128 + nk],
                                   w_ukv[:, k, h * 128 + 64:h * 128 + 128], k == 0, k == 1, [w_ukvB, ckvB],
                                   (k == 1))
                        cp("dve", VA[hb][:, g0:g1, 0:64],
                           pb[7][:, 0:(g1 - g0) * 64].rearrange("p (j d) -> p j d", d=64), [], [(VAB[hb], "p"), PW(7)])

                def build_q(s, h, t0):
                    w = min(512, L - t0)
                    for k in range(6):
                        mm(6, pb[6][0:96, :w], w_uq[:, k, h * 96:(h + 1) * 96], cq[:, k, t0:t0 + w],
                           k == 0, k == 5, [w_uqB, cqB], k == 5)
                    for k in range(6):
                        mm(7, pb[7][0:96, :w], wqr[:, k, h * 96:(h + 1) * 96], cq[:, k, t0:t0 + w],
                           k == 0, k == 5, [wqrB, cqB], k == 5)
                    QT, QTB = QT_r.next()
                    t1, t1B = t1_r.next()
                    t2, t2B = t2_r.next()
                    cp("act", QT[0:64, :w], pb[6][0:64, :w], [], [(QTB, "p"), PW(6)])
                    tt("dve", t1[64:96, :w], pb[6][64:96, :w], cosT[64:96, t0:t0 + w], ALU.mult, [csB], [t1B, PW(6)])
                    tt("dve", t2[64:96, :w], pb[7][64:96, :w], sinT[64:96, t0:t0 + w], ALU.mult, [csB], [t2B, PW(7)])
                    tt("pool", QT[64:96, :w], t1[64:96, :w], t2[64:96, :w], ALU.add, [t1B, t2B], [(QTB, "p")])
                    qbuf[(s, h, t0)] = (QT, QTB)

                def prep(j):
                    s, h, t0 = items[j]
                    if j == 0 or items[j - 1][0] != s:
                        load_seq(s)
                    if j == 0 or items[j - 1][:2] != (s, h):
                        build_kv(s, h)
                    build_q(s, h, t0)

                prep(0)
                for j, (s, h, t0) in enumerate(items):
                    hb = h % 2
                    w = min(512, L - t0)
                    QT, QTB = qbuf.pop((s, h, t0))
                    ob = ot_banks[oti[0] % 2]
                    oti[0] += 1
                    has_next = j + 1 < len(items)
                    early = has_next and items[j + 1][0] == s
                    stb = {}

                    def qk(kt):
                        nk = min(128, L - kt * 128)
                        sbk = st_banks[sti[0] % 3]
                        sti[0] += 1
                        stb[kt] = sbk
                        mm(sbk, pb[sbk][:nk, :w], KT[hb][0:96, kt * 128:kt * 128 + nk], QT[0:96, :w],
                           True, True, [KTB[hb], QTB], True)

                    LA = 3
                    for i0 in range(min(LA, NKT)):
                        qk(i0)
                    for kt in range(NKT):
                        nk = min(128, L - kt * 128)
                        sbk = stb[kt]
                        PT, PTB = PT_r.next()
                        act(PT[:nk, :w], pb[sbk][:nk, :w], AF.Exp, [], [PTB, PW(sbk)], scale=MLA_SCALE)
                        mm(ob, pb[ob][:, :w], VA[hb][0:nk, kt, :], PT[:nk, :w], kt == 0, kt == NKT - 1,
                           [VAB[hb], PTB], kt == NKT - 1)
                        if kt + LA < NKT:
                            qk(kt + LA)
                        if early and kt == min(6, NKT - 1):
                            prep(j + 1)
                    rd, rdB = rd_r.next()
                    on, onB = on_r.next()
                    S.op("dve", (lambda o, i: (lambda e: e.reciprocal(out=o, in_=i)))(rd[64:128, :w], pb[ob][64:128, :w]),
                         reads=[], writes=[rdB, PW(ob)])
                    tt("dve", on[0:64, :w], pb[ob][0:64, :w], rd[64:128, :w], ALU.mult, [rdB], [onB, PW(ob)])
                    S.dma(OT[s, h * 64:(h + 1) * 64, t0:t0 + w], on[0:64, :w], reads=[onB])
                    if has_next and not early:
                        prep(j + 1)
                S.barrier()

        def phase_mix_out(name, w_d, src_T, src_off, res_fn, layer, dst_tok, dst_T, dst_off):
            with ExitStack() as ph:
                ((wo, woB),) = load_ws(ph, [(name, w_d, D, D)])
                cx = LNCtx(ph, layer, 0, nr=3, nxs=2, nybf=3, nyo=3)
                ot_r = Ring(nc, ph, "ot", [128, 8, 512], BF16, 2)
                rs_r = Ring(nc, ph, "res", [128, D], F32, 3)
                mi = [0]
                pend = []
                for s in range(NS):
                    srcv = fm(src_T[s])
                    dstv = fm(dst_T[s])
                    zero_cols(dstv[:, :, 0:dst_off])
                    tail = dst_T.shape[2] - dst_off - L
                    zero_cols(dstv[:, :, dst_off + L:dst_off + L + tail])
                    for r0 in range(0, L, 128):
                        n = min(128, L - r0)
                        if r0 % 512 == 0:
                            ot, otB = ot_r.next()
                            gw = min(512, L - r0)
                            S.dma(ot[:, :, :gw], srcv[:, :, src_off + r0:src_off + r0 + gw], writes=[otB])
                        o0 = r0 % 512
                        rs, rsB = rs_r.next()
                        res_fn(rs, rsB, s, r0, n)
                        bks = ((0, 1), (2, 3))[mi[0] % 2]
                        mi[0] += 1
                        for h in range(2):
                            for c in range(8):
                                mm(bks[h], pb[bks[h]][:n, :], ot[:, c, o0:o0 + n], wo[:, c, h * 512:(h + 1) * 512], c == 0, c == 7,
                                   [otB, woB], c == 7)
                        if len(pend) >= 2:
                            pend.pop(0)()
                        pend.append(ln_epilogue(cx, bks, n, rs, rsB, dst_tok[s, r0:r0 + n, :],
                                                (dstv, dst_off + r0)))
                while pend:
                    pend.pop(0)()
                cx.xs.flush()
                S.barrier()

        def phase_ffn(layer, src_T, src_tok, dst_tok, dst_T, dst_off, final):
            TF = 510
            with ExitStack() as ph:
                (wup, wupB), (wdn, wdnB) = load_ws(ph, [("wup", f_up_d[layer], D, 2 * DFF), ("wdn", f_dn_d[layer], DFF, D)])
                cw = [load_cols(ph, f"cw{j}", f_cw_d[layer, j, :], 44) for j in range(3)]
                cb, cbB = load_cols(ph, "cb", f_cb_d[layer, :], 44)
                cwB = [c[1] for c in cw] + [cbB]
                cx = LNCtx(ph, layer, 1, nr=2, nxs=1, nyo=0, xsw=384)
                xt_r = Ring(nc, ph, "fxt", [128, 8, TF + 2], BF16, 1)
                G = sbt(ph, "G", [128, 22, TF + 2], BF16)
                GBs = [Buf(f"G{j}") for j in range(22)]
                a1_r = Ring(nc, ph, "a1", [128, TF + 2], F32, 2)
                cv_r = Ring(nc, ph, "cv", [128, TF + 2], F32, 2)
                cg_r = Ring(nc, ph, "cg", [128, TF + 2], F32, 2)
                rs_r = Ring(nc, ph, "res", [128, D], F32, 1)
                bi = [0]
                di = [0]
                pend = [None]
                dn_banks = [(5, 6), (5, 6)]
                for s in range(NS):
                    srcv = fm(src_T[s])
                    if dst_T is not None:
                        dstv = fm(dst_T[s])
                        zero_cols(dstv[:, :, 0:dst_off])
                        tail = dst_T.shape[2] - dst_off - L
                        for z0 in range(0, tail, 128):
                            zero_cols(dstv[:, :, dst_off + L + z0:dst_off + L + min(tail, z0 + 128)])
                    for t0 in range(0, L, TF):
                        w = min(TF, L - t0)
                        xt, xtB = xt_r.next()
                        S.dma(xt[:, :, :w + 2], srcv[:, :, t0:t0 + w + 2], writes=[xtB])
                        fin = None
                        for j in range(22):
                            res = []
                            for part in range(2):
                                bk = (0, 1, 2, 3, 7)[bi[0] % 5]
                                bi[0] += 1
                                col = part * DFF + j * 128
                                for k in range(8):
                                    mm(bk, pb[bk][:, :w + 2], wup[:, k, col:col + 128], xt[:, k, :w + 2], k == 0, k == 7,
                                       [wupB, xtB], k == 7)
                                c, cB = (cv_r if part == 0 else cg_r).next()
                                jj = part * 22 + j
                                a1, a1B = a1_r.next()
                                act(c[:, :w], pb[bk][:, 0:w], AF.Identity, cwB, [cB, PW(bk)],
                                    scale=cw[0][0][:, jj:jj + 1], bias=cb[:, jj:jj + 1])
                                act(a1[:, :w], pb[bk][:, 1:w + 1], AF.Identity, cwB, [a1B, PW(bk)],
                                    scale=cw[1][0][:, jj:jj + 1])
                                stt("dve", c[:, :w], pb[bk][:, 2:w + 2], cw[2][0][:, jj:jj + 1], c[:, :w], ALU.mult, ALU.add,
                                    cwB + [cB], [(cB, "p"), PW(bk)])
                                tt("dve", c[:, :w], c[:, :w], a1[:, :w], ALU.add, [cB, a1B], [(cB, "p")])
                                res.append((c, cB))
                                if part == 0 and fin is not None:
                                    fin()
                                    fin = None

                            def mk_fin(j_, res_):
                                def f():
                                    act(res_[1][0][:, :w], res_[1][0][:, :w], AF.Silu, [res_[1][1]], [(res_[1][1], "p")])
                                    tt("pool", G[:, j_, :w], res_[1][0][:, :w], res_[0][0][:, :w], ALU.mult,
                                       [res_[1][1], res_[0][1]], [GBs[j_]])
                                return f
                            fin = mk_fin(j, res)
                        fin()
                        for sub in range(0, w, 128):
                            n = min(128, w - sub)
                            r0 = t0 + sub
                            rs, rsB = rs_r.next()
                            S.dma(rs[:n, :], src_tok[s, r0:r0 + n, :], writes=[rsB])
                            bks = dn_banks[di[0] % 2]
                            di[0] += 1
                            for jr in (range(0, 20), range(20, 22)):
                                for h in range(2):
                                    for j in jr:
                                        mm(bks[h], pb[bks[h]][:n, :], G[:, j, sub:sub + n], wdn[:, j, h * 512:(h + 1) * 512],
                                           j == 0, j == 21, [GBs[j], wdnB], j == 21 or j == 19)
                            if pend[0] is not None:
                                pend[0]()
                                pend[0] = None
                            if final:
                                lo = max(r0, NMETA)
                                if lo < r0 + n:
                                    pend[0] = ln_epilogue(cx, bks, n, rs, rsB, y[s, lo - NMETA:r0 + n - NMETA, :], None,
                                                          row_lo=lo - r0)
                            else:
                                pend[0] = ln_epilogue(cx, bks, n, rs, rsB, dst_tok[s, r0:r0 + n, :],
                                                      (dstv, dst_off + r0))
                if pend[0] is not None:
                    pend[0]()
                cx.xs.flush()
                S.barrier()

        def phase_hgrn(dirn, ODST):
            rev = dirn == 1
            with ExitStack() as ph:
                if rev:
                    wq = wqB = wi = wiB = None
                else:
                    (wq, wqB), (wi, wiB) = load_ws(ph, [("hwq", hw_in_d[:, 0:D], D, D), ("hwi", hw_in_d[:, D:2 * D], D, D)])
                ((wf, wfB),) = load_ws(ph, [("hwf", hw_in_d[:, (2 + dirn) * D:(3 + dirn) * D], D, D)])
                l0, l0B = load_cols(ph, "lb0", hlb_d[dirn, 0, :], 8)
                l1, l1B = load_cols(ph, "lb1", hlb_d[dirn, 1, :], 8)
                lb = sbt(ph, "lb", [128, 8], F32)
                oml = sbt(ph, "oml", [128, 8], F32)
                lbB = Buf("lb")
                tt("dve", lb[:, :], l0[:, :], l1[:, :], ALU.subtract, [l0B, l1B], [lbB])
                act(lb[:, :], lb[:, :], AF.Exp, [lbB], [(lbB, "p")])
                ts("dve", lb[:, :], lb[:, :], 1.0, None, ALU.add, None, [lbB], [(lbB, "p")])
                S.op("dve", lambda e: e.reciprocal(out=lb[:, :], in_=lb[:, :]), reads=[lbB], writes=[(lbB, "p")])
                ts("dve", oml[:, :], lb[:, :], -1.0, 1.0, ALU.mult, ALU.add, [lbB], [(lbB, "p")])
                smask = sbt(ph, "smask", [128, 512], F32)
                smB = Buf("smask")
                memset("pool", smask[:], 1.0, [smB])
                memset("pool", smask[:].rearrange("p (c j) -> p c j", j=64)[:, :, 0:1], 0.0, [(smB, "p")], reads=[smB])
                tri = sbt(ph, "tri", [128, 64], F32)
                triB = Buf("tri")
                memset("pool", tri[:], 1.0, [triB])
                S.op("pool", lambda e: e.affine_select(out=tri[0:64, :], in_=tri[0:64, :], pattern=[[-1 if rev else 1, 64]],
                                                       compare_op=ALU.is_ge, fill=0.0, base=0,
                                                       channel_multiplier=(1 if rev else -1)), reads=[triB], writes=[(triB, "p")])
                cp("act", tri[64:128, :], tri[0:64, :], [triB], [(triB, "p")])
                S32 = sbt(ph, "S32", [128, 8, 128], F32)
                Sbf = sbt(ph, "Sbf", [128, 8, 128], BF16)
                S32B = [Buf(f"S32_{h}") for h in range(8)]
                SbfB = [Buf(f"Sbf_{h}") for h in range(8)]
                bt_r = Ring(nc, ph, "bt", [128, 8, 512], BF16, 2)
                vt_r = Ring(nc, ph, "vt", [128, 4, D], BF16, 2)
                NSET = 2
                qd = [[sbt(ph, f"qd{a}_{h}", [128, 512], BF16) for h in range(8)] for a in range(NSET)]
                qdec = [[sbt(ph, f"qe{a}_{h}", [128, 512], BF16) for h in range(8)] for a in range(NSET)]
                kd = [[sbt(ph, f"kd{a}_{h}", [128, 512], BF16) for h in range(8)] for a in range(NSET)]
                kdT = [[sbt(ph, f"kT{a}_{h}", [128, 4, 128], BF16) for h in range(8)] for a in range(NSET)]
                dl = [[sbt(ph, f"dl{a}_{h}", [128, 8], F32) for h in range(8)] for a in range(NSET)]
                prepB = [[Buf(f"prep{a}_{h}") for h in range(8)] for a in range(NSET)]
                kdec_r = Ring(nc, ph, "kdec", [128, 512], BF16, 2)
                ef = [Ring(nc, ph, f"ef{i}", [128, 512], F32, 2) for i in range(6)]
                am_r = Ring(nc, ph, "am", [128, 64], BF16, 4)
                os_r = Ring(nc, ph, "osb", [128, 512], F32, 2)
                tiles = [(t0, min(512, LP - t0)) for t0 in range(0, LP, 512)]
                if rev:
                    tiles = tiles[::-1]
                pu_banks = [3, 7]
                pui = [0]
                pendA2 = [None]
                pendA1b = [None]
                qcur = {}
                SbfG = [Buf("SbfG0"), Buf("SbfG1")]
                amA_r = Ring(nc, ph, "amA", [128, 8, 64], BF16, 3)
                cur = {}

                qall_r = Ring(nc, ph, "qall", [128, 8, 512], F32, 2) if rev else None
                pre = {}

                def loads(s, ti):
                    t0, w = tiles[ti]
                    nb = (w + 127) // 128
                    bt, btB = bt_r.next()
                    S.dma(bt[:, :, :w], fm(B_T[s])[:, :, t0:t0 + w], writes=[btB])
                    vt, vtB = vt_r.next()
                    qa = qaB = None
                    if rev:
                        S.dma(vt[:, 0:nb, :], VT_d[s, t0:t0 + w, :].rearrange("(b p) d -> p b d", p=128), writes=[vtB])
                        qa, qaB = qall_r.next()
                        S.dma(qa[:, :, :w], Q_d[s].rearrange("(h p) t -> p h t", p=128)[:, :, t0:t0 + w], writes=[qaB])
                    pre[ti] = (bt, btB, vt, vtB, qa, qaB)

                def stageA_common(s, ti):
                    t0, w = tiles[ti]
                    nb = (w + 127) // 128
                    if ti not in pre:
                        loads(s, ti)
                    bt, btB, vt, vtB, qa, qaB = pre.pop(ti)
                    qcur[ti] = (qa, qaB)
                    vview = VT_d[s, t0:t0 + w, :].rearrange("(b p) d -> p b d", p=128)
                    if rev:
                        pass
                    else:
                        for blk in range(nb):
                            n = min(128, w - blk * 128)
                            for hf in range(2):
                                for k in range(8):
                                    mm(2, pb[2][:n, :], bt[:, k, blk * 128:blk * 128 + n], wi[:, k, hf * 512:(hf + 1) * 512],
                                       k == 0, k == 7, [btB, wiB], k == 7)
                                cp("act", vt[:n, blk, hf * 512:(hf + 1) * 512], pb[2][:n, :], [], [(vtB, "p"), PW(2)])
                        S.dma(vview, vt[:, 0:nb, :], reads=[vtB], q="act")
                    cur[ti] = (bt, btB, vt, vtB)

                def stageA_head(s, ti, h):
                    t0, w = tiles[ti]
                    a = ti % NSET
                    nch = w // 64
                    nb = (w + 127) // 128
                    bt, btB, vt, vtB = cur[ti]
                    PB = prepB[a][h]
                    for k in range(8):
                        mm(0, pb[0][:, :w], wf[:, k, h * 128:(h + 1) * 128], bt[:, k, :w], k == 0, k == 7, [wfB, btB], k == 7)
                    if not rev:
                        for k in range(8):
                            mm(1, pb[1][:, :w], wq[:, k, h * 128:(h + 1) * 128], bt[:, k, :w], k == 0, k == 7, [wqB, btB], k == 7)
                    e0, e0B = ef[0].next()
                    e1, e1B = ef[1].next()
                    e2, e2B = ef[2].next()
                    e3, e3B = ef[3].next()
                    e4, e4B = ef[4].next()
                    e5, e5B = ef[5].next()
                    act(e0[:, :w], pb[0][:, :w], AF.Exp, [], [e0B, PW(0)], scale=-1.0)
                    if rev:
                        qa, qaB = qcur[ti]
                        e5, e5B = qa[:, h, :], qaB
                    else:
                        cp("act", e5[:, :w], pb[1][:, :w], [], [e5B, PW(1)])
                        S.dma(Q_d[s, h * 128:(h + 1) * 128, t0:t0 + w], e5[:, :w], reads=[e5B], q="act")
                    act(e0[:, :w], e0[:, :w], AF.Ln, [e0B], [(e0B, "p")], bias=1.0)
                    act(e0[:, :w], e0[:, :w], AF.Exp, [e0B], [(e0B, "p")], scale=-1.0)
                    ts("dve", e0[:, :w], e0[:, :w], oml[:, h:h + 1], lb[:, h:h + 1], ALU.mult, ALU.add, [e0B, lbB], [(e0B, "p")])
                    ts("pool", e2[:, :w], e0[:, :w], -1.0, 1.0, ALU.mult, ALU.add, [e0B], [e2B])
                    act(e1[:, :w], e0[:, :w], AF.Ln, [e0B], [e1B])
                    for (p0, p1) in ((0, 48), (L + 48, LP)):
                        lo, hi = max(p0, t0), min(p1, t0 + w)
                        if lo < hi:
                            memset("pool", e1[:, lo - t0:hi - t0], 0.0, [(e1B, "p")], reads=[e1B])
                            memset("pool", e2[:, lo - t0:hi - t0], 0.0, [(e2B, "p")], reads=[e2B])
                    S.op("dve", (lambda o, m_, i: (lambda e: e.tensor_tensor_scan(out=o, data0=m_, data1=i, initial=0.0,
                                                                                 op0=ALU.mult, op1=ALU.add)))(
                        e3[:, :w], smask[:, :w], e1[:, :w]), reads=[smB, e1B], writes=[e3B])
                    B3 = e3[:, :w].rearrange("p (c j) -> p c j", j=64)
                    if rev:
                        tt("pool", e1[:, :w], e1[:, :w], e3[:, :w], ALU.subtract, [e1B, e3B], [(e1B, "p")])
                        tt("dve", B3, e1[:, :w].rearrange("p (c j) -> p c j", j=64),
                           B3[:, :, 63:64].to_broadcast([128, nch, 64]), ALU.add, [e1B, e3B], [(e3B, "p")])
                        i_ref, i_last = 31, 0
                    else:
                        i_ref, i_last = 32, 63
                    def a1b():
                        a1b_body(h, a, w, nch, nb, PB, e0, e0B, e1, e1B, e2, e2B, e3, e3B, e4, e4B, e5, e5B, B3, i_ref, i_last)
                    if pendA1b[0] is not None:
                        pendA1b[0]()
                    pendA1b[0] = a1b

                def a1b_body(h, a, w, nch, nb, PB, e0, e0B, e1, e1B, e2, e2B, e3, e3B, e4, e4B, e5, e5B, B3, i_ref, i_last):
                    E4 = e4[:, :w].rearrange("p (c j) -> p c j", j=64)
                    tt("dve", E4, B3, B3[:, :, i_ref:i_ref + 1].to_broadcast([128, nch, 64]), ALU.subtract, [e3B], [e4B])
                    act(e0[:, :w], e4[:, :w], AF.Exp, [e4B], [e0B])
                    act(e1[:, :w], e4[:, :w], AF.Exp, [e4B], [e1B], scale=-1.0)
                    tt("dve", qd[a][h][:, :w], e5[:, :w], e0[:, :w], ALU.mult, [e5B, e0B], [(PB, "p")])
                    tt("pool", kd[a][h][:, :w], e2[:, :w], e1[:, :w], ALU.mult, [e2B, e1B], [(PB, "p")])
                    act(e0[:, :w], e3[:, :w], AF.Exp, [e3B, PB], [e0B])
                    tt("dve", qdec[a][h][:, :w], e5[:, :w], e0[:, :w], ALU.mult, [e5B, e0B], [(PB, "p")])
                    tt("pool", E4, B3, B3[:, :, i_last:i_last + 1].to_broadcast([128, nch, 64]), ALU.subtract, [e3B, e1B], [e4B])
                    act(e1[:, :w], e4[:, :w], AF.Exp, [e4B, PB], [e1B], scale=-1.0)
                    kdc, kdcB = kdec_r.next()
                    tt("pool", kdc[:, :w], e2[:, :w], e1[:, :w], ALU.mult, [e2B, e1B], [kdcB])
                    act(dl[a][h][:, 0:nch], B3[:, :, i_last], AF.Exp, [e3B], [(PB, "p")])
                    def a2():
                        for blk in range(nb):
                            n = min(128, w - blk * 128)
                            transpose(4, ptr[:n, blk * 128:(blk + 1) * 128], kdc[:, blk * 128:blk * 128 + n], ident[:, :],
                                      [kdcB, identB], blk == nb - 1)
                        cp("act", kdT[a][h][:, 0:nb, :], ptr[:, 0:nb * 128].rearrange("p (b d) -> p b d", d=128),
                           [], [(PB, "p"), PW(4)])
                    if pendA2[0] is not None:
                        pendA2[0]()
                    pendA2[0] = a2

                def flushA2():
                    if pendA1b[0] is not None:
                        pendA1b[0]()
                        pendA1b[0] = None
                    if pendA2[0] is not None:
                        pendA2[0]()
                        pendA2[0] = None

                def stageB_chunk(s, ti, c):
                    t0, w = tiles[ti]
                    a = ti % NSET
                    bt, btB, vt, vtB = cur[ti]
                    bl, hf = c // 2, c % 2
                    p0 = hf * 64
                    cols = slice(c * 64, c * 64 + 64)
                    obk = 6
                    for h in range(8):
                        mm(5, pb[5][p0:p0 + 64, h * 64:(h + 1) * 64], kd[a][h][:, cols], qd[a][h][:, cols], True, True,
                           [prepB[a][h]], h == 7, tp=(0, p0))
                    am, amB = amA_r.next()
                    tt("dve", am[p0:p0 + 64, :, :], pb[5][p0:p0 + 64, :].rearrange("p (h t) -> p h t", t=64),
                       tri[p0:p0 + 64, :].unsqueeze(1).to_broadcast([64, 8, 64]), ALU.mult, [triB], [amB, PW(5)])
                    for h in range(8):
                        PB = prepB[a][h]
                        oh0 = (h // 4) * 64
                        ocol = (h % 4) * 128
                        mm(obk, pb[obk][oh0:oh0 + 64, ocol:ocol + 128], am[p0:p0 + 64, h, :], vt[p0:p0 + 64, bl, h * 128:(h + 1) * 128],
                           True, False, [amB, vtB], False, tp=(p0, oh0))
                        mm(obk, pb[obk][oh0:oh0 + 64, ocol:ocol + 128], qdec[a][h][:, cols], Sbf[:, h, :],
                           False, True, [PB, SbfG[h // 4]], True, tp=(0, oh0))
                        pub = pu_banks[pui[0] % 2]
                        pui[0] += 1
                        mm(pub, pb[pub][:, 0:128], kdT[a][h][p0:p0 + 64, bl, :], vt[p0:p0 + 64, bl, h * 128:(h + 1) * 128],
                           True, True, [PB, vtB], True)
                        stt("dve", S32[:, h, :], S32[:, h, :], dl[a][h][:, c:c + 1], pb[pub][:, 0:128], ALU.mult, ALU.add,
                            [PB, S32B[h]], [(S32B[h], "p"), PW(pub)])
                        if h % 4 == 3:
                            g = h // 4
                            cp("act", Sbf[:, g * 4:(g + 1) * 4, :], S32[:, g * 4:(g + 1) * 4, :],
                               [S32B[hh] for hh in range(g * 4, g * 4 + 4)], [SbfG[g]])
                    osb, osB = os_r.next()
                    cp("act", osb[:, :], pb[obk][:, :], [], [osB, PW(obk)])
                    r_ = t0 + c * 64
                    S.dma(ODST[s, r_:r_ + 64, 0:512], osb[0:64, :], reads=[osB], q="act")
                    S.dma(ODST[s, r_:r_ + 64, 512:1024], osb[64:128, :], reads=[osB], q="act")

                for s in range(NS):
                    for h in range(8):
                        memset("pool", S32[:, h, :], 0.0, [S32B[h]])
                    for g in range(2):
                        memset("pool", Sbf[:, g * 4:(g + 1) * 4, :], 0.0, [SbfG[g]])
                    cur.clear()
                    pre.clear()
                    qcur.clear()
                    INTERLEAVE = os.environ.get("MK_HG_IL", "0") == "1"
                    stageA_common(s, 0)
                    for h in range(8):
                        stageA_head(s, 0, h)
                    flushA2()
                    for ti in range(len(tiles)):
                        t0, w = tiles[ti]
                        nch = w // 64
                        chunks = list(range(nch))
                        if rev:
                            chunks = chunks[::-1]
                        nxt = ti + 1 < len(tiles)
                        pending = list(range(8)) if nxt else []
                        if nxt:
                            loads(s, ti + 1)
                        for ci, c in enumerate(chunks):
                            stageB_chunk(s, ti, c)
                            if nxt and INTERLEAVE:
                                if ci == 0:
                                    stageA_common(s, ti + 1)
                                target = -(-8 * (ci + 1) // nch)
                                while 8 - len(pending) < target and pending:
                                    stageA_head(s, ti + 1, pending.pop(0))
                        if nxt and not INTERLEAVE:
                            stageA_common(s, ti + 1)
                        while pending:
                            stageA_head(s, ti + 1, pending.pop(0))
                        flushA2()
                S.barrier()

        def phase_hgrn_out():
            with ExitStack() as ph:
                (wg, wgB), (wo, woB) = load_ws(ph, [("hwg", hw_in_d[:, 4 * D:5 * D], D, D), ("hwo", hw_o_d, D, D)])
                gn, gnB = load_bcast(ph, "hon", hon_d)
                cx = LNCtx(ph, 1, 0, nr=2, nxs=2, nybf=2, stq=os.environ.get("MK_STQ7", "act"))
                bt_r = Ring(nc, ph, "gbt", [128, 8, 512], BF16, 2)
                btc = [None]
                of_r = Ring(nc, ph, "ofw", [128, D], F32, 3)
                ob_r = Ring(nc, ph, "obw", [128, D], F32, 3)
                sq_r = Ring(nc, ph, "osq", [128, D], F32, 2)
                ss_r = Ring(nc, ph, "oss", [128, 8], F32, 3)
                eg_r = Ring(nc, ph, "eg", [128, D], F32, 3)
                yb_r = Ring(nc, ph, "yb", [128, D], BF16, 3)
                yT_r = Ring(nc, ph, "yT", [128, 8, 128], BF16, 2)
                rs_r = Ring(nc, ph, "res", [128, D], F32, 3)
                wi_ = [0]
                pend = [None]
                for s in range(NS):
                    BTv = fm(B_T[s])
                    CTv = fm(C_T[s])
                    zero_cols(CTv[:, :, 0:1])
                    zero_cols(CTv[:, :, L + 1:L + 2])
                    tl = [(r0, min(128, L - r0)) for r0 in range(0, L, 128)]
                    stt_ = {}
                    stt1_ = {}

                    def part1(i):
                        r0, n = tl[i]
                        pc = r0 + 48
                        if r0 % 512 == 0:
                            gw = min(512, L - r0)
                            btc[0] = bt_r.next()
                            S.dma(btc[0][0][:, :, :gw], BTv[:, :, pc:pc + gw], writes=[btc[0][1]])
                        bt_full, btB = btc[0]
                        bt = bt_full[:, :, r0 % 512:r0 % 512 + n]
                        of, ofB = of_r.next()
                        ob, obB = ob_r.next()
                        S.dma(of[:n, :], OFW[s, pc:pc + n, :], writes=[ofB])
                        S.dma(ob[:n, :], OBW[s, pc:pc + n, :], writes=[obB])
                        rs, rsB = rs_r.next()
                        S.dma(rs[:n, :], B_tok[s, r0:r0 + n, :], writes=[rsB])
                        eg, egB = eg_r.next()
                        for hf in range(2):
                            for k in range(8):
                                mm(hf, pb[hf][:n, :], bt[:, k, :], wg[:, k, hf * 512:(hf + 1) * 512], k == 0, k == 7, [btB, wgB], k == 7)
                            act(eg[:n, hf * 512:(hf + 1) * 512], pb[hf][:n, :], AF.Exp, [], [(egB, "p"), PW(hf)], scale=-1.0)
                        act(eg[:n, :], eg[:n, :], AF.Ln, [egB], [(egB, "p")], bias=1.0)
                        act(eg[:n, :], eg[:n, :], AF.Exp, [egB], [(egB, "p")], scale=-1.0)
                        for hf in range(2):
                            tt("dve", eg[:n, hf * 512:(hf + 1) * 512], eg[:n, hf * 512:(hf + 1) * 512], pb[hf][:n, :], ALU.mult,
                               [egB], [(egB, "p"), PW(hf)])
                        stt1_[i] = (n, of, ofB, ob, obB, eg, egB, rs, rsB)

                    def part1b(i):
                        n, of, ofB, ob, obB, eg, egB, rs, rsB = stt1_.pop(i)
                        tt("dve", of[:n, :], of[:n, :], ob[:n, :], ALU.add, [ofB, obB], [(ofB, "p")])
                        sq, sqB = sq_r.next()
                        ss, ssB = ss_r.next()
                        tt("dve", sq[:n, :], of[:n, :], of[:n, :], ALU.mult, [ofB], [sqB])
                        S.op("dve", (lambda o, i_: (lambda e: e.tensor_reduce(out=o, in_=i_, axis=AX.X, op=ALU.add)))(
                            ss[:n, :], sq[:n, :].rearrange("p (h d) -> p h d", d=128)), reads=[sqB], writes=[ssB])
                        ts("dve", ss[:n, :], ss[:n, :], 1.0 / 128, EPS, ALU.mult, ALU.add, [ssB], [(ssB, "p")])
                        rsqrt_inplace(ss[:n, :], ssB)
                        o3 = of[:n, :].rearrange("p (h d) -> p h d", d=128)
                        tt("dve", o3, o3, ss[:n, :].unsqueeze(2).to_broadcast([n, 8, 128]), ALU.mult, [ofB, ssB], [(ofB, "p")])
                        tt("dve", of[:n, :], of[:n, :], gn[:n, :], ALU.mult, [ofB, gnB], [(ofB, "p")])
                        yb, ybB = yb_r.next()
                        tt("dve", yb[:n, :], of[:n, :], eg[:n, :], ALU.mult, [ofB, egB], [ybB])
                        stt_[i] = (yb, ybB, rs, rsB)

                    stt2_ = {}

                    def part2a(i):
                        r0, n = tl[i]
                        yb, ybB, rs, rsB = stt_.pop(i)
                        for c in range(8):
                            transpose(4, ptr[:, c * 128:c * 128 + n], yb[:n, c * 128:(c + 1) * 128], ident[:n, :n], [ybB, identB], c == 7)
                        yT, yTB = yT_r.next()
                        cp("act", yT[:, :, :n], ptr[:, :].rearrange("p (c j) -> p c j", j=128)[:, :, :n], [], [yTB, PW(4)])
                        stt2_[i] = (yT, yTB, rs, rsB)

                    def part2(i):
                        r0, n = tl[i]
                        yT, yTB, rs, rsB = stt2_.pop(i)
                        bks = ((2, 3), (5, 6))[wi_[0] % 2]
                        wi_[0] += 1
                        for hf in range(2):
                            for c in range(8):
                                mm(bks[hf], pb[bks[hf]][:n, :], yT[:, c, :n], wo[:, c, hf * 512:(hf + 1) * 512], c == 0, c == 7,
                                   [yTB, woB], c == 7)
                        if pend[0] is not None:
                            pend[0]()
                        pend[0] = ln_epilogue(cx, bks, n, rs, rsB, C_tok[s, r0:r0 + n, :], (CTv, 1 + r0))

                    part1(0)
                    part1b(0)
                    if len(tl) > 1:
                        part1(1)
                        part1b(1)
                    part2a(0)
                    for i in range(len(tl)):
                        if i + 2 < len(tl):
                            part1(i + 2)
                        part2(i)
                        if i + 1 < len(tl):
                            part2a(i + 1)
                        if i + 2 < len(tl):
                            part1b(i + 2)
                if pend[0] is not None:
                    pend[0]()
                cx.xs.flush()
                S.barrier()

        def on(p):
            return phases is None or p in phases
        if on(0):
            phase_T0()
        if on(1):
            phase_mla_proj()
        if on(2):
            phase_attn()
        if on(3):
            phase_mix_out("mwo", w_o_d, OT, 0, load_h0_rows, 0, A_tok, A_T, 1)
        if on(4):
            phase_ffn(0, A_T, A_tok, B_tok, B_T, 48, False)
        if on(5):
            phase_hgrn(0, OFW)
        if on(6):
            phase_hgrn(1, OBW)
        if on(7):
            phase_hgrn_out()
        if on(8):
            phase_ffn(1, C_T, C_tok, None, None, 0, True)
        S.barrier()
        print("instructions:", S.n_inst, {k: len(v) for k, v in S.streams.items()}, flush=True)
        with nc.Block() as block:
            S.emit(block)
    return nc


def rope_tables_np(L):
    inv = (1.0 / (10000.0 ** (np.arange(0, DR, 2, dtype=np.float32) / np.float32(DR)))).astype(np.float32)
    ang = (np.arange(L, dtype=np.float32)[:, None] * inv[None, :]).astype(np.float32)
    c = np.cos(ang).astype(np.float32).T
    s = np.sin(ang).astype(np.float32).T
    return (np.ascontiguousarray(np.concatenate([c, c], 0)), np.ascontiguousarray(np.concatenate([s, s], 0)))


def run(seqs, weights, n_cores=8):
    S_LEN = seqs[0].shape[0]
    nseq = len(seqs)
    NS = (nseq + n_cores - 1) // n_cores
    nc = build(S_LEN, NS)
    cosT, sinT = rope_tables_np(S_LEN + NMETA)
    f32 = lambda a: np.ascontiguousarray(np.asarray(a, dtype=np.float32))
    common = {
        "meta": f32(weights["meta_tokens"]),
        "mla_w_in": f32(weights["mla_w_in"][0]), "mla_q_norm": f32(weights["mla_q_norm"][0]),
        "mla_kv_norm": f32(weights["mla_kv_norm"][0]), "mla_w_uq": f32(weights["mla_w_uq"][0]),
        "mla_w_ukv": f32(weights["mla_w_ukv"][0]), "mla_w_o": f32(weights["mla_w_o"][0]),
        "hgrn_w_in": f32(weights["hgrn_w_in"][0]), "hgrn_lb": f32(weights["hgrn_lower_bound"]),
        "hgrn_o_norm": f32(weights["hgrn_o_norm"][0]), "hgrn_w_o": f32(weights["hgrn_w_o"][0]),
        "ffn_w_up": f32(weights["ffn_w_up"]), "ffn_conv_w": f32(weights["ffn_conv_w"]),
        "ffn_conv_b": f32(weights["ffn_conv_b"]), "ffn_w_down": f32(weights["ffn_w_down"]),
        "ln_gain": f32(weights["ln_gain"]), "ln_bias": f32(weights["ln_bias"]),
        "rope_cos": cosT, "rope_sin": sinT,
    }
    assign = []
    in_maps = []
    for c in range(n_cores):
        ids = [c + n_cores * j for j in range(NS)]
        ids = [i if i < nseq else ids[0] % nseq for i in ids]
        assign.append(ids)
        m = dict(common)
        m["x"] = np.ascontiguousarray(np.stack([seqs[i] for i in ids], 0))
        in_maps.append(m)
    res = run_bass_kernel_spmd(nc, in_maps, core_ids=list(range(n_cores)))
    outs = [None] * nseq
    for c in range(n_cores):
        yc = np.asarray(res.results[c]["y"])
        for j in range(NS):
            i = c + n_cores * j
            if i < nseq:
                outs[i] = yc[j]
    return outs


def kernel(x_prompt, x_sample, meta_tokens, mla_w_in, mla_q_norm, mla_kv_norm, mla_w_uq, mla_w_ukv, mla_w_o,
           hgrn_w_in, hgrn_lower_bound, hgrn_o_norm, hgrn_w_o,
           ffn_w_up, ffn_conv_w, ffn_conv_b, ffn_w_down, ln_gain, ln_bias):
    x_prompt = np.asarray(x_prompt, dtype=np.float32)
    x_sample = np.asarray(x_sample, dtype=np.float32)
    weights = dict(meta_tokens=meta_tokens, mla_w_in=mla_w_in, mla_q_norm=mla_q_norm, mla_kv_norm=mla_kv_norm,
                   mla_w_uq=mla_w_uq, mla_w_ukv=mla_w_ukv, mla_w_o=mla_w_o, hgrn_w_in=hgrn_w_in,
                   hgrn_lower_bound=hgrn_lower_bound, hgrn_o_norm=hgrn_o_norm, hgrn_w_o=hgrn_w_o,
                   ffn_w_up=ffn_w_up, ffn_conv_w=ffn_conv_w, ffn_conv_b=ffn_conv_b, ffn_w_down=ffn_w_down,
                   ln_gain=ln_gain, ln_bias=ln_bias)
    weights = {k: np.asarray(v, dtype=np.float32) for k, v in weights.items()}
    seqs = [x_prompt[i] for i in range(x_prompt.shape[0])] + [x_sample[i] for i in range(x_sample.shape[0])]
    outs = run(seqs, weights)
    nb = x_prompt.shape[0]
    y_prompt = np.stack(outs[:nb], 0).astype(np.float32)
    y_sample = np.stack(outs[nb:], 0).astype(np.float32)
    return (y_prompt, y_sample)
```

```python
import math
import os
from contextlib import ExitStack
import numpy as np
import concourse.bass as bass
import concourse.mybir as mybir
from concourse.bass_utils import run_bass_kernel_spmd

F32 = mybir.dt.float32
BF16 = mybir.dt.bfloat16
AF = mybir.ActivationFunctionType
ALU = mybir.AluOpType
AX = mybir.AxisListType

D = 1024
NMETA = 16
QL, KVL, DR, DN, DV, NH = 768, 256, 32, 64, 64, 16
MLA_SCALE = (DN + DR) ** -0.5
HG_H = 8
DFF = 2816
DEPTH = 2
ALPHA = (2 * DEPTH) ** 0.25
EPS = 1e-6
COMPUTE = ("pe", "act", "dve", "pool")


class Buf:
    __slots__ = ("name", "w", "r")

    def __init__(self, name=""):
        self.name = name
        self.w = {}
        self.r = {}


class Sched:
    K = 8
    ND = 24

    def __init__(self, nc, same_engine_sync=True):
        self.nc = nc
        self.same = same_engine_sync
        self.streams = {e: [] for e in COMPUTE + ("sp",)}
        self.cnt = {e: 0 for e in COMPUTE}
        self.dcnt = {"dma": 0, "dmaA": 0}
        self.known = {s: {e: -1 for e in COMPUTE} for s in self.streams}
        self.dma_low = {(s, k): 0 for s in self.streams for k in ("dma", "dmaA")}
        self.dma_set = {(s, k): set() for s in self.streams for k in ("dma", "dmaA")}
        self.sems = {}
        self.n_inst = 0

    def alloc(self, stack):
        for e in COMPUTE:
            self.sems[e] = [stack.enter_context(self.nc.semaphore(f"s_{e}{i}")) for i in range(self.K)]
        self.sems["dma"] = [stack.enter_context(self.nc.semaphore(f"s_dma{i}")) for i in range(self.ND)]
        self.sems["dmaA"] = [stack.enter_context(self.nc.semaphore(f"s_dmaA{i}")) for i in range(self.ND)]

    def _knows(self, s, ev):
        k, idx = ev
        if k.startswith("dma"):
            return idx < self.dma_low[(s, k)] or idx in self.dma_set[(s, k)]
        return self.known[s][k] >= idx

    def _learn(self, s, ev):
        k, idx = ev
        if k.startswith("dma"):
            self.dma_set[(s, k)].add(idx)
            self.dma_low[(s, k)] = max(self.dma_low[(s, k)], idx - self.ND + 1)
        else:
            self.known[s][k] = max(self.known[s][k], idx)

    def _wait(self, s, ev):
        if self._knows(s, ev):
            return
        k, idx = ev
        if k.startswith("dma"):
            sem = self.sems[k][idx % self.ND]
            val = 16 * (idx // self.ND + 1)
        else:
            sem = self.sems[k][idx % self.K]
            val = idx // self.K + 1
        self.streams[s].append(("wait", sem, val))
        self._learn(s, ev)

    @staticmethod
    def _events(d):
        for k, v in d.items():
            if isinstance(k, tuple):
                yield k
            else:
                yield (k, v)

    def _add(self, d, ev):
        k, idx = ev
        if k.startswith("dma"):
            for kk in [kk for kk in d if isinstance(kk, tuple) and kk[0] == k and kk[1] <= idx - self.ND]:
                del d[kk]
            d[ev] = True
        elif d.get(k, -1) < idx:
            d[k] = idx

    def _deps(self, s, eng, reads, writes):
        deps = set()
        for b in reads:
            deps.update(self._events(b.w))
        for b in writes:
            if isinstance(b, tuple):
                b = b[0]
            deps.update(self._events(b.w))
            deps.update(self._events(b.r))
        for ev in sorted(deps, key=lambda e: (str(e[0]), e[1])):
            if ev[0] == eng and (eng == "pe" or not self.same):
                continue
            self._wait(s, ev)

    def _post(self, ev, reads, writes):
        for b in reads:
            self._add(b.r, ev)
        for b in writes:
            if isinstance(b, tuple):
                self._add(b[0].w, ev)
            else:
                b.w = {}
                b.r = {}
                self._add(b.w, ev)

    def op(self, eng, fn, reads=(), writes=(), track=True):
        self._deps(eng, eng, reads, writes)
        idx = self.cnt[eng]
        if track:
            self.cnt[eng] += 1
            self.streams[eng].append(("inst", fn, self.sems[eng][idx % self.K], 1))
        else:
            self.streams[eng].append(("inst", fn, None, 0))
        self._post((eng, idx), reads, writes)
        self.n_inst += 1

    def dma(self, out, in_, reads=(), writes=(), q="sp", **kw):
        s = q
        kind = "dma" if q == "sp" else "dmaA"
        self._deps(s, kind, reads, writes)
        idx = self.dcnt[kind]
        if idx >= self.ND:
            self._wait(s, (kind, idx - self.ND))
        self.dcnt[kind] += 1
        sem = self.sems[kind][idx % self.ND]
        self.streams[s].append(("inst", lambda e: e.dma_start(out=out, in_=in_, **kw), sem, 16))
        self._post((kind, idx), reads, writes)
        self.n_inst += 1

    def barrier(self):
        for s in self.streams:
            for kind in ("dma", "dmaA"):
                for idx in range(max(0, self.dcnt[kind] - self.ND), self.dcnt[kind]):
                    self._wait(s, (kind, idx))
            for e in COMPUTE:
                if self.cnt[e] > 0:
                    self._wait(s, (e, self.cnt[e] - 1))

    def emit(self, block):
        def replay(name):
            def run(eng):
                for ent in self.streams[name]:
                    if ent[0] == "wait":
                        eng.wait_ge(ent[1], ent[2])
                    else:
                        ins = ent[1](eng)
                        if ent[2] is not None:
                            ins.then_inc(ent[2], ent[3])
            return run

        block.tensor(replay("pe"))
        block.scalar(replay("act"))
        block.vector(replay("dve"))
        block.gpsimd(replay("pool"))
        block.sync(replay("sp"))


_UID = [0]


def _uniq(name):
    _UID[0] += 1
    return f"{name}_u{_UID[0]}"


class Ring:
    def __init__(self, nc, st, name, shape, dt, n):
        self.t = [st.enter_context(nc.sbuf_tensor(_uniq(f"{name}{i}"), shape, dt)) for i in range(n)]
        self.b = [Buf(f"{name}{i}") for i in range(n)]
        self.i = 0

    def next(self):
        j = self.i % len(self.t)
        self.i += 1
        return self.t[j], self.b[j]


def build(S_LEN, NS, phases=None, same=True):
    L = S_LEN + NMETA
    LP = L + 112
    NKT = (L + 127) // 128
    nc = bass.Bass("TRN2", target_bir_lowering=False)

    def din(name, shape, dt=F32):
        return nc.dram_tensor(name, list(shape), dt, kind="ExternalInput").ap()

    def dscr(name, shape, dt):
        return nc.dram_tensor(name, list(shape), dt, kind="Internal").ap()

    x = din("x", [NS, S_LEN, D])
    meta = din("meta", [NMETA, D])
    w_in_d = din("mla_w_in", [D, QL + KVL + DR])
    qn_d = din("mla_q_norm", [QL])
    kvn_d = din("mla_kv_norm", [KVL])
    w_uq_d = din("mla_w_uq", [QL, NH * 96])
    w_ukv_d = din("mla_w_ukv", [KVL, NH * 128])
    w_o_d = din("mla_w_o", [D, D])
    hw_in_d = din("hgrn_w_in", [D, 5 * D])
    hlb_d = din("hgrn_lb", [2, 2, D])
    hon_d = din("hgrn_o_norm", [D])
    hw_o_d = din("hgrn_w_o", [D, D])
    f_up_d = din("ffn_w_up", [2, D, 2 * DFF])
    f_cw_d = din("ffn_conv_w", [2, 3, 2 * DFF])
    f_cb_d = din("ffn_conv_b", [2, 2 * DFF])
    f_dn_d = din("ffn_w_down", [2, DFF, D])
    lng_d = din("ln_gain", [2, 2, D])
    lnb_d = din("ln_bias", [2, 2, D])
    cos_d = din("rope_cos", [DR, L])
    sin_d = din("rope_sin", [DR, L])
    y = nc.dram_tensor("y", [NS, S_LEN, D], F32, kind="ExternalOutput").ap()

    XT = dscr("XT", [NS, D, L], BF16)
    CQ = dscr("CQ", [NS, QL, L], BF16)
    CKV = dscr("CKV", [NS, KVL, L], BF16)
    KR = dscr("KR", [NS, DR, L], BF16)
    OT = dscr("OT", [NS, D, L], BF16)
    A_tok = dscr("A_tok", [NS, L, D], F32)
    A_T = dscr("A_T", [NS, D, L + 2], BF16)
    B_tok = dscr("B_tok", [NS, L, D], F32)
    B_T = dscr("B_T", [NS, D, LP], BF16)
    VT_d = dscr("VT_d", [NS, LP, D], BF16)
    Q_d = dscr("Q_d", [NS, D, LP], F32)
    OFW = dscr("OFW", [NS, LP, D], F32)
    OBW = dscr("OBW", [NS, LP, D], F32)
    C_tok = dscr("C_tok", [NS, L, D], F32)
    C_T = dscr("C_T", [NS, D, L + 2], BF16)

    def fm(ap2d):
        return ap2d.rearrange("(c p) t -> p c t", p=128)

    with ExitStack() as st:
        S = Sched(nc, same_engine_sync=(same and os.environ.get('MK_SAME', '1') == '1'))
        S.alloc(st)

        def sbt(stack, name, shape, dt=F32):
            return stack.enter_context(nc.sbuf_tensor(_uniq(name), list(shape), dt))

        pb, pbB = [], []
        for i in range(8):
            if i == 4:
                pb.append(st.enter_context(nc.psum_tensor("ptr", [128, 1024], BF16)))
            else:
                pb.append(st.enter_context(nc.psum_tensor(f"pb{i}", [128, 512], F32)))
            pbB.append(Buf(f"pb{i}"))
        ptr = pb[4]

        def PW(i):
            return (pbB[i], "p")

        def mm(bank, out, lhsT, rhs, start, stop, reads, track, tp=None):
            kw = {"tile_position": tp} if tp is not None else {}
            S.op("pe", lambda e: e.matmul(out, lhsT=lhsT, rhs=rhs, start=start, stop=stop, **kw),
                 reads=reads, writes=[PW(bank)], track=track)

        def transpose(bank, out, in_, ident_ap, reads, track):
            S.op("pe", lambda e: e.transpose(out, in_, ident_ap), reads=reads, writes=[PW(bank)], track=track)

        def act(out, in_, func, reads, writes, scale=None, bias=None, accum_out=None):
            kw = {}
            if scale is not None:
                kw["scale"] = scale
            if bias is not None:
                kw["bias"] = bias
            if accum_out is not None:
                kw["accum_out"] = accum_out
            S.op("act", lambda e: e.activation(out=out, in_=in_, func=func, **kw), reads=reads, writes=writes)

        def cp(eng, out, in_, reads, writes):
            if eng == "act":
                S.op("act", lambda e: e.copy(out=out, in_=in_), reads=reads, writes=writes)
            else:
                S.op(eng, lambda e: e.tensor_copy(out=out, in_=in_), reads=reads, writes=writes)

        def tt(eng, out, in0, in1, op, reads, writes):
            S.op(eng, lambda e: e.tensor_tensor(out=out, in0=in0, in1=in1, op=op), reads=reads, writes=writes)

        def ts(eng, out, in0, s1, s2, op0, op1, reads, writes):
            if s2 is None:
                if op0 == ALU.pow:
                    s1, s2, op0, op1 = 1.0, s1, ALU.mult, ALU.pow
                else:
                    s2, op1 = 0.0, ALU.add
            if True:
                S.op(eng, lambda e: e.tensor_scalar(out=out, in0=in0, scalar1=s1, scalar2=s2, op0=op0, op1=op1),
                     reads=reads, writes=writes)

        def stt(eng, out, in0, scalar, in1, op0, op1, reads, writes):
            S.op(eng, lambda e: e.scalar_tensor_tensor(out=out, in0=in0, scalar=scalar, in1=in1, op0=op0, op1=op1),
                 reads=reads, writes=writes)

        def rsqrt_inplace(ap, B):
            act(ap, ap, AF.Ln, [B], [(B, "p")])
            act(ap, ap, AF.Exp, [B], [(B, "p")], scale=-0.5)

        def memset(eng, ap, val, writes, reads=()):
            S.op(eng, lambda e: e.memset(ap, val), reads=reads, writes=writes)

        ident = sbt(st, "ident", [128, 128], BF16)
        identB = Buf("ident")
        ones = sbt(st, "ones", [128, 128], BF16)
        onesB = Buf("ones")
        zer = sbt(st, "zer", [128, 128], BF16)
        zerB = Buf("zer")
        memset("pool", ident[:], 1.0, [identB])
        S.op("pool", lambda e: e.affine_select(out=ident[:], in_=ident[:], pattern=[[-1, 128]],
                                               compare_op=ALU.is_equal, fill=0.0, base=0, channel_multiplier=1),
             reads=[identB], writes=[identB])
        memset("pool", ones[:], 1.0, [onesB])
        memset("pool", zer[:], 0.0, [zerB])
        stg = Ring(nc, st, "stg", [128, 512], F32, 2)
        cast_i = [0]

        def load_w(stack, name, src, K, N):
            kc = K // 128
            wt = sbt(stack, name, [128, kc, N], BF16)
            wb = Buf(name)
            first = True
            for k in range(kc):
                for c0 in range(0, N, 512):
                    cw = min(512, N - c0)
                    t, tb = stg.next()
                    S.dma(t[:, :cw], src[k * 128:(k + 1) * 128, c0:c0 + cw], writes=[tb])
                    eng = ("pool", "dve")[cast_i[0] % 2]
                    cast_i[0] += 1
                    cp(eng, wt[:, k, c0:c0 + cw], t[:, :cw], [tb], [wb] if first else [(wb, "p")])
                    first = False
            return wt, wb

        def load_ws(stack, specs):
            outs = []
            for (name, src, K, N) in specs:
                outs.append((sbt(stack, name, [128, K // 128, N], BF16), Buf(name)))
            with ExitStack() as tmp:
                big = Ring(nc, tmp, "stgb", [128, 1024], F32, 6)
                for (name, src, K, N), (wt, wb) in zip(specs, outs):
                    first = True
                    for k in range(K // 128):
                        for c0 in range(0, N, 1024):
                            cw = min(1024, N - c0)
                            t, tb = big.next()
                            S.dma(t[:, :cw], src[k * 128:(k + 1) * 128, c0:c0 + cw], writes=[tb])
                            eng = ("pool", "dve", "dve")[cast_i[0] % 3]
                            cast_i[0] += 1
                            cp(eng, wt[:, k, c0:c0 + cw], t[:, :cw], [tb], [wb] if first else [(wb, "p")])
                            first = False
                S.barrier()
            return outs

        def load_cols(stack, name, src1d, ncol):
            t = sbt(stack, name, [128, ncol], F32)
            b = Buf(name)
            v = src1d.rearrange("(c p) -> p c", p=128)
            first = True
            for c0 in range(0, ncol, 11):
                c1 = min(ncol, c0 + 11)
                S.dma(t[:, c0:c1], v[:, c0:c1], writes=[b] if first else [(b, "p")], allow_slow_non_contiguous=True)
                first = False
            return t, b

        def load_bcast(stack, name, src1d):
            t = sbt(stack, name, [128, D], F32)
            b = Buf(name)
            S.dma(t[:], src1d.partition_broadcast(128), writes=[b])
            return t, b

        def load_h0_rows(t, tb, s, r0, n):
            if r0 == 0:
                S.dma(t[0:NMETA, :], meta[:, :], writes=[tb])
                S.dma(t[NMETA:n, :], x[s, 0:n - NMETA, :], writes=[(tb, "p")])
            else:
                S.dma(t[0:n, :], x[s, r0 - NMETA:r0 - NMETA + n, :], writes=[tb])

        class FMStore:
            def __init__(self, stack, name, nbuf=2, width=512, q="act"):
                self.q = q
                self.ring = Ring(nc, stack, name, [128, 8, width], BF16, nbuf)
                self.width = width
                self.cur = None

            def add(self, src, srcB, n, dst, col):
                c = self.cur
                if c is None or c["dst"] is not dst or c["col0"] + c["filled"] != col or c["filled"] + n > self.width:
                    self.flush()
                    t, b = self.ring.next()
                    c = self.cur = dict(t=t, b=b, dst=dst, col0=col, filled=0)
                for k in range(8):
                    transpose(4, ptr[:, k * 128:k * 128 + n], src[:n, k * 128:(k + 1) * 128], ident[:n, :n],
                              [srcB, identB], track=(k == 7))
                f = c["filled"]
                cp("act", c["t"][:, :, f:f + n], ptr[:, :].rearrange("p (c j) -> p c j", j=128)[:, :, :n], [],
                   [c["b"] if f == 0 else (c["b"], "p"), PW(4)])
                c["filled"] = f + n
                if c["filled"] >= self.width:
                    self.flush()

            def flush(self):
                c = self.cur
                if c is not None and c["filled"] > 0:
                    S.dma(c["dst"][:, :, c["col0"]:c["col0"] + c["filled"]], c["t"][:, :, :c["filled"]], reads=[c["b"]], q=self.q)
                self.cur = None

        def to_featmajor(fms, src, srcB, n, dst, col):
            fms.add(src, srcB, n, dst, col)

        class LNCtx:
            def __init__(self, stack, layer, idx, nr=2, nxs=2, nybf=1, nyo=1, xsw=512, stq="act"):
                self.stq = stq
                self.g, self.gB = load_bcast(stack, "lng", lng_d[layer, idx, :])
                self.b, self.bB = load_bcast(stack, "lnb", lnb_d[layer, idx, :])
                self.r = Ring(nc, stack, "ln_r", [128, D], F32, nr)
                self.yo = Ring(nc, stack, "ln_y", [128, D], F32, nyo) if nyo > 0 else None
                self.ybf = Ring(nc, stack, "ln_ybf", [128, D], BF16, nybf)
                self.xs = FMStore(stack, "ln_xs", nbuf=nxs, width=xsw, q=stq)
                self.st = Ring(nc, stack, "ln_st", [128, 2, 6], F32, 2)
                self.mv = Ring(nc, stack, "ln_mv", [128, 4], F32, 2)

        def ln_epilogue(cx, banks, n, res, resB, out_rows, outT, row_lo=0):
            r, rB = cx.r.next()
            for h in range(2):
                stt("dve", r[:n, h * 512:(h + 1) * 512], res[:n, h * 512:(h + 1) * 512], ALPHA,
                    pb[banks[h]][:n, :], ALU.mult, ALU.add, [resB], [(rB, "p"), PW(banks[h])])
            sT, sB = cx.st.next()
            mv, mvB = cx.mv.next()
            for h in range(2):
                S.op("dve", (lambda o, i: (lambda e: e.bn_stats(out=o, in_=i)))(sT[:n, h, :], r[:n, h * 512:(h + 1) * 512]),
                     reads=[rB], writes=[(sB, "p")])
            S.op("dve", lambda e: e.bn_aggr(out=mv[:n, 0:2], in_=sT[:n, :, :].rearrange("p a b -> p (a b)")),
                 reads=[sB], writes=[(mvB, "p")])
            ts("dve", mv[:n, 2:3], mv[:n, 1:2], EPS, None, ALU.add, None, [mvB], [(mvB, "p")])
            rsqrt_inplace(mv[:n, 2:3], mvB)
            stt("dve", mv[:n, 3:4], mv[:n, 0:1], -1.0, mv[:n, 2:3], ALU.mult, ALU.mult, [mvB], [(mvB, "p")])
            if cx.yo is not None:
                yo, yB = cx.yo.next()
                act(yo[:n, :], r[:n, :], AF.Identity, [rB, mvB], [yB], scale=mv[:n, 2:3], bias=mv[:n, 3:4])
            else:
                yo, yB = r, rB
                act(yo[:n, :], r[:n, :], AF.Identity, [rB, mvB], [(rB, "p")], scale=mv[:n, 2:3], bias=mv[:n, 3:4])
            tt("dve", yo[:n, :], yo[:n, :], cx.g[:n, :], ALU.mult, [cx.gB, yB], [(yB, "p")])
            tt("dve", yo[:n, :], yo[:n, :], cx.b[:n, :], ALU.add, [cx.bB, yB], [(yB, "p")])
            yb = ybB = None
            if outT is not None:
                yb, ybB = cx.ybf.next()
                cp("act", yb[:n, :], yo[:n, :], [yB], [ybB])

            def fin():
                if out_rows is not None:
                    S.dma(out_rows, yo[row_lo:n, :], reads=[yB], q=cx.stq)
                if outT is not None:
                    to_featmajor(cx.xs, yb, ybB, n, outT[0], outT[1])
            return fin

        def zero_cols(dst3):
            w = dst3.shape[2]
            for c in range(8):
                S.dma(dst3[:, c, :], zer[:, 0:w], reads=[zerB], allow_slow_non_contiguous=True)

        def phase_T0():
            with ExitStack() as ph:
                xr = Ring(nc, ph, "t0x", [128, D], F32, 2)
                xb = Ring(nc, ph, "t0b", [128, D], BF16, 2)
                xs = FMStore(ph, "t0s", nbuf=2)
                for s in range(NS):
                    XTv = fm(XT[s])
                    for r0 in range(0, L, 128):
                        n = min(128, L - r0)
                        t, tb = xr.next()
                        load_h0_rows(t, tb, s, r0, n)
                        u, ub = xb.next()
                        cp("dve", u[:n, :], t[:n, :], [tb], [ub])
                        to_featmajor(xs, u, ub, n, XTv, r0)
                xs.flush()
                S.barrier()

        def phase_mla_proj():
            with ExitStack() as ph:
                ((w_in, w_inB),) = load_ws(ph, [("w_in", w_in_d, D, QL + KVL + DR)])
                wr = sbt(ph, "w_inr", [128, 8, 96], BF16)
                wrB = Buf("w_inr")
                memset("pool", wr[:], 0.0, [wrB])
                act(wr[:, :, 64:80], w_in[:, :, 1040:1056], AF.Identity, [w_inB], [(wrB, "p")], scale=-1.0)
                cp("pool", wr[:, :, 80:96], w_in[:, :, 1024:1040], [w_inB], [(wrB, "p")])
                gq, gqB = load_cols(ph, "gq", qn_d, 6)
                gkv, gkvB = load_cols(ph, "gkv", kvn_d, 2)
                cosT = sbt(ph, "cosT", [128, L], F32)
                sinT = sbt(ph, "sinT", [128, L], F32)
                csB = Buf("cs")
                S.dma(cosT[64:96, :], cos_d[:, :], writes=[csB])
                S.dma(sinT[64:96, :], sin_d[:, :], writes=[(csB, "p")])
                xt_r = Ring(nc, ph, "xt", [128, 8, 512], BF16, 2)
                sq_r = Ring(nc, ph, "sq", [128, 512], BF16, 2)
                craw = sbt(ph, "craw", [128, 8, 512], F32)
                crawB = Buf("craw")
                rq_r = Ring(nc, ph, "rq", [128, 2, 512], F32, 2)
                cn_r = Ring(nc, ph, "cn", [128, 512], BF16, 3)
                t1_r = Ring(nc, ph, "rt1", [128, 512], F32, 2)
                t2_r = Ring(nc, ph, "rt2", [128, 512], F32, 2)
                kr_r = Ring(nc, ph, "krt", [128, 512], BF16, 2)
                for s in range(NS):
                    XTv = fm(XT[s])
                    CQv = fm(CQ[s])
                    CKVv = fm(CKV[s])
                    ptiles = [(t0, min(512, L - t0)) for t0 in range(0, L, 512)]
                    xts = {}

                    def load_xt(i):
                        t0_, w_ = ptiles[i]
                        xt_, xtB_ = xt_r.next()
                        S.dma(xt_[:, :, :w_], XTv[:, :, t0_:t0_ + w_], writes=[xtB_])
                        xts[i] = (xt_, xtB_)

                    load_xt(0)
                    for ti_, (t0, w) in enumerate(ptiles):
                        xt, xtB = xts.pop(ti_)
                        if ti_ + 1 < len(ptiles):
                            load_xt(ti_ + 1)
                        for m in range(8):
                            bk = m % 2
                            for k in range(8):
                                mm(bk, pb[bk][:, :w], w_in[:, k, m * 128:(m + 1) * 128], xt[:, k, :w],
                                   k == 0, k == 7, [w_inB, xtB], k == 7)
                            sq, sqB = sq_r.next()
                            act(sq[:, :w], pb[bk][:, :w], AF.Square, [], [sqB, PW(bk)])
                            cp("dve", craw[:, m, :w], pb[bk][:, :w], [], [(crawB, "p"), PW(bk)])
                            sb_ = 2 if m < 6 else 3
                            mm(sb_, pb[sb_][:, :w], ones[:, :], sq[:, :w], m in (0, 6), m in (5, 7), [onesB, sqB], True)
                        rq, rqB = rq_r.next()
                        ts("dve", rq[:, 0, :w], pb[2][:, :w], 1.0 / QL, EPS, ALU.mult, ALU.add, [], [(rqB, "p"), PW(2)])
                        rsqrt_inplace(rq[:, 0, :w], rqB)
                        ts("dve", rq[:, 1, :w], pb[3][:, :w], 1.0 / KVL, EPS, ALU.mult, ALU.add, [], [(rqB, "p"), PW(3)])
                        rsqrt_inplace(rq[:, 1, :w], rqB)
                        for m in range(8):
                            cn, cnB = cn_r.next()
                            g_ap = gq[:, m:m + 1] if m < 6 else gkv[:, m - 6:m - 5]
                            stt("dve", cn[:, :w], craw[:, m, :w], g_ap, rq[:, 0 if m < 6 else 1, :w],
                                ALU.mult, ALU.mult, [crawB, gqB, gkvB, rqB], [cnB])
                            dst = CQv[:, m, t0:t0 + w] if m < 6 else CKVv[:, m - 6, t0:t0 + w]
                            S.dma(dst, cn[:, :w], reads=[cnB])
                        for k in range(8):
                            mm(5, pb[5][0:96, :w], w_in[:, k, 960:1056], xt[:, k, :w], k == 0, k == 7, [w_inB, xtB], k == 7)
                        for k in range(8):
                            mm(6, pb[6][0:96, :w], wr[:, k, :], xt[:, k, :w], k == 0, k == 7, [wrB, xtB], k == 7)
                        t1, t1B = t1_r.next()
                        t2, t2B = t2_r.next()
                        kr, krB = kr_r.next()
                        tt("dve", t1[64:96, :w], pb[5][64:96, :w], cosT[64:96, t0:t0 + w], ALU.mult, [csB], [t1B, PW(5)])
                        tt("dve", t2[64:96, :w], pb[6][64:96, :w], sinT[64:96, t0:t0 + w], ALU.mult, [csB], [t2B, PW(6)])
                        tt("pool", kr[64:96, :w], t1[64:96, :w], t2[64:96, :w], ALU.add, [t1B, t2B], [krB])
                        S.dma(KR[s, :, t0:t0 + w], kr[64:96, :w], reads=[krB])
                S.barrier()

        def phase_attn():
            with ExitStack() as ph:
                (w_uq, w_uqB), (w_ukv, w_ukvB) = load_ws(ph, [("w_uq", w_uq_d, QL, NH * 96), ("w_ukv", w_ukv_d, KVL, NH * 128)])
                wqr = sbt(ph, "w_uqr", [128, 6, NH * 96], BF16)
                wqrB = Buf("wqr")
                memset("pool", wqr[:], 0.0, [wqrB])
                for k in range(6):
                    a4 = w_uq[:, k, :].rearrange("p (h d) -> p h d", d=96)
                    r4 = wqr[:, k, :].rearrange("p (h d) -> p h d", d=96)
                    act(r4[:, :, 64:80], a4[:, :, 80:96], AF.Identity, [w_uqB], [(wqrB, "p")], scale=-1.0)
                    cp("pool", r4[:, :, 80:96], a4[:, :, 64:80], [w_uqB], [(wqrB, "p")])
                cosT = sbt(ph, "cosT", [128, L], F32)
                sinT = sbt(ph, "sinT", [128, L], F32)
                csB = Buf("cs")
                S.dma(cosT[64:96, :], cos_d[:, :], writes=[csB])
                S.dma(sinT[64:96, :], sin_d[:, :], writes=[(csB, "p")])
                cq = sbt(ph, "cq", [128, 6, L], BF16)
                cqB = Buf("cq")
                ckv = sbt(ph, "ckv", [128, 2, L], BF16)
                ckvB = Buf("ckv")
                KT = [sbt(ph, f"KT{i}", [128, L], BF16) for i in range(2)]
                KTB = [Buf(f"KT{i}") for i in range(2)]
                VA = [sbt(ph, f"VA{i}", [128, NKT, 128], BF16) for i in range(2)]
                VAB = [Buf(f"VA{i}") for i in range(2)]
                for i in range(2):
                    memset("pool", VA[i][:, :, 64:128], 1.0, [VAB[i]])
                QT_r = Ring(nc, ph, "QT", [128, 512], BF16, 2)
                PT_r = Ring(nc, ph, "PT", [128, 512], BF16, 4)
                t1_r = Ring(nc, ph, "at1", [128, 512], F32, 2)
                t2_r = Ring(nc, ph, "at2", [128, 512], F32, 2)
                rd_r = Ring(nc, ph, "rden", [128, 512], F32, 2)
                on_r = Ring(nc, ph, "on", [128, 512], BF16, 2)
                st_banks = [0, 1, 2]
                ot_banks = [3, 5]
                sti = [0]
                oti = [0]
                items = [(s, h, t0) for s in range(NS) for h in range(NH) for t0 in range(0, L, 512)]
                qbuf = {}

                def load_seq(s):
                    S.dma(cq[:, :, :], fm(CQ[s])[:, :, :], writes=[cqB])
                    S.dma(ckv[:, :, :], fm(CKV[s])[:, :, :], writes=[ckvB])
                    for i in range(2):
                        S.dma(KT[i][64:96, :], KR[s, :, :], writes=[(KTB[i], "p")])

                def build_kv(s, h):
                    hb = h % 2
                    for t0 in range(0, L, 512):
                        w = min(512, L - t0)
                        for k in range(2):
                            mm(6, pb[6][0:64, :w], w_ukv[:, k, h * 128:h * 128 + 64], ckv[:, k, t0:t0 + w],
                               k == 0, k == 1, [w_ukvB, ckvB], k == 1)
                        cp("dve", KT[hb][0:64, t0:t0 + w], pb[6][0:64, :w], [], [(KTB[hb], "p"), PW(6)])
                    for g0 in range(0, NKT, 8):
                        g1 = min(NKT, g0 + 8)
                        for kt in range(g0, g1):
                            nk = min(128, L - kt * 128)
                            for k in range(2):
                                mm(7, pb[7][:nk, (kt - g0) * 64:(kt - g0 + 1) * 64], ckv[:, k, kt * 128:kt * 128 + nk],
                                   w_ukv[:, k, h * 128 + 64:h * 128 + 128], k == 0, k == 1, [w_ukvB, ckvB],
                                   (k == 1))
                        cp("dve", VA[hb][:, g0:g1, 0:64],
                           pb[7][:, 0:(g1 - g0) * 64].rearrange("p (j d) -> p j d", d=64), [], [(VAB[hb], "p"), PW(7)])

                def build_q(s, h, t0):
                    w = min(512, L - t0)
                    for k in range(6):
                        mm(6, pb[6][0:96, :w], w_uq[:, k, h * 96:(h + 1) * 96], cq[:, k, t0:t0 + w],
                           k == 0, k == 5, [w_uqB, cqB], k == 5)
                    for k in range(6):
                        mm(7, pb[7][0:96, :w], wqr[:, k, h * 96:(h + 1) * 96], cq[:, k, t0:t0 + w],
                           k == 0, k == 5, [wqrB, cqB], k == 5)
                    QT, QTB = QT_r.next()
                    t1, t1B = t1_r.next()
                    t2, t2B = t2_r.next()
                    cp("act", QT[0:64, :w], pb[6][0:64, :w], [], [(QTB, "p"), PW(6)])
                    tt("dve", t1[64:96, :w], pb[6][64:96, :w], cosT[64:96, t0:t0 + w], ALU.mult, [csB], [t1B, PW(6)])
                    tt("dve", t2[64:96, :w], pb[7][64:96, :w], sinT[64:96, t0:t0 + w], ALU.mult, [csB], [t2B, PW(7)])
                    tt("pool", QT[64:96, :w], t1[64:96, :w], t2[64:96, :w], ALU.add, [t1B, t2B], [(QTB, "p")])
                    qbuf[(s, h, t0)] = (QT, QTB)

                def prep(j):
                    s, h, t0 = items[j]
                    if j == 0 or items[j - 1][0] != s:
                        load_seq(s)
                    if j == 0 or items[j - 1][:2] != (s, h):
                        build_kv(s, h)
                    build_q(s, h, t0)

                prep(0)
                for j, (s, h, t0) in enumerate(items):
                    hb = h % 2
                    w = min(512, L - t0)
                    QT, QTB = qbuf.pop((s, h, t0))
                    ob = ot_banks[oti[0] % 2]
                    oti[0] += 1
                    has_next = j + 1 < len(items)
                    early = has_next and items[j + 1][0] == s
                    stb = {}

                    def qk(kt):
                        nk = min(128, L - kt * 128)
                        sbk = st_banks[sti[0] % 3]
                        sti[0] += 1
                        stb[kt] = sbk
                        mm(sbk, pb[sbk][:nk, :w], KT[hb][0:96, kt * 128:kt * 128 + nk], QT[0:96, :w],
                           True, True, [KTB[hb], QTB], True)

                    LA = 3
                    for i0 in range(min(LA, NKT)):
                        qk(i0)
                    for kt in range(NKT):
                        nk = min(128, L - kt * 128)
                        sbk = stb[kt]
                        PT, PTB = PT_r.next()
                        act(PT[:nk, :w], pb[sbk][:nk, :w], AF.Exp, [], [PTB, PW(sbk)], scale=MLA_SCALE)
                        mm(ob, pb[ob][:, :w], VA[hb][0:nk, kt, :], PT[:nk, :w], kt == 0, kt == NKT - 1,
                           [VAB[hb], PTB], kt == NKT - 1)
                        if kt + LA < NKT:
                            qk(kt + LA)
                        if early and kt == min(6, NKT - 1):
                            prep(j + 1)
                    rd, rdB = rd_r.next()
                    on, onB = on_r.next()
                    S.op("dve", (lambda o, i: (lambda e: e.reciprocal(out=o, in_=i)))(rd[64:128, :w], pb[ob][64:128, :w]),
                         reads=[], writes=[rdB, PW(ob)])
                    tt("dve", on[0:64, :w], pb[ob][0:64, :w], rd[64:128, :w], ALU.mult, [rdB], [onB, PW(ob)])
                    S.dma(OT[s, h * 64:(h + 1) * 64, t0:t0 + w], on[0:64, :w], reads=[onB])
                    if has_next and not early:
                        prep(j + 1)
                S.barrier()

        def phase_mix_out(name, w_d, src_T, src_off, res_fn, layer, dst_tok, dst_T, dst_off):
            with ExitStack() as ph:
                ((wo, woB),) = load_ws(ph, [(name, w_d, D, D)])
                cx = LNCtx(ph, layer, 0, nr=3, nxs=2, nybf=3, nyo=3)
                ot_r = Ring(nc, ph, "ot", [128, 8, 512], BF16, 2)
                rs_r = Ring(nc, ph, "res", [128, D], F32, 3)
                mi = [0]
                pend = []
                for s in range(NS):
                    srcv = fm(src_T[s])
                    dstv = fm(dst_T[s])
                    zero_cols(dstv[:, :, 0:dst_off])
                    tail = dst_T.shape[2] - dst_off - L
                    zero_cols(dstv[:, :, dst_off + L:dst_off + L + tail])
                    for r0 in range(0, L, 128):
                        n = min(128, L - r0)
                        if r0 % 512 == 0:
                            ot, otB = ot_r.next()
                            gw = min(512, L - r0)
                            S.dma(ot[:, :, :gw], srcv[:, :, src_off + r0:src_off + r0 + gw], writes=[otB])
                        o0 = r0 % 512
                        rs, rsB = rs_r.next()
                        res_fn(rs, rsB, s, r0, n)
                        bks = ((0, 1), (2, 3))[mi[0] % 2]
                        mi[0] += 1
                        for h in range(2):
                            for c in range(8):
                                mm(bks[h], pb[bks[h]][:n, :], ot[:, c, o0:o0 + n], wo[:, c, h * 512:(h + 1) * 512], c == 0, c == 7,
                                   [otB, woB], c == 7)
                        if len(pend) >= 2:
                            pend.pop(0)()
                        pend.append(ln_epilogue(cx, bks, n, rs, rsB, dst_tok[s, r0:r0 + n, :],
                                                (dstv, dst_off + r0)))
                while pend:
                    pend.pop(0)()
                cx.xs.flush()
                S.barrier()

        def phase_ffn(layer, src_T, src_tok, dst_tok, dst_T, dst_off, final):
            TF = 510
            with ExitStack() as ph:
                (wup, wupB), (wdn, wdnB) = load_ws(ph, [("wup", f_up_d[layer], D, 2 * DFF), ("wdn", f_dn_d[layer], DFF, D)])
                cw = [load_cols(ph, f"cw{j}", f_cw_d[layer, j, :], 44) for j in range(3)]
                cb, cbB = load_cols(ph, "cb", f_cb_d[layer, :], 44)
                cwB = [c[1] for c in cw] + [cbB]
                cx = LNCtx(ph, layer, 1, nr=2, nxs=1, nyo=0, xsw=384)
                xt_r = Ring(nc, ph, "fxt", [128, 8, TF + 2], BF16, 1)
                G = sbt(ph, "G", [128, 22, TF + 2], BF16)
                GBs = [Buf(f"G{j}") for j in range(22)]
                a1_r = Ring(nc, ph, "a1", [128, TF + 2], F32, 2)
                cv_r = Ring(nc, ph, "cv", [128, TF + 2], F32, 2)
                cg_r = Ring(nc, ph, "cg", [128, TF + 2], F32, 2)
                rs_r = Ring(nc, ph, "res", [128, D], F32, 1)
                bi = [0]
                di = [0]
                pend = [None]
                dn_banks = [(5, 6), (5, 6)]
                for s in range(NS):
                    srcv = fm(src_T[s])
                    if dst_T is not None:
                        dstv = fm(dst_T[s])
                        zero_cols(dstv[:, :, 0:dst_off])
                        tail = dst_T.shape[2] - dst_off - L
                        for z0 in range(0, tail, 128):
                            zero_cols(dstv[:, :, dst_off + L + z0:dst_off + L + min(tail, z0 + 128)])
                    for t0 in range(0, L, TF):
                        w = min(TF, L - t0)
                        xt, xtB = xt_r.next()
                        S.dma(xt[:, :, :w + 2], srcv[:, :, t0:t0 + w + 2], writes=[xtB])
                        fin = None
                        for j in range(22):
                            res = []
                            for part in range(2):
                                bk = (0, 1, 2, 3, 7)[bi[0] % 5]
                                bi[0] += 1
                                col = part * DFF + j * 128
                                for k in range(8):
                                    mm(bk, pb[bk][:, :w + 2], wup[:, k, col:col + 128], xt[:, k, :w + 2], k == 0, k == 7,
                                       [wupB, xtB], k == 7)
                                c, cB = (cv_r if part == 0 else cg_r).next()
                                jj = part * 22 + j
                                a1, a1B = a1_r.next()
                                act(c[:, :w], pb[bk][:, 0:w], AF.Identity, cwB, [cB, PW(bk)],
                                    scale=cw[0][0][:, jj:jj + 1], bias=cb[:, jj:jj + 1])
                                act(a1[:, :w], pb[bk][:, 1:w + 1], AF.Identity, cwB, [a1B, PW(bk)],
                                    scale=cw[1][0][:, jj:jj + 1])
                                stt("dve", c[:, :w], pb[bk][:, 2:w + 2], cw[2][0][:, jj:jj + 1], c[:, :w], ALU.mult, ALU.add,
                                    cwB + [cB], [(cB, "p"), PW(bk)])
                                tt("dve", c[:, :w], c[:, :w], a1[:, :w], ALU.add, [cB, a1B], [(cB, "p")])
                                res.append((c, cB))
                                if part == 0 and fin is not None:
                                    fin()
                                    fin = None

                            def mk_fin(j_, res_):
                                def f():
                                    act(res_[1][0][:, :w], res_[1][0][:, :w], AF.Silu, [res_[1][1]], [(res_[1][1], "p")])
                                    tt("pool", G[:, j_, :w], res_[1][0][:, :w], res_[0][0][:, :w], ALU.mult,
                                       [res_[1][1], res_[0][1]], [GBs[j_]])
                                return f
                            fin = mk_fin(j, res)
                        fin()
                        for sub in range(0, w, 128):
                            n = min(128, w - sub)
                            r0 = t0 + sub
                            rs, rsB = rs_r.next()
                            S.dma(rs[:n, :], src_tok[s, r0:r0 + n, :], writes=[rsB])
                            bks = dn_banks[di[0] % 2]
                            di[0] += 1
                            for jr in (range(0, 20), range(20, 22)):
                                for h in range(2):
                                    for j in jr:
                                        mm(bks[h], pb[bks[h]][:n, :], G[:, j, sub:sub + n], wdn[:, j, h * 512:(h + 1) * 512],
                                           j == 0, j == 21, [GBs[j], wdnB], j == 21 or j == 19)
                            if pend[0] is not None:
                                pend[0]()
                                pend[0] = None
                            if final:
                                lo = max(r0, NMETA)
                                if lo < r0 + n:
                                    pend[0] = ln_epilogue(cx, bks, n, rs, rsB, y[s, lo - NMETA:r0 + n - NMETA, :], None,
                                                          row_lo=lo - r0)
                            else:
                                pend[0] = ln_epilogue(cx, bks, n, rs, rsB, dst_tok[s, r0:r0 + n, :],
                                                      (dstv, dst_off + r0))
                if pend[0] is not None:
                    pend[0]()
                cx.xs.flush()
                S.barrier()

        def phase_hgrn(dirn, ODST):
            rev = dirn == 1
            with ExitStack() as ph:
                if rev:
                    wq = wqB = wi = wiB = None
                else:
                    (wq, wqB), (wi, wiB) = load_ws(ph, [("hwq", hw_in_d[:, 0:D], D, D), ("hwi", hw_in_d[:, D:2 * D], D, D)])
                ((wf, wfB),) = load_ws(ph, [("hwf", hw_in_d[:, (2 + dirn) * D:(3 + dirn) * D], D, D)])
                l0, l0B = load_cols(ph, "lb0", hlb_d[dirn, 0, :], 8)
                l1, l1B = load_cols(ph, "lb1", hlb_d[dirn, 1, :], 8)
                lb = sbt(ph, "lb", [128, 8], F32)
                oml = sbt(ph, "oml", [128, 8], F32)
                lbB = Buf("lb")
                tt("dve", lb[:, :], l0[:, :], l1[:, :], ALU.subtract, [l0B, l1B], [lbB])
                act(lb[:, :], lb[:, :], AF.Exp, [lbB], [(lbB, "p")])
                ts("dve", lb[:, :], lb[:, :], 1.0, None, ALU.add, None, [lbB], [(lbB, "p")])
                S.op("dve", lambda e: e.reciprocal(out=lb[:, :], in_=lb[:, :]), reads=[lbB], writes=[(lbB, "p")])
                ts("dve", oml[:, :], lb[:, :], -1.0, 1.0, ALU.mult, ALU.add, [lbB], [(lbB, "p")])
                smask = sbt(ph, "smask", [128, 512], F32)
                smB = Buf("smask")
                memset("pool", smask[:], 1.0, [smB])
                memset("pool", smask[:].rearrange("p (c j) -> p c j", j=64)[:, :, 0:1], 0.0, [(smB, "p")], reads=[smB])
                tri = sbt(ph, "tri", [128, 64], F32)
                triB = Buf("tri")
                memset("pool", tri[:], 1.0, [triB])
                S.op("pool", lambda e: e.affine_select(out=tri[0:64, :], in_=tri[0:64, :], pattern=[[-1 if rev else 1, 64]],
                                                       compare_op=ALU.is_ge, fill=0.0, base=0,
                                                       channel_multiplier=(1 if rev else -1)), reads=[triB], writes=[(triB, "p")])
                cp("act", tri[64:128, :], tri[0:64, :], [triB], [(triB, "p")])
                S32 = sbt(ph, "S32", [128, 8, 128], F32)
                Sbf = sbt(ph, "Sbf", [128, 8, 128], BF16)
                S32B = [Buf(f"S32_{h}") for h in range(8)]
                SbfB = [Buf(f"Sbf_{h}") for h in range(8)]
                bt_r = Ring(nc, ph, "bt", [128, 8, 512], BF16, 2)
                vt_r = Ring(nc, ph, "vt", [128, 4, D], BF16, 2)
                NSET = 2
                qd = [[sbt(ph, f"qd{a}_{h}", [128, 512], BF16) for h in range(8)] for a in range(NSET)]
                qdec = [[sbt(ph, f"qe{a}_{h}", [128, 512], BF16) for h in range(8)] for a in range(NSET)]
                kd = [[sbt(ph, f"kd{a}_{h}", [128, 512], BF16) for h in range(8)] for a in range(NSET)]
                kdT = [[sbt(ph, f"kT{a}_{h}", [128, 4, 128], BF16) for h in range(8)] for a in range(NSET)]
                dl = [[sbt(ph, f"dl{a}_{h}", [128, 8], F32) for h in range(8)] for a in range(NSET)]
                prepB = [[Buf(f"prep{a}_{h}") for h in range(8)] for a in range(NSET)]
                kdec_r = Ring(nc, ph, "kdec", [128, 512], BF16, 2)
                ef = [Ring(nc, ph, f"ef{i}", [128, 512], F32, 2) for i in range(6)]
                am_r = Ring(nc, ph, "am", [128, 64], BF16, 4)
                os_r = Ring(nc, ph, "osb", [128, 512], F32, 2)
                tiles = [(t0, min(512, LP - t0)) for t0 in range(0, LP, 512)]
                if rev:
                    tiles = tiles[::-1]
                pu_banks = [3, 7]
                pui = [0]
                pendA2 = [None]
                pendA1b = [None]
                qcur = {}
                bki = [0]
                SbfG = [Buf("SbfG0"), Buf("SbfG1")]
                amA_r = Ring(nc, ph, "amA", [128, 8, 64], BF16, 3)
                cur = {}

                qall_r = Ring(nc, ph, "qall", [128, 8, 512], F32, 2) if rev else None
                pre = {}

                def loads(s, ti):
                    t0, w = tiles[ti]
                    nb = (w + 127) // 128
                    bt, btB = bt_r.next()
                    S.dma(bt[:, :, :w], fm(B_T[s])[:, :, t0:t0 + w], writes=[btB])
                    vt, vtB = vt_r.next()
                    qa = qaB = None
                    if rev:
                        S.dma(vt[:, 0:nb, :], VT_d[s, t0:t0 + w, :].rearrange("(b p) d -> p b d", p=128), writes=[vtB])
                        qa, qaB = qall_r.next()
                        S.dma(qa[:, :, :w], Q_d[s].rearrange("(h p) t -> p h t", p=128)[:, :, t0:t0 + w], writes=[qaB])
                    pre[ti] = (bt, btB, vt, vtB, qa, qaB)

                def stageA_common(s, ti):
                    t0, w = tiles[ti]
                    nb = (w + 127) // 128
                    if ti not in pre:
                        loads(s, ti)
                    bt, btB, vt, vtB, qa, qaB = pre.pop(ti)
                    qcur[ti] = (qa, qaB)
                    vview = VT_d[s, t0:t0 + w, :].rearrange("(b p) d -> p b d", p=128)
                    if rev:
                        pass
                    else:
                        for blk in range(nb):
                            n = min(128, w - blk * 128)
                            for hf in range(2):
                                for k in range(8):
                                    mm(2, pb[2][:n, :], bt[:, k, blk * 128:blk * 128 + n], wi[:, k, hf * 512:(hf + 1) * 512],
                                       k == 0, k == 7, [btB, wiB], k == 7)
                                cp("act", vt[:n, blk, hf * 512:(hf + 1) * 512], pb[2][:n, :], [], [(vtB, "p"), PW(2)])
                        S.dma(vview, vt[:, 0:nb, :], reads=[vtB], q="act")
                    cur[ti] = (bt, btB, vt, vtB)

                def stageA_head(s, ti, h):
                    t0, w = tiles[ti]
                    a = ti % NSET
                    nch = w // 64
                    nb = (w + 127) // 128
                    bt, btB, vt, vtB = cur[ti]
                    PB = prepB[a][h]
                    for k in range(8):
                        mm(0, pb[0][:, :w], wf[:, k, h * 128:(h + 1) * 128], bt[:, k, :w], k == 0, k == 7, [wfB, btB], k == 7)
                    if not rev:
                        for k in range(8):
                            mm(1, pb[1][:, :w], wq[:, k, h * 128:(h + 1) * 128], bt[:, k, :w], k == 0, k == 7, [wqB, btB], k == 7)
                    e0, e0B = ef[0].next()
                    e1, e1B = ef[1].next()
                    e2, e2B = ef[2].next()
                    e3, e3B = ef[3].next()
                    e4, e4B = ef[4].next()
                    e5, e5B = ef[5].next()
                    act(e0[:, :w], pb[0][:, :w], AF.Exp, [], [e0B, PW(0)], scale=-1.0)
                    if rev:
                        qa, qaB = qcur[ti]
                        e5, e5B = qa[:, h, :], qaB
                    else:
                        cp("act", e5[:, :w], pb[1][:, :w], [], [e5B, PW(1)])
                        S.dma(Q_d[s, h * 128:(h + 1) * 128, t0:t0 + w], e5[:, :w], reads=[e5B], q="act")
                    act(e0[:, :w], e0[:, :w], AF.Ln, [e0B], [(e0B, "p")], bias=1.0)
                    act(e0[:, :w], e0[:, :w], AF.Exp, [e0B], [(e0B, "p")], scale=-1.0)
                    ts("dve", e0[:, :w], e0[:, :w], oml[:, h:h + 1], lb[:, h:h + 1], ALU.mult, ALU.add, [e0B, lbB], [(e0B, "p")])
                    ts("pool", e2[:, :w], e0[:, :w], -1.0, 1.0, ALU.mult, ALU.add, [e0B], [e2B])
                    act(e1[:, :w], e0[:, :w], AF.Ln, [e0B], [e1B])
                    for (p0, p1) in ((0, 48), (L + 48, LP)):
                        lo, hi = max(p0, t0), min(p1, t0 + w)
                        if lo < hi:
                            memset("pool", e1[:, lo - t0:hi - t0], 0.0, [(e1B, "p")], reads=[e1B])
                            memset("pool", e2[:, lo - t0:hi - t0], 0.0, [(e2B, "p")], reads=[e2B])
                    S.op("dve", (lambda o, m_, i: (lambda e: e.tensor_tensor_scan(out=o, data0=m_, data1=i, initial=0.0,
                                                                                 op0=ALU.mult, op1=ALU.add)))(
                        e3[:, :w], smask[:, :w], e1[:, :w]), reads=[smB, e1B], writes=[e3B])
                    B3 = e3[:, :w].rearrange("p (c j) -> p c j", j=64)
                    if rev:
                        tt("pool", e1[:, :w], e1[:, :w], e3[:, :w], ALU.subtract, [e1B, e3B], [(e1B, "p")])
                        tt("dve", B3, e1[:, :w].rearrange("p (c j) -> p c j", j=64),
                           B3[:, :, 63:64].to_broadcast([128, nch, 64]), ALU.add, [e1B, e3B], [(e3B, "p")])
                        i_ref, i_last = 31, 0
                    else:
                        i_ref, i_last = 32, 63
                    def a1b():
                        a1b_body(h, a, w, nch, nb, PB, e0, e0B, e1, e1B, e2, e2B, e3, e3B, e4, e4B, e5, e5B, B3, i_ref, i_last)
                    if pendA1b[0] is not None:
                        pendA1b[0]()
                    pendA1b[0] = a1b

                def a1b_body(h, a, w, nch, nb, PB, e0, e0B, e1, e1B, e2, e2B, e3, e3B, e4, e4B, e5, e5B, B3, i_ref, i_last):
                    E4 = e4[:, :w].rearrange("p (c j) -> p c j", j=64)
                    tt("dve", E4, B3, B3[:, :, i_ref:i_ref + 1].to_broadcast([128, nch, 64]), ALU.subtract, [e3B], [e4B])
                    act(e0[:, :w], e4[:, :w], AF.Exp, [e4B], [e0B])
                    act(e1[:, :w], e4[:, :w], AF.Exp, [e4B], [e1B], scale=-1.0)
                    tt("dve", qd[a][h][:, :w], e5[:, :w], e0[:, :w], ALU.mult, [e5B, e0B], [(PB, "p")])
                    tt("pool", kd[a][h][:, :w], e2[:, :w], e1[:, :w], ALU.mult, [e2B, e1B], [(PB, "p")])
                    act(e0[:, :w], e3[:, :w], AF.Exp, [e3B, PB], [e0B])
                    tt("dve", qdec[a][h][:, :w], e5[:, :w], e0[:, :w], ALU.mult, [e5B, e0B], [(PB, "p")])
                    tt("pool", E4, B3, B3[:, :, i_last:i_last + 1].to_broadcast([128, nch, 64]), ALU.subtract, [e3B, e1B], [e4B])
                    act(e1[:, :w], e4[:, :w], AF.Exp, [e4B, PB], [e1B], scale=-1.0)
                    kdc, kdcB = kdec_r.next()
                    tt("pool", kdc[:, :w], e2[:, :w], e1[:, :w], ALU.mult, [e2B, e1B], [kdcB])
                    act(dl[a][h][:, 0:nch], B3[:, :, i_last], AF.Exp, [e3B], [(PB, "p")])
                    def a2():
                        for blk in range(nb):
                            n = min(128, w - blk * 128)
                            transpose(4, ptr[:n, blk * 128:(blk + 1) * 128], kdc[:, blk * 128:blk * 128 + n], ident[:, :],
                                      [kdcB, identB], blk == nb - 1)
                        cp("act", kdT[a][h][:, 0:nb, :], ptr[:, 0:nb * 128].rearrange("p (b d) -> p b d", d=128),
                           [], [(PB, "p"), PW(4)])
                    if pendA2[0] is not None:
                        pendA2[0]()
                    pendA2[0] = a2

                def flushA2():
                    if pendA1b[0] is not None:
                        pendA1b[0]()
                        pendA1b[0] = None
                    if pendA2[0] is not None:
                        pendA2[0]()
                        pendA2[0] = None

                def stageB_chunk(s, ti, c):
                    t0, w = tiles[ti]
                    a = ti % NSET
                    bt, btB, vt, vtB = cur[ti]
                    bl, hf = c // 2, c % 2
                    p0 = hf * 64
                    cols = slice(c * 64, c * 64 + 64)
                    obk = (6, 1)[bki[0] % 2] if rev else 6
                    pab = (5, 2)[bki[0] % 2] if rev else 5
                    bki[0] += 1
                    for h in range(8):
                        mm(pab, pb[pab][p0:p0 + 64, h * 64:(h + 1) * 64], kd[a][h][:, cols], qd[a][h][:, cols], True, True,
                           [prepB[a][h]], h == 7, tp=(0, p0))
                    am, amB = amA_r.next()
                    tt("dve", am[p0:p0 + 64, :, :], pb[pab][p0:p0 + 64, :].rearrange("p (h t) -> p h t", t=64),
                       tri[p0:p0 + 64, :].unsqueeze(1).to_broadcast([64, 8, 64]), ALU.mult, [triB], [amB, PW(pab)])
                    for h in range(8):
                        PB = prepB[a][h]
                        oh0 = (h // 4) * 64
                        ocol = (h % 4) * 128
                        mm(obk, pb[obk][oh0:oh0 + 64, ocol:ocol + 128], am[p0:p0 + 64, h, :], vt[p0:p0 + 64, bl, h * 128:(h + 1) * 128],
                           True, False, [amB, vtB], False, tp=(p0, oh0))
                        mm(obk, pb[obk][oh0:oh0 + 64, ocol:ocol + 128], qdec[a][h][:, cols], Sbf[:, h, :],
                           False, True, [PB, SbfG[h // 4]], True, tp=(0, oh0))
                        pub = pu_banks[pui[0] % 2]
                        pui[0] += 1
                        mm(pub, pb[pub][:, 0:128], kdT[a][h][p0:p0 + 64, bl, :], vt[p0:p0 + 64, bl, h * 128:(h + 1) * 128],
                           True, True, [PB, vtB], True)
                        stt("dve", S32[:, h, :], S32[:, h, :], dl[a][h][:, c:c + 1], pb[pub][:, 0:128], ALU.mult, ALU.add,
                            [PB, S32B[h]], [(S32B[h], "p"), PW(pub)])
                        if h % 4 == 3:
                            g = h // 4
                            cp("act", Sbf[:, g * 4:(g + 1) * 4, :], S32[:, g * 4:(g + 1) * 4, :],
                               [S32B[hh] for hh in range(g * 4, g * 4 + 4)], [SbfG[g]])
                    osb, osB = os_r.next()
                    cp("act", osb[:, :], pb[obk][:, :], [], [osB, PW(obk)])
                    r_ = t0 + c * 64
                    S.dma(ODST[s, r_:r_ + 64, 0:512], osb[0:64, :], reads=[osB], q="act")
                    S.dma(ODST[s, r_:r_ + 64, 512:1024], osb[64:128, :], reads=[osB], q="act")

                for s in range(NS):
                    for h in range(8):
                        memset("pool", S32[:, h, :], 0.0, [S32B[h]])
                    for g in range(2):
                        memset("pool", Sbf[:, g * 4:(g + 1) * 4, :], 0.0, [SbfG[g]])
                    cur.clear()
                    pre.clear()
                    qcur.clear()
                    INTERLEAVE = os.environ.get("MK_HG_IL", "0") == "1"
                    stageA_common(s, 0)
                    for h in range(8):
                        stageA_head(s, 0, h)
                    flushA2()
                    for ti in range(len(tiles)):
                        t0, w = tiles[ti]
                        nch = w // 64
                        chunks = list(range(nch))
                        if rev:
                            chunks = chunks[::-1]
                        nxt = ti + 1 < len(tiles)
                        pending = list(range(8)) if nxt else []
                        if nxt:
                            loads(s, ti + 1)
                        for ci, c in enumerate(chunks):
                            stageB_chunk(s, ti, c)
                            if nxt and INTERLEAVE:
                                if ci == 0:
                                    stageA_common(s, ti + 1)
                                target = -(-8 * (ci + 1) // nch)
                                while 8 - len(pending) < target and pending:
                                    stageA_head(s, ti + 1, pending.pop(0))
                        if nxt and not INTERLEAVE:
                            stageA_common(s, ti + 1)
                        while pending:
                            stageA_head(s, ti + 1, pending.pop(0))
                        flushA2()
                S.barrier()

        def phase_hgrn_out():
            with ExitStack() as ph:
                (wg, wgB), (wo, woB) = load_ws(ph, [("hwg", hw_in_d[:, 4 * D:5 * D], D, D), ("hwo", hw_o_d, D, D)])
                gn, gnB = load_bcast(ph, "hon", hon_d)
                cx = LNCtx(ph, 1, 0, nr=2, nxs=2, nybf=2, stq=os.environ.get("MK_STQ7", "act"))
                bt_r = Ring(nc, ph, "gbt", [128, 8, 512], BF16, 2)
                btc = [None]
                of_r = Ring(nc, ph, "ofw", [128, D], F32, 3)
                ob_r = Ring(nc, ph, "obw", [128, D], F32, 3)
                sq_r = Ring(nc, ph, "osq", [128, D], F32, 2)
                ss_r = Ring(nc, ph, "oss", [128, 8], F32, 3)
                eg_r = Ring(nc, ph, "eg", [128, D], F32, 3)
                yb_r = Ring(nc, ph, "yb", [128, D], BF16, 3)
                yT_r = Ring(nc, ph, "yT", [128, 8, 128], BF16, 2)
                rs_r = Ring(nc, ph, "res", [128, D], F32, 3)
                wi_ = [0]
                pend = [None]
                for s in range(NS):
                    BTv = fm(B_T[s])
                    CTv = fm(C_T[s])
                    zero_cols(CTv[:, :, 0:1])
                    zero_cols(CTv[:, :, L + 1:L + 2])
                    tl = [(r0, min(128, L - r0)) for r0 in range(0, L, 128)]
                    stt_ = {}
                    stt1_ = {}

                    def part1(i):
                        r0, n = tl[i]
                        pc = r0 + 48
                        if r0 % 512 == 0:
                            gw = min(512, L - r0)
                            btc[0] = bt_r.next()
                            S.dma(btc[0][0][:, :, :gw], BTv[:, :, pc:pc + gw], writes=[btc[0][1]])
                        bt_full, btB = btc[0]
                        bt = bt_full[:, :, r0 % 512:r0 % 512 + n]
                        of, ofB = of_r.next()
                        ob, obB = ob_r.next()
                        S.dma(of[:n, :], OFW[s, pc:pc + n, :], writes=[ofB])
                        S.dma(ob[:n, :], OBW[s, pc:pc + n, :], writes=[obB])
                        rs, rsB = rs_r.next()
                        S.dma(rs[:n, :], B_tok[s, r0:r0 + n, :], writes=[rsB])
                        eg, egB = eg_r.next()
                        for hf in range(2):
                            for k in range(8):
                                mm(hf, pb[hf][:n, :], bt[:, k, :], wg[:, k, hf * 512:(hf + 1) * 512], k == 0, k == 7, [btB, wgB], k == 7)
                            act(eg[:n, hf * 512:(hf + 1) * 512], pb[hf][:n, :], AF.Exp, [], [(egB, "p"), PW(hf)], scale=-1.0)
                        act(eg[:n, :], eg[:n, :], AF.Ln, [egB], [(egB, "p")], bias=1.0)
                        act(eg[:n, :], eg[:n, :], AF.Exp, [egB], [(egB, "p")], scale=-1.0)
                        for hf in range(2):
                            tt("dve", eg[:n, hf * 512:(hf + 1) * 512], eg[:n, hf * 512:(hf + 1) * 512], pb[hf][:n, :], ALU.mult,
                               [egB], [(egB, "p"), PW(hf)])
                        stt1_[i] = (n, of, ofB, ob, obB, eg, egB, rs, rsB)

                    def part1b(i):
                        n, of, ofB, ob, obB, eg, egB, rs, rsB = stt1_.pop(i)
                        tt("dve", of[:n, :], of[:n, :], ob[:n, :], ALU.add, [ofB, obB], [(ofB, "p")])
                        sq, sqB = sq_r.next()
                        ss, ssB = ss_r.next()
                        tt("dve", sq[:n, :], of[:n, :], of[:n, :], ALU.mult, [ofB], [sqB])
                        S.op("dve", (lambda o, i_: (lambda e: e.tensor_reduce(out=o, in_=i_, axis=AX.X, op=ALU.add)))(
                            ss[:n, :], sq[:n, :].rearrange("p (h d) -> p h d", d=128)), reads=[sqB], writes=[ssB])
                        ts("dve", ss[:n, :], ss[:n, :], 1.0 / 128, EPS, ALU.mult, ALU.add, [ssB], [(ssB, "p")])
                        rsqrt_inplace(ss[:n, :], ssB)
                        o3 = of[:n, :].rearrange("p (h d) -> p h d", d=128)
                        tt("dve", o3, o3, ss[:n, :].unsqueeze(2).to_broadcast([n, 8, 128]), ALU.mult, [ofB, ssB], [(ofB, "p")])
                        tt("dve", of[:n, :], of[:n, :], gn[:n, :], ALU.mult, [ofB, gnB], [(ofB, "p")])
                        yb, ybB = yb_r.next()
                        tt("dve", yb[:n, :], of[:n, :], eg[:n, :], ALU.mult, [ofB, egB], [ybB])
                        stt_[i] = (yb, ybB, rs, rsB)

                    stt2_ = {}

                    def part2a(i):
                        r0, n = tl[i]
                        yb, ybB, rs, rsB = stt_.pop(i)
                        for c in range(8):
                            transpose(4, ptr[:, c * 128:c * 128 + n], yb[:n, c * 128:(c + 1) * 128], ident[:n, :n], [ybB, identB], c == 7)
                        yT, yTB = yT_r.next()
                        cp("act", yT[:, :, :n], ptr[:, :].rearrange("p (c j) -> p c j", j=128)[:, :, :n], [], [yTB, PW(4)])
                        stt2_[i] = (yT, yTB, rs, rsB)

                    def part2(i):
                        r0, n = tl[i]
                        yT, yTB, rs, rsB = stt2_.pop(i)
                        bks = ((2, 3), (5, 6))[wi_[0] % 2]
                        wi_[0] += 1
                        for hf in range(2):
                            for c in range(8):
                                mm(bks[hf], pb[bks[hf]][:n, :], yT[:, c, :n], wo[:, c, hf * 512:(hf + 1) * 512], c == 0, c == 7,
                                   [yTB, woB], c == 7)
                        if pend[0] is not None:
                            pend[0]()
                        pend[0] = ln_epilogue(cx, bks, n, rs, rsB, C_tok[s, r0:r0 + n, :], (CTv, 1 + r0))

                    part1(0)
                    part1b(0)
                    if len(tl) > 1:
                        part1(1)
                        part1b(1)
                    part2a(0)
                    for i in range(len(tl)):
                        if i + 2 < len(tl):
                            part1(i + 2)
                        part2(i)
                        if i + 1 < len(tl):
                            part2a(i + 1)
                        if i + 2 < len(tl):
                            part1b(i + 2)
                if pend[0] is not None:
                    pend[0]()
                cx.xs.flush()
                S.barrier()

        def on(p):
            return phases is None or p in phases
        if on(0):
            phase_T0()
        if on(1):
            phase_mla_proj()
        if on(2):
            phase_attn()
        if on(3):
            phase_mix_out("mwo", w_o_d, OT, 0, load_h0_rows, 0, A_tok, A_T, 1)
        if on(4):
            phase_ffn(0, A_T, A_tok, B_tok, B_T, 48, False)
        if on(5):
            phase_hgrn(0, OFW)
        if on(6):
            phase_hgrn(1, OBW)
        if on(7):
            phase_hgrn_out()
        if on(8):
            phase_ffn(1, C_T, C_tok, None, None, 0, True)
        S.barrier()
        print("instructions:", S.n_inst, {k: len(v) for k, v in S.streams.items()}, flush=True)
        with nc.Block() as block:
            S.emit(block)
    return nc


def rope_tables_np(L):
    inv = (1.0 / (10000.0 ** (np.arange(0, DR, 2, dtype=np.float32) / np.float32(DR)))).astype(np.float32)
    ang = (np.arange(L, dtype=np.float32)[:, None] * inv[None, :]).astype(np.float32)
    c = np.cos(ang).astype(np.float32).T
    s = np.sin(ang).astype(np.float32).T
    return (np.ascontiguousarray(np.concatenate([c, c], 0)), np.ascontiguousarray(np.concatenate([s, s], 0)))


def run(seqs, weights, n_cores=8):
    S_LEN = seqs[0].shape[0]
    nseq = len(seqs)
    NS = (nseq + n_cores - 1) // n_cores
    nc = build(S_LEN, NS)
    cosT, sinT = rope_tables_np(S_LEN + NMETA)
    f32 = lambda a: np.ascontiguousarray(np.asarray(a, dtype=np.float32))
    common = {
        "meta": f32(weights["meta_tokens"]),
        "mla_w_in": f32(weights["mla_w_in"][0]), "mla_q_norm": f32(weights["mla_q_norm"][0]),
        "mla_kv_norm": f32(weights["mla_kv_norm"][0]), "mla_w_uq": f32(weights["mla_w_uq"][0]),
        "mla_w_ukv": f32(weights["mla_w_ukv"][0]), "mla_w_o": f32(weights["mla_w_o"][0]),
        "hgrn_w_in": f32(weights["hgrn_w_in"][0]), "hgrn_lb": f32(weights["hgrn_lower_bound"]),
        "hgrn_o_norm": f32(weights["hgrn_o_norm"][0]), "hgrn_w_o": f32(weights["hgrn_w_o"][0]),
        "ffn_w_up": f32(weights["ffn_w_up"]), "ffn_conv_w": f32(weights["ffn_conv_w"]),
        "ffn_conv_b": f32(weights["ffn_conv_b"]), "ffn_w_down": f32(weights["ffn_w_down"]),
        "ln_gain": f32(weights["ln_gain"]), "ln_bias": f32(weights["ln_bias"]),
        "rope_cos": cosT, "rope_sin": sinT,
    }
    assign = []
    in_maps = []
    for c in range(n_cores):
        ids = [c + n_cores * j for j in range(NS)]
        ids = [i if i < nseq else ids[0] % nseq for i in ids]
        assign.append(ids)
        m = dict(common)
        m["x"] = np.ascontiguousarray(np.stack([seqs[i] for i in ids], 0))
        in_maps.append(m)
    res = run_bass_kernel_spmd(nc, in_maps, core_ids=list(range(n_cores)))
    outs = [None] * nseq
    for c in range(n_cores):
        yc = np.asarray(res.results[c]["y"])
        for j in range(NS):
            i = c + n_cores * j
            if i < nseq:
                outs[i] = yc[j]
    return outs


def kernel(x_prompt, x_sample, meta_tokens, mla_w_in, mla_q_norm, mla_kv_norm, mla_w_uq, mla_w_ukv, mla_w_o,
           hgrn_w_in, hgrn_lower_bound, hgrn_o_norm, hgrn_w_o,
           ffn_w_up, ffn_conv_w, ffn_conv_b, ffn_w_down, ln_gain, ln_bias):
    x_prompt = np.asarray(x_prompt, dtype=np.float32)
    x_sample = np.asarray(x_sample, dtype=np.float32)
    weights = dict(meta_tokens=meta_tokens, mla_w_in=mla_w_in, mla_q_norm=mla_q_norm, mla_kv_norm=mla_kv_norm,
                   mla_w_uq=mla_w_uq, mla_w_ukv=mla_w_ukv, mla_w_o=mla_w_o, hgrn_w_in=hgrn_w_in,
                   hgrn_lower_bound=hgrn_lower_bound, hgrn_o_norm=hgrn_o_norm, hgrn_w_o=hgrn_w_o,
                   ffn_w_up=ffn_w_up, ffn_conv_w=ffn_conv_w, ffn_conv_b=ffn_conv_b, ffn_w_down=ffn_w_down,
                   ln_gain=ln_gain, ln_bias=ln_bias)
    weights = {k: np.asarray(v, dtype=np.float32) for k, v in weights.items()}
    seqs = [x_prompt[i] for i in range(x_prompt.shape[0])] + [x_sample[i] for i in range(x_sample.shape[0])]
    outs = run(seqs, weights)
    nb = x_prompt.shape[0]
    y_prompt = np.stack(outs[:nb], 0).astype(np.float32)
    y_sample = np.stack(outs[nb:], 0).astype(np.float32)
    return (y_prompt, y_sample)
```

```python
import math
import os
from contextlib import ExitStack
import numpy as np
import concourse.bass as bass
import concourse.mybir as mybir
from concourse.bass_utils import run_bass_kernel_spmd

F32 = mybir.dt.float32
BF16 = mybir.dt.bfloat16
AF = mybir.ActivationFunctionType
ALU = mybir.AluOpType
AX = mybir.AxisListType

D = 1024
NMETA = 16
QL, KVL, DR, DN, DV, NH = 768, 256, 32, 64, 64, 16
MLA_SCALE = (DN + DR) ** -0.5
HG_H = 8
DFF = 2816
DEPTH = 2
ALPHA = (2 * DEPTH) ** 0.25
EPS = 1e-6
COMPUTE = ("pe", "act", "dve", "pool")


class Buf:
    __slots__ = ("name", "w", "r")

    def __init__(self, name=""):
        self.name = name
        self.w = {}
        self.r = {}


class Sched:
    K = 8
    ND = 24

    def __init__(self, nc, same_engine_sync=True):
        self.nc = nc
        self.same = same_engine_sync
        self.streams = {e: [] for e in COMPUTE + ("sp",)}
        self.cnt = {e: 0 for e in COMPUTE}
        self.dcnt = {"dma": 0, "dmaA": 0}
        self.known = {s: {e: -1 for e in COMPUTE} for s in self.streams}
        self.dma_low = {(s, k): 0 for s in self.streams for k in ("dma", "dmaA")}
        self.dma_set = {(s, k): set() for s in self.streams for k in ("dma", "dmaA")}
        self.sems = {}
        self.n_inst = 0

    def alloc(self, stack):
        for e in COMPUTE:
            self.sems[e] = [stack.enter_context(self.nc.semaphore(f"s_{e}{i}")) for i in range(self.K)]
        self.sems["dma"] = [stack.enter_context(self.nc.semaphore(f"s_dma{i}")) for i in range(self.ND)]
        self.sems["dmaA"] = [stack.enter_context(self.nc.semaphore(f"s_dmaA{i}")) for i in range(self.ND)]

    def _knows(self, s, ev):
        k, idx = ev
        if k.startswith("dma"):
            return idx < self.dma_low[(s, k)] or idx in self.dma_set[(s, k)]
        return self.known[s][k] >= idx

    def _learn(self, s, ev):
        k, idx = ev
        if k.startswith("dma"):
            self.dma_set[(s, k)].add(idx)
            self.dma_low[(s, k)] = max(self.dma_low[(s, k)], idx - self.ND + 1)
        else:
            self.known[s][k] = max(self.known[s][k], idx)

    def _wait(self, s, ev):
        if self._knows(s, ev):
            return
        k, idx = ev
        if k.startswith("dma"):
            sem = self.sems[k][idx % self.ND]
            val = 16 * (idx // self.ND + 1)
        else:
            sem = self.sems[k][idx % self.K]
            val = idx // self.K + 1
        self.streams[s].append(("wait", sem, val))
        self._learn(s, ev)

    @staticmethod
    def _events(d):
        for k, v in d.items():
            if isinstance(k, tuple):
                yield k
            else:
                yield (k, v)

    def _add(self, d, ev):
        k, idx = ev
        if k.startswith("dma"):
            for kk in [kk for kk in d if isinstance(kk, tuple) and kk[0] == k and kk[1] <= idx - self.ND]:
                del d[kk]
            d[ev] = True
        elif d.get(k, -1) < idx:
            d[k] = idx

    def _deps(self, s, eng, reads, writes):
        deps = set()
        for b in reads:
            deps.update(self._events(b.w))
        for b in writes:
            if isinstance(b, tuple):
                b = b[0]
            deps.update(self._events(b.w))
            deps.update(self._events(b.r))
        for ev in sorted(deps, key=lambda e: (str(e[0]), e[1])):
            if ev[0] == eng and (eng == "pe" or not self.same):
                continue
            self._wait(s, ev)

    def _post(self, ev, reads, writes):
        for b in reads:
            self._add(b.r, ev)
        for b in writes:
            if isinstance(b, tuple):
                self._add(b[0].w, ev)
            else:
                b.w = {}
                b.r = {}
                self._add(b.w, ev)

    def op(self, eng, fn, reads=(), writes=(), track=True):
        self._deps(eng, eng, reads, writes)
        idx = self.cnt[eng]
        if track:
            self.cnt[eng] += 1
            self.streams[eng].append(("inst", fn, self.sems[eng][idx % self.K], 1))
        else:
            self.streams[eng].append(("inst", fn, None, 0))
        self._post((eng, idx), reads, writes)
        self.n_inst += 1

    def dma(self, out, in_, reads=(), writes=(), q="sp", **kw):
        s = q
        kind = "dma" if q == "sp" else "dmaA"
        self._deps(s, kind, reads, writes)
        idx = self.dcnt[kind]
        if idx >= self.ND:
            self._wait(s, (kind, idx - self.ND))
        self.dcnt[kind] += 1
        sem = self.sems[kind][idx % self.ND]
        self.streams[s].append(("inst", lambda e: e.dma_start(out=out, in_=in_, **kw), sem, 16))
        self._post((kind, idx), reads, writes)
        self.n_inst += 1

    def barrier(self):
        for s in self.streams:
            for kind in ("dma", "dmaA"):
                for idx in range(max(0, self.dcnt[kind] - self.ND), self.dcnt[kind]):
                    self._wait(s, (kind, idx))
            for e in COMPUTE:
                if self.cnt[e] > 0:
                    self._wait(s, (e, self.cnt[e] - 1))

    def emit(self, block):
        def replay(name):
            def run(eng):
                for ent in self.streams[name]:
                    if ent[0] == "wait":
                        eng.wait_ge(ent[1], ent[2])
                    else:
                        ins = ent[1](eng)
                        if ent[2] is not None:
                            ins.then_inc(ent[2], ent[3])
            return run

        block.tensor(replay("pe"))
        block.scalar(replay("act"))
        block.vector(replay("dve"))
        block.gpsimd(replay("pool"))
        block.sync(replay("sp"))


_UID = [0]


def _uniq(name):
    _UID[0] += 1
    return f"{name}_u{_UID[0]}"


class Ring:
    def __init__(self, nc, st, name, shape, dt, n):
        self.t = [st.enter_context(nc.sbuf_tensor(_uniq(f"{name}{i}"), shape, dt)) for i in range(n)]
        self.b = [Buf(f"{name}{i}") for i in range(n)]
        self.i = 0

    def next(self):
        j = self.i % len(self.t)
        self.i += 1
        return self.t[j], self.b[j]


def build(S_LEN, NS, phases=None, same=True):
    L = S_LEN + NMETA
    LP = L + 112
    NKT = (L + 127) // 128
    nc = bass.Bass("TRN2", target_bir_lowering=False)

    def din(name, shape, dt=F32):
        return nc.dram_tensor(name, list(shape), dt, kind="ExternalInput").ap()

    def dscr(name, shape, dt):
        return nc.dram_tensor(name, list(shape), dt, kind="Internal").ap()

    x = din("x", [NS, S_LEN, D])
    meta = din("meta", [NMETA, D])
    w_in_d = din("mla_w_in", [D, QL + KVL + DR])
    qn_d = din("mla_q_norm", [QL])
    kvn_d = din("mla_kv_norm", [KVL])
    w_uq_d = din("mla_w_uq", [QL, NH * 96])
    w_ukv_d = din("mla_w_ukv", [KVL, NH * 128])
    w_o_d = din("mla_w_o", [D, D])
    hw_in_d = din("hgrn_w_in", [D, 5 * D])
    hlb_d = din("hgrn_lb", [2, 2, D])
    hon_d = din("hgrn_o_norm", [D])
    hw_o_d = din("hgrn_w_o", [D, D])
    f_up_d = din("ffn_w_up", [2, D, 2 * DFF])
    f_cw_d = din("ffn_conv_w", [2, 3, 2 * DFF])
    f_cb_d = din("ffn_conv_b", [2, 2 * DFF])
    f_dn_d = din("ffn_w_down", [2, DFF, D])
    lng_d = din("ln_gain", [2, 2, D])
    lnb_d = din("ln_bias", [2, 2, D])
    cos_d = din("rope_cos", [DR, L])
    sin_d = din("rope_sin", [DR, L])
    y = nc.dram_tensor("y", [NS, S_LEN, D], F32, kind="ExternalOutput").ap()

    XT = dscr("XT", [NS, D, L], BF16)
    CQ = dscr("CQ", [NS, QL, L], BF16)
    CKV = dscr("CKV", [NS, KVL, L], BF16)
    KR = dscr("KR", [NS, DR, L], BF16)
    OT = dscr("OT", [NS, D, L], BF16)
    A_tok = dscr("A_tok", [NS, L, D], F32)
    A_T = dscr("A_T", [NS, D, L + 2], BF16)
    B_tok = dscr("B_tok", [NS, L, D], F32)
    B_T = dscr("B_T", [NS, D, LP], BF16)
    VT_d = dscr("VT_d", [NS, LP, D], BF16)
    Q_d = dscr("Q_d", [NS, D, LP], F32)
    OFW = dscr("OFW", [NS, LP, D], F32)
    OBW = dscr("OBW", [NS, LP, D], F32)
    C_tok = dscr("C_tok", [NS, L, D], F32)
    C_T = dscr("C_T", [NS, D, L + 2], BF16)

    def fm(ap2d):
        return ap2d.rearrange("(c p) t -> p c t", p=128)

    with ExitStack() as st:
        S = Sched(nc, same_engine_sync=(same and os.environ.get('MK_SAME', '1') == '1'))
        S.alloc(st)

        def sbt(stack, name, shape, dt=F32):
            return stack.enter_context(nc.sbuf_tensor(_uniq(name), list(shape), dt))

        pb, pbB = [], []
        for i in range(8):
            if i == 4:
                pb.append(st.enter_context(nc.psum_tensor("ptr", [128, 1024], BF16)))
            else:
                pb.append(st.enter_context(nc.psum_tensor(f"pb{i}", [128, 512], F32)))
            pbB.append(Buf(f"pb{i}"))
        ptr = pb[4]

        def PW(i):
            return (pbB[i], "p")

        def mm(bank, out, lhsT, rhs, start, stop, reads, track, tp=None):
            kw = {"tile_position": tp} if tp is not None else {}
            S.op("pe", lambda e: e.matmul(out, lhsT=lhsT, rhs=rhs, start=start, stop=stop, **kw),
                 reads=reads, writes=[PW(bank)], track=track)

        def transpose(bank, out, in_, ident_ap, reads, track):
            S.op("pe", lambda e: e.transpose(out, in_, ident_ap), reads=reads, writes=[PW(bank)], track=track)

        def act(out, in_, func, reads, writes, scale=None, bias=None, accum_out=None):
            kw = {}
            if scale is not None:
                kw["scale"] = scale
            if bias is not None:
                kw["bias"] = bias
            if accum_out is not None:
                kw["accum_out"] = accum_out
            S.op("act", lambda e: e.activation(out=out, in_=in_, func=func, **kw), reads=reads, writes=writes)

        def cp(eng, out, in_, reads, writes):
            if eng == "act":
                S.op("act", lambda e: e.copy(out=out, in_=in_), reads=reads, writes=writes)
            else:
                S.op(eng, lambda e: e.tensor_copy(out=out, in_=in_), reads=reads, writes=writes)

        def tt(eng, out, in0, in1, op, reads, writes):
            S.op(eng, lambda e: e.tensor_tensor(out=out, in0=in0, in1=in1, op=op), reads=reads, writes=writes)

        def ts(eng, out, in0, s1, s2, op0, op1, reads, writes):
            if s2 is None:
                if op0 == ALU.pow:
                    s1, s2, op0, op1 = 1.0, s1, ALU.mult, ALU.pow
                else:
                    s2, op1 = 0.0, ALU.add
            if True:
                S.op(eng, lambda e: e.tensor_scalar(out=out, in0=in0, scalar1=s1, scalar2=s2, op0=op0, op1=op1),
                     reads=reads, writes=writes)

        def stt(eng, out, in0, scalar, in1, op0, op1, reads, writes):
            S.op(eng, lambda e: e.scalar_tensor_tensor(out=out, in0=in0, scalar=scalar, in1=in1, op0=op0, op1=op1),
                 reads=reads, writes=writes)

        def rsqrt_inplace(ap, B):
            act(ap, ap, AF.Ln, [B], [(B, "p")])
            act(ap, ap, AF.Exp, [B], [(B, "p")], scale=-0.5)

        def memset(eng, ap, val, writes, reads=()):
            S.op(eng, lambda e: e.memset(ap, val), reads=reads, writes=writes)

        ident = sbt(st, "ident", [128, 128], BF16)
        identB = Buf("ident")
        ones = sbt(st, "ones", [128, 128], BF16)
        onesB = Buf("ones")
        zer = sbt(st, "zer", [128, 128], BF16)
        zerB = Buf("zer")
        memset("pool", ident[:], 1.0, [identB])
        S.op("pool", lambda e: e.affine_select(out=ident[:], in_=ident[:], pattern=[[-1, 128]],
                                               compare_op=ALU.is_equal, fill=0.0, base=0, channel_multiplier=1),
             reads=[identB], writes=[identB])
        memset("pool", ones[:], 1.0, [onesB])
        memset("pool", zer[:], 0.0, [zerB])
        stg = Ring(nc, st, "stg", [128, 512], F32, 2)
        cast_i = [0]

        def load_w(stack, name, src, K, N):
            kc = K // 128
            wt = sbt(stack, name, [128, kc, N], BF16)
            wb = Buf(name)
            first = True
            for k in range(kc):
                for c0 in range(0, N, 512):
                    cw = min(512, N - c0)
                    t, tb = stg.next()
                    S.dma(t[:, :cw], src[k * 128:(k + 1) * 128, c0:c0 + cw], writes=[tb])
                    eng = ("pool", "dve")[cast_i[0] % 2]
                    cast_i[0] += 1
                    cp(eng, wt[:, k, c0:c0 + cw], t[:, :cw], [tb], [wb] if first else [(wb, "p")])
                    first = False
            return wt, wb

        def load_ws(stack, specs):
            outs = []
            for (name, src, K, N) in specs:
                outs.append((sbt(stack, name, [128, K // 128, N], BF16), Buf(name)))
            with ExitStack() as tmp:
                big = Ring(nc, tmp, "stgb", [128, 1024], F32, 6)
                for (name, src, K, N), (wt, wb) in zip(specs, outs):
                    first = True
                    for k in range(K // 128):
                        for c0 in range(0, N, 1024):
                            cw = min(1024, N - c0)
                            t, tb = big.next()
                            S.dma(t[:, :cw], src[k * 128:(k + 1) * 128, c0:c0 + cw], writes=[tb])
                            eng = ("pool", "dve", "dve")[cast_i[0] % 3]
                            cast_i[0] += 1
                            cp(eng, wt[:, k, c0:c0 + cw], t[:, :cw], [tb], [wb] if first else [(wb, "p")])
                            first = False
                S.barrier()
            return outs

        def load_cols(stack, name, src1d, ncol):
            t = sbt(stack, name, [128, ncol], F32)
            b = Buf(name)
            v = src1d.rearrange("(c p) -> p c", p=128)
            first = True
            for c0 in range(0, ncol, 11):
                c1 = min(ncol, c0 + 11)
                S.dma(t[:, c0:c1], v[:, c0:c1], writes=[b] if first else [(b, "p")], allow_slow_non_contiguous=True)
                first = False
            return t, b

        def load_bcast(stack, name, src1d):
            t = sbt(stack, name, [128, D], F32)
            b = Buf(name)
            S.dma(t[:], src1d.partition_broadcast(128), writes=[b])
            return t, b

        def load_h0_rows(t, tb, s, r0, n):
            if r0 == 0:
                S.dma(t[0:NMETA, :], meta[:, :], writes=[tb])
                S.dma(t[NMETA:n, :], x[s, 0:n - NMETA, :], writes=[(tb, "p")])
            else:
                S.dma(t[0:n, :], x[s, r0 - NMETA:r0 - NMETA + n, :], writes=[tb])

        class FMStore:
            def __init__(self, stack, name, nbuf=2, width=512, q="act"):
                self.q = q
                self.ring = Ring(nc, stack, name, [128, 8, width], BF16, nbuf)
                self.width = width
                self.cur = None

            def add(self, src, srcB, n, dst, col):
                c = self.cur
                if c is None or c["dst"] is not dst or c["col0"] + c["filled"] != col or c["filled"] + n > self.width:
                    self.flush()
                    t, b = self.ring.next()
                    c = self.cur = dict(t=t, b=b, dst=dst, col0=col, filled=0)
                for k in range(8):
                    transpose(4, ptr[:, k * 128:k * 128 + n], src[:n, k * 128:(k + 1) * 128], ident[:n, :n],
                              [srcB, identB], track=(k == 7))
                f = c["filled"]
                cp("act", c["t"][:, :, f:f + n], ptr[:, :].rearrange("p (c j) -> p c j", j=128)[:, :, :n], [],
                   [c["b"] if f == 0 else (c["b"], "p"), PW(4)])
                c["filled"] = f + n
                if c["filled"] >= self.width:
                    self.flush()

            def flush(self):
                c = self.cur
                if c is not None and c["filled"] > 0:
                    S.dma(c["dst"][:, :, c["col0"]:c["col0"] + c["filled"]], c["t"][:, :, :c["filled"]], reads=[c["b"]], q=self.q)
                self.cur = None

        def to_featmajor(fms, src, srcB, n, dst, col):
            fms.add(src, srcB, n, dst, col)

        class LNCtx:
            def __init__(self, stack, layer, idx, nr=2, nxs=2, nybf=1, nyo=1, xsw=512, stq="act"):
                self.stq = stq
                self.g, self.gB = load_bcast(stack, "lng", lng_d[layer, idx, :])
                self.b, self.bB = load_bcast(stack, "lnb", lnb_d[layer, idx, :])
                self.r = Ring(nc, stack, "ln_r", [128, D], F32, nr)
                self.yo = Ring(nc, stack, "ln_y", [128, D], F32, nyo) if nyo > 0 else None
                self.ybf = Ring(nc, stack, "ln_ybf", [128, D], BF16, nybf)
                self.xs = FMStore(stack, "ln_xs", nbuf=nxs, width=xsw, q=stq)
                self.st = Ring(nc, stack, "ln_st", [128, 2, 6], F32, 2)
                self.mv = Ring(nc, stack, "ln_mv", [128, 4], F32, 2)

        def ln_epilogue(cx, banks, n, res, resB, out_rows, outT, row_lo=0):
            r, rB = cx.r.next()
            for h in range(2):
                stt("dve", r[:n, h * 512:(h + 1) * 512], res[:n, h * 512:(h + 1) * 512], ALPHA,
                    pb[banks[h]][:n, :], ALU.mult, ALU.add, [resB], [(rB, "p"), PW(banks[h])])
            sT, sB = cx.st.next()
            mv, mvB = cx.mv.next()
            for h in range(2):
                S.op("dve", (lambda o, i: (lambda e: e.bn_stats(out=o, in_=i)))(sT[:n, h, :], r[:n, h * 512:(h + 1) * 512]),
                     reads=[rB], writes=[(sB, "p")])
            S.op("dve", lambda e: e.bn_aggr(out=mv[:n, 0:2], in_=sT[:n, :, :].rearrange("p a b -> p (a b)")),
                 reads=[sB], writes=[(mvB, "p")])
            ts("dve", mv[:n, 2:3], mv[:n, 1:2], EPS, None, ALU.add, None, [mvB], [(mvB, "p")])
            rsqrt_inplace(mv[:n, 2:3], mvB)
            stt("dve", mv[:n, 3:4], mv[:n, 0:1], -1.0, mv[:n, 2:3], ALU.mult, ALU.mult, [mvB], [(mvB, "p")])
            if cx.yo is not None:
                yo, yB = cx.yo.next()
                act(yo[:n, :], r[:n, :], AF.Identity, [rB, mvB], [yB], scale=mv[:n, 2:3], bias=mv[:n, 3:4])
            else:
                yo, yB = r, rB
                act(yo[:n, :], r[:n, :], AF.Identity, [rB, mvB], [(rB, "p")], scale=mv[:n, 2:3], bias=mv[:n, 3:4])
            tt("dve", yo[:n, :], yo[:n, :], cx.g[:n, :], ALU.mult, [cx.gB, yB], [(yB, "p")])
            tt("dve", yo[:n, :], yo[:n, :], cx.b[:n, :], ALU.add, [cx.bB, yB], [(yB, "p")])
            yb = ybB = None
            if outT is not None:
                yb, ybB = cx.ybf.next()
                cp("act", yb[:n, :], yo[:n, :], [yB], [ybB])

            def fin():
                if out_rows is not None:
                    S.dma(out_rows, yo[row_lo:n, :], reads=[yB], q=cx.stq)
                if outT is not None:
                    to_featmajor(cx.xs, yb, ybB, n, outT[0], outT[1])
            return fin

        def zero_cols(dst3):
            w = dst3.shape[2]
            for c in range(8):
                S.dma(dst3[:, c, :], zer[:, 0:w], reads=[zerB], allow_slow_non_contiguous=True)

        def phase_T0():
            with ExitStack() as ph:
                xr = Ring(nc, ph, "t0x", [128, D], F32, 2)
                xb = Ring(nc, ph, "t0b", [128, D], BF16, 2)
                xs = FMStore(ph, "t0s", nbuf=2)
                for s in range(NS):
                    XTv = fm(XT[s])
                    for r0 in range(0, L, 128):
                        n = min(128, L - r0)
                        t, tb = xr.next()
                        load_h0_rows(t, tb, s, r0, n)
                        u, ub = xb.next()
                        cp("dve", u[:n, :], t[:n, :], [tb], [ub])
                        to_featmajor(xs, u, ub, n, XTv, r0)
                xs.flush()
                S.barrier()

        def phase_mla_proj():
            with ExitStack() as ph:
                ((w_in, w_inB),) = load_ws(ph, [("w_in", w_in_d, D, QL + KVL + DR)])
                wr = sbt(ph, "w_inr", [128, 8, 96], BF16)
                wrB = Buf("w_inr")
                memset("pool", wr[:], 0.0, [wrB])
                act(wr[:, :, 64:80], w_in[:, :, 1040:1056], AF.Identity, [w_inB], [(wrB, "p")], scale=-1.0)
                cp("pool", wr[:, :, 80:96], w_in[:, :, 1024:1040], [w_inB], [(wrB, "p")])
                gq, gqB = load_cols(ph, "gq", qn_d, 6)
                gkv, gkvB = load_cols(ph, "gkv", kvn_d, 2)
                cosT = sbt(ph, "cosT", [128, L], F32)
                sinT = sbt(ph, "sinT", [128, L], F32)
                csB = Buf("cs")
                S.dma(cosT[64:96, :], cos_d[:, :], writes=[csB])
                S.dma(sinT[64:96, :], sin_d[:, :], writes=[(csB, "p")])
                xt_r = Ring(nc, ph, "xt", [128, 8, 512], BF16, 2)
                sq_r = Ring(nc, ph, "sq", [128, 512], BF16, 2)
                craw = sbt(ph, "craw", [128, 8, 512], F32)
                crawB = Buf("craw")
                rq_r = Ring(nc, ph, "rq", [128, 2, 512], F32, 2)
                cn_r = Ring(nc, ph, "cn", [128, 512], BF16, 3)
                t1_r = Ring(nc, ph, "rt1", [128, 512], F32, 2)
                t2_r = Ring(nc, ph, "rt2", [128, 512], F32, 2)
                kr_r = Ring(nc, ph, "krt", [128, 512], BF16, 2)
                for s in range(NS):
                    XTv = fm(XT[s])
                    CQv = fm(CQ[s])
                    CKVv = fm(CKV[s])
                    ptiles = [(t0, min(512, L - t0)) for t0 in range(0, L, 512)]
                    xts = {}

                    def load_xt(i):
                        t0_, w_ = ptiles[i]
                        xt_, xtB_ = xt_r.next()
                        S.dma(xt_[:, :, :w_], XTv[:, :, t0_:t0_ + w_], writes=[xtB_])
                        xts[i] = (xt_, xtB_)

                    load_xt(0)
                    for ti_, (t0, w) in enumerate(ptiles):
                        xt, xtB = xts.pop(ti_)
                        if ti_ + 1 < len(ptiles):
                            load_xt(ti_ + 1)
                        for m in range(8):
                            bk = m % 2
                            for k in range(8):
                                mm(bk, pb[bk][:, :w], w_in[:, k, m * 128:(m + 1) * 128], xt[:, k, :w],
                                   k == 0, k == 7, [w_inB, xtB], k == 7)
                            sq, sqB = sq_r.next()
                            act(sq[:, :w], pb[bk][:, :w], AF.Square, [], [sqB, PW(bk)])
                            cp("dve", craw[:, m, :w], pb[bk][:, :w], [], [(crawB, "p"), PW(bk)])
                            sb_ = 2 if m < 6 else 3
                            mm(sb_, pb[sb_][:, :w], ones[:, :], sq[:, :w], m in (0, 6), m in (5, 7), [onesB, sqB], True)
                        rq, rqB = rq_r.next()
                        ts("dve", rq[:, 0, :w], pb[2][:, :w], 1.0 / QL, EPS, ALU.mult, ALU.add, [], [(rqB, "p"), PW(2)])
                        rsqrt_inplace(rq[:, 0, :w], rqB)
                        ts("dve", rq[:, 1, :w], pb[3][:, :w], 1.0 / KVL, EPS, ALU.mult, ALU.add, [], [(rqB, "p"), PW(3)])
                        rsqrt_inplace(rq[:, 1, :w], rqB)
                        for m in range(8):
                            cn, cnB = cn_r.next()
                            g_ap = gq[:, m:m + 1] if m < 6 else gkv[:, m - 6:m - 5]
                            stt("dve", cn[:, :w], craw[:, m, :w], g_ap, rq[:, 0 if m < 6 else 1, :w],
                                ALU.mult, ALU.mult, [crawB, gqB, gkvB, rqB], [cnB])
                            dst = CQv[:, m, t0:t0 + w] if m < 6 else CKVv[:, m - 6, t0:t0 + w]
                            S.dma(dst, cn[:, :w], reads=[cnB])
                        for k in range(8):
                            mm(5, pb[5][0:96, :w], w_in[:, k, 960:1056], xt[:, k, :w], k == 0, k == 7, [w_inB, xtB], k == 7)
                        for k in range(8):
                            mm(6, pb[6][0:96, :w], wr[:, k, :], xt[:, k, :w], k == 0, k == 7, [wrB, xtB], k == 7)
                        t1, t1B = t1_r.next()
                        t2, t2B = t2_r.next()
                        kr, krB = kr_r.next()
                        tt("dve", t1[64:96, :w], pb[5][64:96, :w], cosT[64:96, t0:t0 + w], ALU.mult, [csB], [t1B, PW(5)])
                        tt("dve", t2[64:96, :w], pb[6][64:96, :w], sinT[64:96, t0:t0 + w], ALU.mult, [csB], [t2B, PW(6)])
                        tt("pool", kr[64:96, :w], t1[64:96, :w], t2[64:96, :w], ALU.add, [t1B, t2B], [krB])
                        S.dma(KR[s, :, t0:t0 + w], kr[64:96, :w], reads=[krB])
                S.barrier()

        def phase_attn():
            with ExitStack() as ph:
                (w_uq, w_uqB), (w_ukv, w_ukvB) = load_ws(ph, [("w_uq", w_uq_d, QL, NH * 96), ("w_ukv", w_ukv_d, KVL, NH * 128)])
                wqr = sbt(ph, "w_uqr", [128, 6, NH * 96], BF16)
                wqrB = Buf("wqr")
                memset("pool", wqr[:], 0.0, [wqrB])
                for k in range(6):
                    a4 = w_uq[:, k, :].rearrange("p (h d) -> p h d", d=96)
                    r4 = wqr[:, k, :].rearrange("p (h d) -> p h d", d=96)
                    act(r4[:, :, 64:80], a4[:, :, 80:96], AF.Identity, [w_uqB], [(wqrB, "p")], scale=-1.0)
                    cp("pool", r4[:, :, 80:96], a4[:, :, 64:80], [w_uqB], [(wqrB, "p")])
                cosT = sbt(ph, "cosT", [128, L], F32)
                sinT = sbt(ph, "sinT", [128, L], F32)
                csB = Buf("cs")
                S.dma(cosT[64:96, :], cos_d[:, :], writes=[csB])
                S.dma(sinT[64:96, :], sin_d[:, :], writes=[(csB, "p")])
                cq = sbt(ph, "cq", [128, 6, L], BF16)
                cqB = Buf("cq")
                ckv = sbt(ph, "ckv", [128, 2, L], BF16)
                ckvB = Buf("ckv")
                KT = [sbt(ph, f"KT{i}", [128, L], BF16) for i in range(2)]
                KTB = [Buf(f"KT{i}") for i in range(2)]
                VA = [sbt(ph, f"VA{i}", [128, NKT, 128], BF16) for i in range(2)]
                VAB = [Buf(f"VA{i}") for i in range(2)]
                for i in range(2):
                    memset("pool", VA[i][:, :, 64:128], 1.0, [VAB[i]])
                QT_r = Ring(nc, ph, "QT", [128, 512], BF16, 2)
                PT_r = Ring(nc, ph, "PT", [128, 512], BF16, 4)
                t1_r = Ring(nc, ph, "at1", [128, 512], F32, 2)
                t2_r = Ring(nc, ph, "at2", [128, 512], F32, 2)
                rd_r = Ring(nc, ph, "rden", [128, 512], F32, 2)
                on_r = Ring(nc, ph, "on", [128, 512], BF16, 2)
                st_banks = [0, 1, 2]
                ot_banks = [3, 5]
                sti = [0]
                oti = [0]
                items = [(s, h, t0) for s in range(NS) for h in range(NH) for t0 in range(0, L, 512)]
                qbuf = {}

                def load_seq(s):
                    S.dma(cq[:, :, :], fm(CQ[s])[:, :, :], writes=[cqB])
                    S.dma(ckv[:, :, :], fm(CKV[s])[:, :, :], writes=[ckvB])
                    for i in range(2):
                        S.dma(KT[i][64:96, :], KR[s, :, :], writes=[(KTB[i], "p")])

                def build_kv(s, h):
                    hb = h % 2
                    for t0 in range(0, L, 512):
                        w = min(512, L - t0)
                        for k in range(2):
                            mm(6, pb[6][0:64, :w], w_ukv[:, k, h * 128:h * 128 + 64], ckv[:, k, t0:t0 + w],
                               k == 0, k == 1, [w_ukvB, ckvB], k == 1)
                        cp("dve", KT[hb][0:64, t0:t0 + w], pb[6][0:64, :w], [], [(KTB[hb], "p"), PW(6)])
                    for g0 in range(0, NKT, 8):
                        g1 = min(NKT, g0 + 8)
                        for kt in range(g0, g1):
                            nk = min(128, L - kt * 128)
                            for k in range(2):
                                mm(7, pb[7][:nk, (kt - g0) * 64:(kt - g0 + 1) * 64], ckv[:, k, kt * 128:kt * 128 + nk],
                                   w_ukv[:, k, h * 128 + 64:h * 128 + 128], k == 0, k == 1, [w_ukvB, ckvB],
                                   (k == 1))
                        cp("dve", VA[hb][:, g0:g1, 0:64],
                           pb[7][:, 0:(g1 - g0) * 64].rearrange("p (j d) -> p j d", d=64), [], [(VAB[hb], "p"), PW(7)])

                def build_q(s, h, t0):
                    w = min(512, L - t0)
                    for k in range(6):
                        mm(6, pb[6][0:96, :w], w_uq[:, k, h * 96:(h + 1) * 96], cq[:, k, t0:t0 + w],
                           k == 0, k == 5, [w_uqB, cqB], k == 5)
                    for k in range(6):
                        mm(7, pb[7][0:96, :w], wqr[:, k, h * 96:(h + 1) * 96], cq[:, k, t0:t0 + w],
                           k == 0, k == 5, [wqrB, cqB], k == 5)
                    QT, QTB = QT_r.next()
                    t1, t1B = t1_r.next()
                    t2, t2B = t2_r.next()
                    cp("act", QT[0:64, :w], pb[6][0:64, :w], [], [(QTB, "p"), PW(6)])
                    tt("dve", t1[64:96, :w], pb[6][64:96, :w], cosT[64:96, t0:t0 + w], ALU.mult, [csB], [t1B, PW(6)])
                    tt("dve", t2[64:96, :w], pb[7][64:96, :w], sinT[64:96, t0:t0 + w], ALU.mult, [csB], [t2B, PW(7)])
                    tt("pool", QT[64:96, :w], t1[64:96, :w], t2[64:96, :w], ALU.add, [t1B, t2B], [(QTB, "p")])
                    qbuf[(s, h, t0)] = (QT, QTB)

                def prep(j):
                    s, h, t0 = items[j]
                    if j == 0 or items[j - 1][0] != s:
                        load_seq(s)
                    if j == 0 or items[j - 1][:2] != (s, h):
                        build_kv(s, h)
                    build_q(s, h, t0)

                prep(0)
                for j, (s, h, t0) in enumerate(items):
                    hb = h % 2
                    w = min(512, L - t0)
                    QT, QTB = qbuf.pop((s, h, t0))
                    ob = ot_banks[oti[0] % 2]
                    oti[0] += 1
                    has_next = j + 1 < len(items)
                    early = has_next and items[j + 1][0] == s
                    stb = {}

                    def qk(kt):
                        nk = min(128, L - kt * 128)
                        sbk = st_banks[sti[0] % 3]
                        sti[0] += 1
                        stb[kt] = sbk
                        mm(sbk, pb[sbk][:nk, :w], KT[hb][0:96, kt * 128:kt * 128 + nk], QT[0:96, :w],
                           True, True, [KTB[hb], QTB], True)

                    LA = 3
                    for i0 in range(min(LA, NKT)):
                        qk(i0)
                    for kt in range(NKT):
                        nk = min(128, L - kt * 128)
                        sbk = stb[kt]
                        PT, PTB = PT_r.next()
                        act(PT[:nk, :w], pb[sbk][:nk, :w], AF.Exp, [], [PTB, PW(sbk)], scale=MLA_SCALE)
                        mm(ob, pb[ob][:, :w], VA[hb][0:nk, kt, :], PT[:nk, :w], kt == 0, kt == NKT - 1,
                           [VAB[hb], PTB], kt == NKT - 1)
                        if kt + LA < NKT:
                            qk(kt + LA)
                        if early and kt == min(6, NKT - 1):
                            prep(j + 1)
                    rd, rdB = rd_r.next()
                    on, onB = on_r.next()
                    S.op("dve", (lambda o, i: (lambda e: e.reciprocal(out=o, in_=i)))(rd[64:128, :w], pb[ob][64:128, :w]),
                         reads=[], writes=[rdB, PW(ob)])
                    tt("dve", on[0:64, :w], pb[ob][0:64, :w], rd[64:128, :w], ALU.mult, [rdB], [onB, PW(ob)])
                    S.dma(OT[s, h * 64:(h + 1) * 64, t0:t0 + w], on[0:64, :w], reads=[onB])
                    if has_next and not early:
                        prep(j + 1)
                S.barrier()

        def phase_mix_out(name, w_d, src_T, src_off, res_fn, layer, dst_tok, dst_T, dst_off):
            with ExitStack() as ph:
                ((wo, woB),) = load_ws(ph, [(name, w_d, D, D)])
                cx = LNCtx(ph, layer, 0, nr=3, nxs=2, nybf=3, nyo=3)
                ot_r = Ring(nc, ph, "ot", [128, 8, 512], BF16, 2)
                rs_r = Ring(nc, ph, "res", [128, D], F32, 3)
                mi = [0]
                pend = []
                for s in range(NS):
                    srcv = fm(src_T[s])
                    dstv = fm(dst_T[s])
                    zero_cols(dstv[:, :, 0:dst_off])
                    tail = dst_T.shape[2] - dst_off - L
                    zero_cols(dstv[:, :, dst_off + L:dst_off + L + tail])
                    for r0 in range(0, L, 128):
                        n = min(128, L - r0)
                        if r0 % 512 == 0:
                            ot, otB = ot_r.next()
                            gw = min(512, L - r0)
                            S.dma(ot[:, :, :gw], srcv[:, :, src_off + r0:src_off + r0 + gw], writes=[otB])
                        o0 = r0 % 512
                        rs, rsB = rs_r.next()
                        res_fn(rs, rsB, s, r0, n)
                        bks = ((0, 1), (2, 3))[mi[0] % 2]
                        mi[0] += 1
                        for h in range(2):
                            for c in range(8):
                                mm(bks[h], pb[bks[h]][:n, :], ot[:, c, o0:o0 + n], wo[:, c, h * 512:(h + 1) * 512], c == 0, c == 7,
                                   [otB, woB], c == 7)
                        if len(pend) >= 2:
                            pend.pop(0)()
                        pend.append(ln_epilogue(cx, bks, n, rs, rsB, dst_tok[s, r0:r0 + n, :],
                                                (dstv, dst_off + r0)))
                while pend:
                    pend.pop(0)()
                cx.xs.flush()
                S.barrier()

        def phase_ffn(layer, src_T, src_tok, dst_tok, dst_T, dst_off, final):
            TF = 510
            with ExitStack() as ph:
                (wup, wupB), (wdn, wdnB) = load_ws(ph, [("wup", f_up_d[layer], D, 2 * DFF), ("wdn", f_dn_d[layer], DFF, D)])
                cw = [load_cols(ph, f"cw{j}", f_cw_d[layer, j, :], 44) for j in range(3)]
                cb, cbB = load_cols(ph, "cb", f_cb_d[layer, :], 44)
                cwB = [c[1] for c in cw] + [cbB]
                cx = LNCtx(ph, layer, 1, nr=2, nxs=1, nyo=0, xsw=384)
                xt_r = Ring(nc, ph, "fxt", [128, 8, TF + 2], BF16, 1)
                G = sbt(ph, "G", [128, 22, TF + 2], BF16)
                GBs = [Buf(f"G{j}") for j in range(22)]
                a1_r = Ring(nc, ph, "a1", [128, TF + 2], F32, 2)
                cv_r = Ring(nc, ph, "cv", [128, TF + 2], F32, 2)
                cg_r = Ring(nc, ph, "cg", [128, TF + 2], F32, 2)
                rs_r = Ring(nc, ph, "res", [128, D], F32, 1)
                bi = [0]
                di = [0]
                pend = [None]
                dn_banks = [(5, 6), (5, 6)]
                for s in range(NS):
                    srcv = fm(src_T[s])
                    if dst_T is not None:
                        dstv = fm(dst_T[s])
                        zero_cols(dstv[:, :, 0:dst_off])
                        tail = dst_T.shape[2] - dst_off - L
                        for z0 in range(0, tail, 128):
                            zero_cols(dstv[:, :, dst_off + L + z0:dst_off + L + min(tail, z0 + 128)])
                    for t0 in range(0, L, TF):
                        w = min(TF, L - t0)
                        xt, xtB = xt_r.next()
                        S.dma(xt[:, :, :w + 2], srcv[:, :, t0:t0 + w + 2], writes=[xtB])
                        fin = None
                        for j in range(22):
                            res = []
                            for part in range(2):
                                bk = (0, 1, 2, 3, 7)[bi[0] % 5]
                                bi[0] += 1
                                col = part * DFF + j * 128
                                for k in range(8):
                                    mm(bk, pb[bk][:, :w + 2], wup[:, k, col:col + 128], xt[:, k, :w + 2], k == 0, k == 7,
                                       [wupB, xtB], k == 7)
                                c, cB = (cv_r if part == 0 else cg_r).next()
                                jj = part * 22 + j
                                a1, a1B = a1_r.next()
                                act(c[:, :w], pb[bk][:, 0:w], AF.Identity, cwB, [cB, PW(bk)],
                                    scale=cw[0][0][:, jj:jj + 1], bias=cb[:, jj:jj + 1])
                                act(a1[:, :w], pb[bk][:, 1:w + 1], AF.Identity, cwB, [a1B, PW(bk)],
                                    scale=cw[1][0][:, jj:jj + 1])
                                stt("dve", c[:, :w], pb[bk][:, 2:w + 2], cw[2][0][:, jj:jj + 1], c[:, :w], ALU.mult, ALU.add,
                                    cwB + [cB], [(cB, "p"), PW(bk)])
                                tt("dve", c[:, :w], c[:, :w], a1[:, :w], ALU.add, [cB, a1B], [(cB, "p")])
                                res.append((c, cB))
                                if part == 0 and fin is not None:
                                    fin()
                                    fin = None

                            def mk_fin(j_, res_):
                                def f():
                                    act(res_[1][0][:, :w], res_[1][0][:, :w], AF.Silu, [res_[1][1]], [(res_[1][1], "p")])
                                    tt("pool", G[:, j_, :w], res_[1][0][:, :w], res_[0][0][:, :w], ALU.mult,
                                       [res_[1][1], res_[0][1]], [GBs[j_]])
                                return f
                            fin = mk_fin(j, res)
                        fin()
                        for sub in range(0, w, 128):
                            n = min(128, w - sub)
                            r0 = t0 + sub
                            rs, rsB = rs_r.next()
                            S.dma(rs[:n, :], src_tok[s, r0:r0 + n, :], writes=[rsB])
                            bks = dn_banks[di[0] % 2]
                            di[0] += 1
                            for jr in (range(0, 20), range(20, 22)):
                                for h in range(2):
                                    for j in jr:
                                        mm(bks[h], pb[bks[h]][:n, :], G[:, j, sub:sub + n], wdn[:, j, h * 512:(h + 1) * 512],
                                           j == 0, j == 21, [GBs[j], wdnB], j == 21 or j == 19)
                            if pend[0] is not None:
                                pend[0]()
                                pend[0] = None
                            if final:
                                lo = max(r0, NMETA)
                                if lo < r0 + n:
                                    pend[0] = ln_epilogue(cx, bks, n, rs, rsB, y[s, lo - NMETA:r0 + n - NMETA, :], None,
                                                          row_lo=lo - r0)
                            else:
                                pend[0] = ln_epilogue(cx, bks, n, rs, rsB, dst_tok[s, r0:r0 + n, :],
                                                      (dstv, dst_off + r0))
                if pend[0] is not None:
                    pend[0]()
                cx.xs.flush()
                S.barrier()

        def phase_hgrn(dirn, ODST):
            rev = dirn == 1
            with ExitStack() as ph:
                if rev:
                    wq = wqB = wi = wiB = None
                else:
                    (wq, wqB), (wi, wiB) = load_ws(ph, [("hwq", hw_in_d[:, 0:D], D, D), ("hwi", hw_in_d[:, D:2 * D], D, D)])
                ((wf, wfB),) = load_ws(ph, [("hwf", hw_in_d[:, (2 + dirn) * D:(3 + dirn) * D], D, D)])
                l0, l0B = load_cols(ph, "lb0", hlb_d[dirn, 0, :], 8)
                l1, l1B = load_cols(ph, "lb1", hlb_d[dirn, 1, :], 8)
                lb = sbt(ph, "lb", [128, 8], F32)
                oml = sbt(ph, "oml", [128, 8], F32)
                lbB = Buf("lb")
                tt("dve", lb[:, :], l0[:, :], l1[:, :], ALU.subtract, [l0B, l1B], [lbB])
                act(lb[:, :], lb[:, :], AF.Exp, [lbB], [(lbB, "p")])
                ts("dve", lb[:, :], lb[:, :], 1.0, None, ALU.add, None, [lbB], [(lbB, "p")])
                S.op("dve", lambda e: e.reciprocal(out=lb[:, :], in_=lb[:, :]), reads=[lbB], writes=[(lbB, "p")])
                ts("dve", oml[:, :], lb[:, :], -1.0, 1.0, ALU.mult, ALU.add, [lbB], [(lbB, "p")])
                smask = sbt(ph, "smask", [128, 512], F32)
                smB = Buf("smask")
                memset("pool", smask[:], 1.0, [smB])
                memset("pool", smask[:].rearrange("p (c j) -> p c j", j=64)[:, :, 0:1], 0.0, [(smB, "p")], reads=[smB])
                tri = sbt(ph, "tri", [128, 64], F32)
                triB = Buf("tri")
                memset("pool", tri[:], 1.0, [triB])
                S.op("pool", lambda e: e.affine_select(out=tri[0:64, :], in_=tri[0:64, :], pattern=[[-1 if rev else 1, 64]],
                                                       compare_op=ALU.is_ge, fill=0.0, base=0,
                                                       channel_multiplier=(1 if rev else -1)), reads=[triB], writes=[(triB, "p")])
                cp("act", tri[64:128, :], tri[0:64, :], [triB], [(triB, "p")])
                S32 = sbt(ph, "S32", [128, 8, 128], F32)
                Sbf = sbt(ph, "Sbf", [128, 8, 128], BF16)
                S32B = [Buf(f"S32_{h}") for h in range(8)]
                SbfB = [Buf(f"Sbf_{h}") for h in range(8)]
                bt_r = Ring(nc, ph, "bt", [128, 8, 512], BF16, 2)
                vt_r = Ring(nc, ph, "vt", [128, 4, D], BF16, 2)
                NSET = 2
                qd = [[sbt(ph, f"qd{a}_{h}", [128, 512], BF16) for h in range(8)] for a in range(NSET)]
                qdec = [[sbt(ph, f"qe{a}_{h}", [128, 512], BF16) for h in range(8)] for a in range(NSET)]
                kd = [[sbt(ph, f"kd{a}_{h}", [128, 512], BF16) for h in range(8)] for a in range(NSET)]
                kdT = [[sbt(ph, f"kT{a}_{h}", [128, 4, 128], BF16) for h in range(8)] for a in range(NSET)]
                dl = [[sbt(ph, f"dl{a}_{h}", [128, 8], F32) for h in range(8)] for a in range(NSET)]
                prepB = [[Buf(f"prep{a}_{h}") for h in range(8)] for a in range(NSET)]
                kdec_r = Ring(nc, ph, "kdec", [128, 512], BF16, 3)
                ef = [Ring(nc, ph, f"ef{i}", [128, 512], F32, 3) for i in range(6)]
                am_r = Ring(nc, ph, "am", [128, 64], BF16, 4)
                os_r = Ring(nc, ph, "osb", [128, 512], F32, 2)
                tiles = [(t0, min(512, LP - t0)) for t0 in range(0, LP, 512)]
                if rev:
                    tiles = tiles[::-1]
                pu_banks = [3, 7]
                pui = [0]
                pendA2 = [None]
                pendA1b = [None]
                qcur = {}
                SbfG = [Buf("SbfG0"), Buf("SbfG1")]
                amA_r = Ring(nc, ph, "amA", [128, 8, 64], BF16, 3)
                cur = {}

                qall_r = Ring(nc, ph, "qall", [128, 8, 512], F32, 2) if rev else None
                pre = {}

                def loads(s, ti):
                    t0, w = tiles[ti]
                    nb = (w + 127) // 128
                    bt, btB = bt_r.next()
                    S.dma(bt[:, :, :w], fm(B_T[s])[:, :, t0:t0 + w], writes=[btB])
                    vt, vtB = vt_r.next()
                    qa = qaB = None
                    if rev:
                        S.dma(vt[:, 0:nb, :], VT_d[s, t0:t0 + w, :].rearrange("(b p) d -> p b d", p=128), writes=[vtB])
                        qa, qaB = qall_r.next()
                        S.dma(qa[:, :, :w], Q_d[s].rearrange("(h p) t -> p h t", p=128)[:, :, t0:t0 + w], writes=[qaB])
                    pre[ti] = (bt, btB, vt, vtB, qa, qaB)

                def stageA_common(s, ti):
                    t0, w = tiles[ti]
                    nb = (w + 127) // 128
                    if ti not in pre:
                        loads(s, ti)
                    bt, btB, vt, vtB, qa, qaB = pre.pop(ti)
                    qcur[ti] = (qa, qaB)
                    vview = VT_d[s, t0:t0 + w, :].rearrange("(b p) d -> p b d", p=128)
                    if rev:
                        pass
                    else:
                        for blk in range(nb):
                            n = min(128, w - blk * 128)
                            for hf in range(2):
                                for k in range(8):
                                    mm(2, pb[2][:n, :], bt[:, k, blk * 128:blk * 128 + n], wi[:, k, hf * 512:(hf + 1) * 512],
                                       k == 0, k == 7, [btB, wiB], k == 7)
                                cp("act", vt[:n, blk, hf * 512:(hf + 1) * 512], pb[2][:n, :], [], [(vtB, "p"), PW(2)])
                        S.dma(vview, vt[:, 0:nb, :], reads=[vtB], q="act")
                    cur[ti] = (bt, btB, vt, vtB)

                def stageA_head(s, ti, h):
                    t0, w = tiles[ti]
                    a = ti % NSET
                    nch = w // 64
                    nb = (w + 127) // 128
                    bt, btB, vt, vtB = cur[ti]
                    PB = prepB[a][h]
                    for k in range(8):
                        mm(0, pb[0][:, :w], wf[:, k, h * 128:(h + 1) * 128], bt[:, k, :w], k == 0, k == 7, [wfB, btB], k == 7)
                    if not rev:
                        for k in range(8):
                            mm(1, pb[1][:, :w], wq[:, k, h * 128:(h + 1) * 128], bt[:, k, :w], k == 0, k == 7, [wqB, btB], k == 7)
                    e0, e0B = ef[0].next()
                    e1, e1B = ef[1].next()
                    e2, e2B = ef[2].next()
                    e3, e3B = ef[3].next()
                    e4, e4B = ef[4].next()
                    e5, e5B = ef[5].next()
                    act(e0[:, :w], pb[0][:, :w], AF.Exp, [], [e0B, PW(0)], scale=-1.0)
                    if rev:
                        qa, qaB = qcur[ti]
                        e5, e5B = qa[:, h, :], qaB
                    else:
                        cp("act", e5[:, :w], pb[1][:, :w], [], [e5B, PW(1)])
                        S.dma(Q_d[s, h * 128:(h + 1) * 128, t0:t0 + w], e5[:, :w], reads=[e5B], q="act")
                    act(e0[:, :w], e0[:, :w], AF.Ln, [e0B], [(e0B, "p")], bias=1.0)
                    act(e0[:, :w], e0[:, :w], AF.Exp, [e0B], [(e0B, "p")], scale=-1.0)
                    ts("dve", e0[:, :w], e0[:, :w], oml[:, h:h + 1], lb[:, h:h + 1], ALU.mult, ALU.add, [e0B, lbB], [(e0B, "p")])
                    ts("pool", e2[:, :w], e0[:, :w], -1.0, 1.0, ALU.mult, ALU.add, [e0B], [e2B])
                    act(e1[:, :w], e0[:, :w], AF.Ln, [e0B], [e1B])
                    for (p0, p1) in ((0, 48), (L + 48, LP)):
                        lo, hi = max(p0, t0), min(p1, t0 + w)
                        if lo < hi:
                            memset("pool", e1[:, lo - t0:hi - t0], 0.0, [(e1B, "p")], reads=[e1B])
                            memset("pool", e2[:, lo - t0:hi - t0], 0.0, [(e2B, "p")], reads=[e2B])
                    S.op("dve", (lambda o, m_, i: (lambda e: e.tensor_tensor_scan(out=o, data0=m_, data1=i, initial=0.0,
                                                                                 op0=ALU.mult, op1=ALU.add)))(
                        e3[:, :w], smask[:, :w], e1[:, :w]), reads=[smB, e1B], writes=[e3B])
                    B3 = e3[:, :w].rearrange("p (c j) -> p c j", j=64)
                    if rev:
                        tt("pool", e1[:, :w], e1[:, :w], e3[:, :w], ALU.subtract, [e1B, e3B], [(e1B, "p")])
                        tt("dve", B3, e1[:, :w].rearrange("p (c j) -> p c j", j=64),
                           B3[:, :, 63:64].to_broadcast([128, nch, 64]), ALU.add, [e1B, e3B], [(e3B, "p")])
                        i_ref, i_last = 31, 0
                    else:
                        i_ref, i_last = 32, 63
                    def a1b():
                        a1b_body(h, a, w, nch, nb, PB, e0, e0B, e1, e1B, e2, e2B, e3, e3B, e4, e4B, e5, e5B, B3, i_ref, i_last)
                    if pendA1b[0] is not None:
                        pendA1b[0]()
                    pendA1b[0] = a1b

                def a1b_body(h, a, w, nch, nb, PB, e0, e0B, e1, e1B, e2, e2B, e3, e3B, e4, e4B, e5, e5B, B3, i_ref, i_last):
                    E4 = e4[:, :w].rearrange("p (c j) -> p c j", j=64)
                    tt("dve", E4, B3, B3[:, :, i_ref:i_ref + 1].to_broadcast([128, nch, 64]), ALU.subtract, [e3B], [e4B])
                    act(e0[:, :w], e4[:, :w], AF.Exp, [e4B], [e0B])
                    act(e1[:, :w], e4[:, :w], AF.Exp, [e4B], [e1B], scale=-1.0)
                    tt("dve", qd[a][h][:, :w], e5[:, :w], e0[:, :w], ALU.mult, [e5B, e0B], [(PB, "p")])
                    tt("pool", kd[a][h][:, :w], e2[:, :w], e1[:, :w], ALU.mult, [e2B, e1B], [(PB, "p")])
                    act(e0[:, :w], e3[:, :w], AF.Exp, [e3B, PB], [e0B])
                    tt("dve", qdec[a][h][:, :w], e5[:, :w], e0[:, :w], ALU.mult, [e5B, e0B], [(PB, "p")])
                    tt("pool", E4, B3, B3[:, :, i_last:i_last + 1].to_broadcast([128, nch, 64]), ALU.subtract, [e3B, e1B], [e4B])
                    act(e1[:, :w], e4[:, :w], AF.Exp, [e4B, PB], [e1B], scale=-1.0)
                    kdc, kdcB = kdec_r.next()
                    tt("pool", kdc[:, :w], e2[:, :w], e1[:, :w], ALU.mult, [e2B, e1B], [kdcB])
                    act(dl[a][h][:, 0:nch], B3[:, :, i_last], AF.Exp, [e3B], [(PB, "p")])
                    def a2():
                        for blk in range(nb):
                            n = min(128, w - blk * 128)
                            transpose(4, ptr[:n, blk * 128:(blk + 1) * 128], kdc[:, blk * 128:blk * 128 + n], ident[:, :],
                                      [kdcB, identB], blk == nb - 1)
                        cp("act", kdT[a][h][:, 0:nb, :], ptr[:, 0:nb * 128].rearrange("p (b d) -> p b d", d=128),
                           [], [(PB, "p"), PW(4)])
                    if pendA2[0] is not None:
                        pendA2[0]()
                    pendA2[0] = a2

                def flushA2():
                    if pendA1b[0] is not None:
                        pendA1b[0]()
                        pendA1b[0] = None
                    if pendA2[0] is not None:
                        pendA2[0]()
                        pendA2[0] = None

                def stageB_chunk(s, ti, c):
                    t0, w = tiles[ti]
                    a = ti % NSET
                    bt, btB, vt, vtB = cur[ti]
                    bl, hf = c // 2, c % 2
                    p0 = hf * 64
                    cols = slice(c * 64, c * 64 + 64)
                    obk = 6
                    for h in range(8):
                        mm(5, pb[5][p0:p0 + 64, h * 64:(h + 1) * 64], kd[a][h][:, cols], qd[a][h][:, cols], True, True,
                           [prepB[a][h]], h == 7, tp=(0, p0))
                    am, amB = amA_r.next()
                    tt("dve", am[p0:p0 + 64, :, :], pb[5][p0:p0 + 64, :].rearrange("p (h t) -> p h t", t=64),
                       tri[p0:p0 + 64, :].unsqueeze(1).to_broadcast([64, 8, 64]), ALU.mult, [triB], [amB, PW(5)])
                    for h in range(8):
                        PB = prepB[a][h]
                        oh0 = (h // 4) * 64
                        ocol = (h % 4) * 128
                        mm(obk, pb[obk][oh0:oh0 + 64, ocol:ocol + 128], am[p0:p0 + 64, h, :], vt[p0:p0 + 64, bl, h * 128:(h + 1) * 128],
                           True, False, [amB, vtB], False, tp=(p0, oh0))
                        mm(obk, pb[obk][oh0:oh0 + 64, ocol:ocol + 128], qdec[a][h][:, cols], Sbf[:, h, :],
                           False, True, [PB, SbfG[h // 4]], True, tp=(0, oh0))
                        pub = pu_banks[pui[0] % 2]
                        pui[0] += 1
                        mm(pub, pb[pub][:, 0:128], kdT[a][h][p0:p0 + 64, bl, :], vt[p0:p0 + 64, bl, h * 128:(h + 1) * 128],
                           True, True, [PB, vtB], True)
                        stt("dve", S32[:, h, :], S32[:, h, :], dl[a][h][:, c:c + 1], pb[pub][:, 0:128], ALU.mult, ALU.add,
                            [PB, S32B[h]], [(S32B[h], "p"), PW(pub)])
                        if h % 4 == 3:
                            g = h // 4
                            cp("act", Sbf[:, g * 4:(g + 1) * 4, :], S32[:, g * 4:(g + 1) * 4, :],
                               [S32B[hh] for hh in range(g * 4, g * 4 + 4)], [SbfG[g]])
                    osb, osB = os_r.next()
                    cp("act", osb[:, :], pb[obk][:, :], [], [osB, PW(obk)])
                    r_ = t0 + c * 64
                    S.dma(ODST[s, r_:r_ + 64, 0:512], osb[0:64, :], reads=[osB], q="act")
                    S.dma(ODST[s, r_:r_ + 64, 512:1024], osb[64:128, :], reads=[osB], q="act")

                for s in range(NS):
                    for h in range(8):
                        memset("pool", S32[:, h, :], 0.0, [S32B[h]])
                    for g in range(2):
                        memset("pool", Sbf[:, g * 4:(g + 1) * 4, :], 0.0, [SbfG[g]])
                    cur.clear()
                    pre.clear()
                    qcur.clear()
                    INTERLEAVE = os.environ.get("MK_HG_IL", "0") == "1"
                    stageA_common(s, 0)
                    for h in range(8):
                        stageA_head(s, 0, h)
                    flushA2()
                    for ti in range(len(tiles)):
                        t0, w = tiles[ti]
                        nch = w // 64
                        chunks = list(range(nch))
                        if rev:
                            chunks = chunks[::-1]
                        nxt = ti + 1 < len(tiles)
                        pending = list(range(8)) if nxt else []
                        if nxt:
                            loads(s, ti + 1)
                        for ci, c in enumerate(chunks):
                            stageB_chunk(s, ti, c)
                            if nxt and INTERLEAVE:
                                if ci == 0:
                                    stageA_common(s, ti + 1)
                                target = -(-8 * (ci + 1) // nch)
                                while 8 - len(pending) < target and pending:
                                    stageA_head(s, ti + 1, pending.pop(0))
                        if nxt and not INTERLEAVE:
                            stageA_common(s, ti + 1)
                        while pending:
                            stageA_head(s, ti + 1, pending.pop(0))
                        flushA2()
                S.barrier()

        def phase_hgrn_out():
            with ExitStack() as ph:
                (wg, wgB), (wo, woB) = load_ws(ph, [("hwg", hw_in_d[:, 4 * D:5 * D], D, D), ("hwo", hw_o_d, D, D)])
                gn, gnB = load_bcast(ph, "hon", hon_d)
                cx = LNCtx(ph, 1, 0, nr=2, nxs=2, nybf=2, stq=os.environ.get("MK_STQ7", "act"))
                bt_r = Ring(nc, ph, "gbt", [128, 8, 512], BF16, 2)
                btc = [None]
                of_r = Ring(nc, ph, "ofw", [128, D], F32, 3)
                ob_r = Ring(nc, ph, "obw", [128, D], F32, 3)
                sq_r = Ring(nc, ph, "osq", [128, D], F32, 2)
                ss_r = Ring(nc, ph, "oss", [128, 8], F32, 3)
                eg_r = Ring(nc, ph, "eg", [128, D], F32, 3)
                yb_r = Ring(nc, ph, "yb", [128, D], BF16, 3)
                yT_r = Ring(nc, ph, "yT", [128, 8, 128], BF16, 2)
                rs_r = Ring(nc, ph, "res", [128, D], F32, 3)
                wi_ = [0]
                pend = [None]
                for s in range(NS):
                    BTv = fm(B_T[s])
                    CTv = fm(C_T[s])
                    zero_cols(CTv[:, :, 0:1])
                    zero_cols(CTv[:, :, L + 1:L + 2])
                    tl = [(r0, min(128, L - r0)) for r0 in range(0, L, 128)]
                    stt_ = {}
                    stt1_ = {}

                    def part1(i):
                        r0, n = tl[i]
                        pc = r0 + 48
                        if r0 % 512 == 0:
                            gw = min(512, L - r0)
                            btc[0] = bt_r.next()
                            S.dma(btc[0][0][:, :, :gw], BTv[:, :, pc:pc + gw], writes=[btc[0][1]])
                        bt_full, btB = btc[0]
                        bt = bt_full[:, :, r0 % 512:r0 % 512 + n]
                        of, ofB = of_r.next()
                        ob, obB = ob_r.next()
                        S.dma(of[:n, :], OFW[s, pc:pc + n, :], writes=[ofB])
                        S.dma(ob[:n, :], OBW[s, pc:pc + n, :], writes=[obB])
                        rs, rsB = rs_r.next()
                        S.dma(rs[:n, :], B_tok[s, r0:r0 + n, :], writes=[rsB])
                        eg, egB = eg_r.next()
                        for hf in range(2):
                            for k in range(8):
                                mm(hf, pb[hf][:n, :], bt[:, k, :], wg[:, k, hf * 512:(hf + 1) * 512], k == 0, k == 7, [btB, wgB], k == 7)
                            act(eg[:n, hf * 512:(hf + 1) * 512], pb[hf][:n, :], AF.Exp, [], [(egB, "p"), PW(hf)], scale=-1.0)
                        act(eg[:n, :], eg[:n, :], AF.Ln, [egB], [(egB, "p")], bias=1.0)
                        act(eg[:n, :], eg[:n, :], AF.Exp, [egB], [(egB, "p")], scale=-1.0)
                        for hf in range(2):
                            tt("dve", eg[:n, hf * 512:(hf + 1) * 512], eg[:n, hf * 512:(hf + 1) * 512], pb[hf][:n, :], ALU.mult,
                               [egB], [(egB, "p"), PW(hf)])
                        stt1_[i] = (n, of, ofB, ob, obB, eg, egB, rs, rsB)

                    def part1b(i):
                        n, of, ofB, ob, obB, eg, egB, rs, rsB = stt1_.pop(i)
                        tt("dve", of[:n, :], of[:n, :], ob[:n, :], ALU.add, [ofB, obB], [(ofB, "p")])
                        sq, sqB = sq_r.next()
                        ss, ssB = ss_r.next()
                        tt("dve", sq[:n, :], of[:n, :], of[:n, :], ALU.mult, [ofB], [sqB])
                        S.op("dve", (lambda o, i_: (lambda e: e.tensor_reduce(out=o, in_=i_, axis=AX.X, op=ALU.add)))(
                            ss[:n, :], sq[:n, :].rearrange("p (h d) -> p h d", d=128)), reads=[sqB], writes=[ssB])
                        ts("dve", ss[:n, :], ss[:n, :], 1.0 / 128, EPS, ALU.mult, ALU.add, [ssB], [(ssB, "p")])
                        rsqrt_inplace(ss[:n, :], ssB)
                        o3 = of[:n, :].rearrange("p (h d) -> p h d", d=128)
                        tt("dve", o3, o3, ss[:n, :].unsqueeze(2).to_broadcast([n, 8, 128]), ALU.mult, [ofB, ssB], [(ofB, "p")])
                        tt("dve", of[:n, :], of[:n, :], gn[:n, :], ALU.mult, [ofB, gnB], [(ofB, "p")])
                        yb, ybB = yb_r.next()
                        tt("dve", yb[:n, :], of[:n, :], eg[:n, :], ALU.mult, [ofB, egB], [ybB])
                        stt_[i] = (yb, ybB, rs, rsB)

                    stt2_ = {}

                    def part2a(i):
                        r0, n = tl[i]
                        yb, ybB, rs, rsB = stt_.pop(i)
                        for c in range(8):
                            transpose(4, ptr[:, c * 128:c * 128 + n], yb[:n, c * 128:(c + 1) * 128], ident[:n, :n], [ybB, identB], c == 7)
                        yT, yTB = yT_r.next()
                        cp("act", yT[:, :, :n], ptr[:, :].rearrange("p (c j) -> p c j", j=128)[:, :, :n], [], [yTB, PW(4)])
                        stt2_[i] = (yT, yTB, rs, rsB)

                    def part2(i):
                        r0, n = tl[i]
                        yT, yTB, rs, rsB = stt2_.pop(i)
                        bks = ((2, 3), (5, 6))[wi_[0] % 2]
                        wi_[0] += 1
                        for hf in range(2):
                            for c in range(8):
                                mm(bks[hf], pb[bks[hf]][:n, :], yT[:, c, :n], wo[:, c, hf * 512:(hf + 1) * 512], c == 0, c == 7,
                                   [yTB, woB], c == 7)
                        if pend[0] is not None:
                            pend[0]()
                        pend[0] = ln_epilogue(cx, bks, n, rs, rsB, C_tok[s, r0:r0 + n, :], (CTv, 1 + r0))

                    part1(0)
                    part1b(0)
                    if len(tl) > 1:
                        part1(1)
                        part1b(1)
                    part2a(0)
                    for i in range(len(tl)):
                        if i + 2 < len(tl):
                            part1(i + 2)
                        part2(i)
                        if i + 1 < len(tl):
                            part2a(i + 1)
                        if i + 2 < len(tl):
                            part1b(i + 2)
                if pend[0] is not None:
                    pend[0]()
                cx.xs.flush()
                S.barrier()

        def on(p):
            return phases is None or p in phases
        if on(0):
            phase_T0()
        if on(1):
            phase_mla_proj()
        if on(2):
            phase_attn()
        if on(3):
            phase_mix_out("mwo", w_o_d, OT, 0, load_h0_rows, 0, A_tok, A_T, 1)
        if on(4):
            phase_ffn(0, A_T, A_tok, B_tok, B_T, 48, False)
        if on(5):
            phase_hgrn(0, OFW)
        if on(6):
            phase_hgrn(1, OBW)
        if on(7):
            phase_hgrn_out()
        if on(8):
            phase_ffn(1, C_T, C_tok, None, None, 0, True)
        S.barrier()
        print("instructions:", S.n_inst, {k: len(v) for k, v in S.streams.items()}, flush=True)
        with nc.Block() as block:
            S.emit(block)
    return nc


def rope_tables_np(L):
    inv = (1.0 / (10000.0 ** (np.arange(0, DR, 2, dtype=np.float32) / np.float32(DR)))).astype(np.float32)
    ang = (np.arange(L, dtype=np.float32)[:, None] * inv[None, :]).astype(np.float32)
    c = np.cos(ang).astype(np.float32).T
    s = np.sin(ang).astype(np.float32).T
    return (np.ascontiguousarray(np.concatenate([c, c], 0)), np.ascontiguousarray(np.concatenate([s, s], 0)))


def run(seqs, weights, n_cores=8):
    S_LEN = seqs[0].shape[0]
    nseq = len(seqs)
    NS = (nseq + n_cores - 1) // n_cores
    nc = build(S_LEN, NS)
    cosT, sinT = rope_tables_np(S_LEN + NMETA)
    f32 = lambda a: np.ascontiguousarray(np.asarray(a, dtype=np.float32))
    common = {
        "meta": f32(weights["meta_tokens"]),
        "mla_w_in": f32(weights["mla_w_in"][0]), "mla_q_norm": f32(weights["mla_q_norm"][0]),
        "mla_kv_norm": f32(weights["mla_kv_norm"][0]), "mla_w_uq": f32(weights["mla_w_uq"][0]),
        "mla_w_ukv": f32(weights["mla_w_ukv"][0]), "mla_w_o": f32(weights["mla_w_o"][0]),
        "hgrn_w_in": f32(weights["hgrn_w_in"][0]), "hgrn_lb": f32(weights["hgrn_lower_bound"]),
        "hgrn_o_norm": f32(weights["hgrn_o_norm"][0]), "hgrn_w_o": f32(weights["hgrn_w_o"][0]),
        "ffn_w_up": f32(weights["ffn_w_up"]), "ffn_conv_w": f32(weights["ffn_conv_w"]),
        "ffn_conv_b": f32(weights["ffn_conv_b"]), "ffn_w_down": f32(weights["ffn_w_down"]),
        "ln_gain": f32(weights["ln_gain"]), "ln_bias": f32(weights["ln_bias"]),
        "rope_cos": cosT, "rope_sin": sinT,
    }
    assign = []
    in_maps = []
    for c in range(n_cores):
        ids = [c + n_cores * j for j in range(NS)]
        ids = [i if i < nseq else ids[0] % nseq for i in ids]
        assign.append(ids)
        m = dict(common)
        m["x"] = np.ascontiguousarray(np.stack([seqs[i] for i in ids], 0))
        in_maps.append(m)
    res = run_bass_kernel_spmd(nc, in_maps, core_ids=list(range(n_cores)))
    outs = [None] * nseq
    for c in range(n_cores):
        yc = np.asarray(res.results[c]["y"])
        for j in range(NS):
            i = c + n_cores * j
            if i < nseq:
                outs[i] = yc[j]
    return outs


def kernel(x_prompt, x_sample, meta_tokens, mla_w_in, mla_q_norm, mla_kv_norm, mla_w_uq, mla_w_ukv, mla_w_o,
           hgrn_w_in, hgrn_lower_bound, hgrn_o_norm, hgrn_w_o,
           ffn_w_up, ffn_conv_w, ffn_conv_b, ffn_w_down, ln_gain, ln_bias):
    x_prompt = np.asarray(x_prompt, dtype=np.float32)
    x_sample = np.asarray(x_sample, dtype=np.float32)
    weights = dict(meta_tokens=meta_tokens, mla_w_in=mla_w_in, mla_q_norm=mla_q_norm, mla_kv_norm=mla_kv_norm,
                   mla_w_uq=mla_w_uq, mla_w_ukv=mla_w_ukv, mla_w_o=mla_w_o, hgrn_w_in=hgrn_w_in,
                   hgrn_lower_bound=hgrn_lower_bound, hgrn_o_norm=hgrn_o_norm, hgrn_w_o=hgrn_w_o,
                   ffn_w_up=ffn_w_up, ffn_conv_w=ffn_conv_w, ffn_conv_b=ffn_conv_b, ffn_w_down=ffn_w_down,
                   ln_gain=ln_gain, ln_bias=ln_bias)
    weights = {k: np.asarray(v, dtype=np.float32) for k, v in weights.items()}
    seqs = [x_prompt[i] for i in range(x_prompt.shape[0])] + [x_sample[i] for i in range(x_sample.shape[0])]
    outs = run(seqs, weights)
    nb = x_prompt.shape[0]
    y_prompt = np.stack(outs[:nb], 0).astype(np.float32)
    y_sample = np.stack(outs[nb:], 0).astype(np.float32)
    return (y_prompt, y_sample)
```
